# Optimizing a Trainium2 kernel written in Bass

```python
import jax, jax.numpy as jnp
from jax import lax
import numpy as np

D_MODEL = 1024
BATCH = 16
SEQ = 256
DEPTH = 2
DEC_BATCH = 4
DEC_SEQ = 2048
PAST_LEN = 512

GRID_W = 64
N_MOD = 6
LN_EPS = 1e-5
CONV_WIDTH_A = D_MODEL
CONV_K = 31
HG_HEADS = 8
HG_DK = 128
HG_DV = 128
HG_WIDTH = HG_HEADS * HG_DK
HG_CHUNK = 16
SGU_WIDTH = D_MODEL
SGU_GROUPS = 8
SGU_GROUP_DIM = SGU_WIDTH // SGU_GROUPS
SGU_CHUNK = 128
FFN_HIDDEN = -(-8 * D_MODEL // (3 * 256)) * 256
DEEPNORM_ALPHA = (2 * DEPTH) ** 0.25
DEEPNORM_BETA = (8 * DEPTH) ** -0.25
A_IN = 2 * CONV_WIDTH_A
B_IN = 5 * HG_WIDTH
C_IN = 2 * SGU_WIDTH
GATE_IN = 3 * D_MODEL
D_IN = A_IN + B_IN + C_IN + GATE_IN

kernel_name = "hybrid_conv_hgrn2_sgu_diffusion_step"


def _layernorm(x, g, b):
    xf = x.astype(jnp.float32)
    mu = jnp.mean(xf, axis=-1, keepdims=True)
    var = jnp.mean(jnp.square(xf - mu), axis=-1, keepdims=True)
    y = (xf - mu) * lax.rsqrt(var + LN_EPS)
    return (y * g.astype(jnp.float32) + b.astype(jnp.float32)).astype(x.dtype)


def _sincos_2d(n_tokens, dim, dtype):
    rows = n_tokens // GRID_W
    t = jnp.arange(rows * GRID_W)
    r = (t // GRID_W).astype(jnp.float32)
    col = (t % GRID_W).astype(jnp.float32)
    nf = dim // 4
    omega = 1.0 / (10000.0 ** (jnp.arange(nf, dtype=jnp.float32) / nf))
    ar = r[:, None] * omega
    ac = col[:, None] * omega
    return jnp.concatenate([jnp.sin(ar), jnp.cos(ar), jnp.sin(ac), jnp.cos(ac)], axis=-1).astype(dtype)


def _conformer_conv(a_in, conv_w, conv_b, ln_g, ln_b, w_out):
    val, gate = jnp.split(a_in, 2, axis=-1)
    h = val * jax.nn.sigmoid(gate)
    h = lax.conv_general_dilated(h, conv_w[:, None, :], window_strides=(1,),
                                 padding=[(CONV_K // 2, CONV_K // 2)],
                                 dimension_numbers=('NWC', 'WIO', 'NWC'),
                                 feature_group_count=CONV_WIDTH_A) + conv_b
    h = jax.nn.silu(_layernorm(h, ln_g, ln_b))
    return h @ w_out


def _gla_chunked(q, k, v, logf, s0):
    bsz, t_len, n_h, _ = q.shape
    dv = v.shape[-1]
    n = t_len // HG_CHUNK

    def chunks(a):
        return a.reshape(bsz, n, HG_CHUNK, n_h, a.shape[-1]).transpose(0, 1, 3, 2, 4)

    q, k, v, logf = chunks(q), chunks(k), chunks(v), chunks(logf)
    b = jnp.cumsum(logf, axis=3)
    causal = jnp.tril(jnp.ones((HG_CHUNK, HG_CHUNK), dtype=bool))[:, :, None]
    diff = b[:, :, :, :, None, :] - b[:, :, :, None, :, :]
    decay = jnp.where(causal, jnp.exp(jnp.where(causal, diff, 0.0)), 0.0)
    scores = jnp.einsum('bnhtd,bnhsd,bnhtsd->bnhts', q, k, decay)
    o_intra = jnp.einsum('bnhts,bnhsv->bnhtv', scores, v)
    b_end = b[:, :, :, -1, :]
    k_to_end = k * jnp.exp(b_end[:, :, :, None, :] - b)
    ds = jnp.einsum('bnhsd,bnhsv->bnhdv', k_to_end, v)

    def step(s, inp):
        be, dsn = inp
        return jnp.exp(be)[..., None] * s + dsn, s

    s_fin, s_prev = lax.scan(step, s0, (jnp.moveaxis(b_end, 1, 0), jnp.moveaxis(ds, 1, 0)))
    s_prev = jnp.moveaxis(s_prev, 0, 1)
    o_inter = jnp.einsum('bnhtd,bnhdv->bnhtv', q * jnp.exp(b), s_prev)
    o = (o_intra + o_inter).transpose(0, 1, 3, 2, 4).reshape(bsz, t_len, n_h, dv)
    return o, s_fin


def _hgrn2(b_in, lb, norm_g, w_out, s0):
    q, zf, zb, i, g = jnp.split(b_in, 5, axis=-1)
    bsz, t_len, _ = q.shape

    def heads(a):
        return a.reshape(bsz, t_len, HG_HEADS, -1)

    lbf = lb.astype(jnp.float32)

    def forget(z):
        zf32 = z.astype(jnp.float32)
        f = lbf + (1.0 - lbf) * jax.nn.sigmoid(zf32)
        k = (1.0 - lbf) * jax.nn.sigmoid(-zf32)
        return heads(jnp.log(f)), heads(k)

    qh = heads(q.astype(jnp.float32))
    vh = heads(i.astype(jnp.float32))
    s0f = s0.astype(jnp.float32)
    logf_f, k_f = forget(zf)
    logf_b, k_b = forget(zb)
    o_f, s_f = _gla_chunked(qh, k_f, vh, logf_f, s0f[:, 0])
    o_b, s_b = _gla_chunked(qh[:, ::-1], k_b[:, ::-1], vh[:, ::-1], logf_b[:, ::-1], s0f[:, 1])
    o = o_f + o_b[:, ::-1]
    o = o * lax.rsqrt(jnp.mean(jnp.square(o), axis=-1, keepdims=True) + LN_EPS) \
        * norm_g.astype(jnp.float32).reshape(HG_HEADS, HG_DV)
    o = o.reshape(bsz, t_len, HG_WIDTH).astype(b_in.dtype) * jax.nn.silu(g)
    return o @ w_out, jnp.stack([s_f, s_b], axis=1).astype(b_in.dtype)


def _chunk_sgu(c_in, ln_g, ln_b, w_s, b_s, w_out):
    u, v = jnp.split(c_in, 2, axis=-1)
    v = _layernorm(v, ln_g, ln_b)
    bsz, t_len, _ = v.shape
    n = t_len // SGU_CHUNK
    vc = v.reshape(bsz, n, SGU_CHUNK, SGU_GROUPS, SGU_GROUP_DIM)
    mixed = jnp.einsum('gts,bnsgc->bntgc', w_s, vc) + b_s.T[None, None, :, :, None]
    return (u * mixed.reshape(bsz, t_len, SGU_WIDTH)) @ w_out


def _token_mixer(m, p, lb, s0):
    a_in, b_in, c_in, gates = jnp.split(m @ p['w_in'], [A_IN, A_IN + B_IN, A_IN + B_IN + C_IN], axis=-1)
    ya = _conformer_conv(a_in, p['conv_w'], p['conv_b'], p['conv_ln_g'], p['conv_ln_b'], p['w_a_out'])
    yb, s_out = _hgrn2(b_in, lb, p['hgrn_norm_g'], p['w_b_out'], s0)
    yc = _chunk_sgu(c_in, p['sgu_ln_g'], p['sgu_ln_b'], p['sgu_w'], p['sgu_b'], p['w_c_out'])
    ga, gb, gc = jnp.split(jax.nn.sigmoid(gates), 3, axis=-1)
    return (ga * ya + gb * yb + gc * yc) @ p['w_o'], s_out


def _swiglu(h, w_in, w_out):
    gte, up = jnp.split(h @ w_in, 2, axis=-1)
    return (jax.nn.silu(gte) * up) @ w_out


def _layer(x, cond, p, lb, s0):
    mod = jax.nn.silu(cond) @ p['w_ada'] + p['b_ada']
    sh1, sc1, g1, sh2, sc2, g2 = jnp.split(mod[:, None, :], N_MOD, axis=-1)
    mix, s_out = _token_mixer(x * (1 + sc1) + sh1, p, lb, s0)
    x = _layernorm(DEEPNORM_ALPHA * x + g1 * mix, p['ln1_g'], p['ln1_b'])
    ffn = _swiglu(x * (1 + sc2) + sh2, p['w_ffn_in'], p['w_ffn_out'])
    x = _layernorm(DEEPNORM_ALPHA * x + g2 * ffn, p['ln2_g'], p['ln2_b'])
    return x, s_out


def setup_inputs(seed: int = 0) -> dict:
    key = jax.random.key(seed)
    ks = jax.random.split(key, 32)
    nrm = jax.random.normal
    f32 = jnp.float32
    d = D_MODEL
    beta = DEEPNORM_BETA
    return {
        'x_prompt': nrm(ks[0], (BATCH, SEQ, d), f32),
        'x_sample': nrm(ks[1], (DEC_BATCH, DEC_SEQ, d), f32),
        'state_hgrn': 0.5 * nrm(ks[2], (DEC_BATCH, DEPTH, 2, HG_HEADS, HG_DK, HG_DV), f32),
        'c': nrm(ks[3], (DEC_BATCH, d), f32),
        'c_ctx': nrm(ks[4], (d,), f32),
        'ln_in_g': 1.0 + 0.01 * nrm(ks[5], (d,), f32),
        'ln_in_b': 0.01 * nrm(ks[6], (d,), f32),
        'w_ada': 0.2 * d ** -0.5 * nrm(ks[7], (DEPTH, d, N_MOD * d), f32),
        'b_ada': 0.01 * nrm(ks[8], (DEPTH, N_MOD * d), f32),
        'w_in': d ** -0.5 * nrm(ks[9], (DEPTH, d, D_IN), f32),
        'conv_w': CONV_K ** -0.5 * nrm(ks[10], (DEPTH, CONV_K, CONV_WIDTH_A), f32),
        'conv_b': 0.01 * nrm(ks[11], (DEPTH, CONV_WIDTH_A), f32),
        'conv_ln_g': 1.0 + 0.01 * nrm(ks[12], (DEPTH, CONV_WIDTH_A), f32),
        'conv_ln_b': 0.01 * nrm(ks[13], (DEPTH, CONV_WIDTH_A), f32),
        'w_a_out': beta * CONV_WIDTH_A ** -0.5 * nrm(ks[14], (DEPTH, CONV_WIDTH_A, d), f32),
        'hgrn_lb': nrm(ks[15], (DEPTH, HG_WIDTH), f32),
        'hgrn_norm_g': 1.0 + 0.01 * nrm(ks[16], (DEPTH, HG_WIDTH), f32),
        'w_b_out': beta * HG_WIDTH ** -0.5 * nrm(ks[17], (DEPTH, HG_WIDTH, d), f32),
        'sgu_ln_g': 1.0 + 0.01 * nrm(ks[18], (DEPTH, SGU_WIDTH), f32),
        'sgu_ln_b': 0.01 * nrm(ks[19], (DEPTH, SGU_WIDTH), f32),
        'sgu_w': SGU_CHUNK ** -0.5 * nrm(ks[20], (DEPTH, SGU_GROUPS, SGU_CHUNK, SGU_CHUNK), f32),
        'sgu_b': 1.0 + 0.01 * nrm(ks[21], (DEPTH, SGU_GROUPS, SGU_CHUNK), f32),
        'w_c_out': beta * SGU_WIDTH ** -0.5 * nrm(ks[22], (DEPTH, SGU_WIDTH, d), f32),
        'w_o': beta * d ** -0.5 * nrm(ks[23], (DEPTH, d, d), f32),
        'ln1_g': 1.0 + 0.01 * nrm(ks[24], (DEPTH, d), f32),
        'ln1_b': 0.01 * nrm(ks[25], (DEPTH, d), f32),
        'w_ffn_in': d ** -0.5 * nrm(ks[26], (DEPTH, d, 2 * FFN_HIDDEN), f32),
        'w_ffn_out': beta * FFN_HIDDEN ** -0.5 * nrm(ks[27], (DEPTH, FFN_HIDDEN, d), f32),
        'ln2_g': 1.0 + 0.01 * nrm(ks[28], (DEPTH, d), f32),
        'ln2_b': 0.01 * nrm(ks[29], (DEPTH, d), f32),
    }


def reference(x_prompt, x_sample, state_hgrn, c, c_ctx, ln_in_g, ln_in_b, w_ada, b_ada, w_in,
              conv_w, conv_b, conv_ln_g, conv_ln_b, w_a_out, hgrn_lb, hgrn_norm_g, w_b_out,
              sgu_ln_g, sgu_ln_b, sgu_w, sgu_b, w_c_out, w_o, ln1_g, ln1_b, w_ffn_in, w_ffn_out,
              ln2_g, ln2_b):
    lb_soft = jax.nn.softmax(hgrn_lb.astype(jnp.float32), axis=0)
    lower_bounds = jnp.cumsum(lb_soft, axis=0) - lb_soft[0]

    ctx = _layernorm(x_prompt, ln_in_g, ln_in_b)
    lat = _layernorm(x_sample + _sincos_2d(x_sample.shape[1], D_MODEL, x_sample.dtype), ln_in_g, ln_in_b)
    s0_ctx = jnp.zeros((x_prompt.shape[0], 2, HG_HEADS, HG_DK, HG_DV), x_prompt.dtype)
    cond_ctx = c_ctx[None, :]

    ctx_states = []
    for l in range(DEPTH):
        p = dict(w_ada=w_ada[l], b_ada=b_ada[l], w_in=w_in[l], conv_w=conv_w[l], conv_b=conv_b[l],
                 conv_ln_g=conv_ln_g[l], conv_ln_b=conv_ln_b[l], w_a_out=w_a_out[l],
                 hgrn_norm_g=hgrn_norm_g[l], w_b_out=w_b_out[l], sgu_ln_g=sgu_ln_g[l],
                 sgu_ln_b=sgu_ln_b[l], sgu_w=sgu_w[l], sgu_b=sgu_b[l], w_c_out=w_c_out[l],
                 w_o=w_o[l], ln1_g=ln1_g[l], ln1_b=ln1_b[l], w_ffn_in=w_ffn_in[l],
                 w_ffn_out=w_ffn_out[l], ln2_g=ln2_g[l], ln2_b=ln2_b[l])
        ctx, s_ctx = _layer(ctx, cond_ctx, p, lower_bounds[l], s0_ctx)
        ctx_states.append(s_ctx)
        lat, _ = _layer(lat, c, p, lower_bounds[l], state_hgrn[:, l])

    new_state_hgrn = jnp.stack(ctx_states, axis=1)
    return (ctx, lat, new_state_hgrn)
```

```python
import math
from contextlib import ExitStack

import numpy as np
import concourse.bass as bass
import concourse.mybir as mybir
from concourse.bass_utils import run_bass_kernel_spmd

F32 = mybir.dt.float32
BF16 = mybir.dt.bfloat16
I32 = mybir.dt.int32
AF = mybir.ActivationFunctionType
ALU = mybir.AluOpType

D = 1024
KC = 8
T = 2048
HT = 1024
BLK = 512
NBLK = 4
NSEG = 8
SEG = 256
CH = 32
NCH = 64
FF = 2816
FKC = 22
L = 2
NH = 8
CONV_K = 31
HALO = 15
SEGP = SEG + 2 * HALO
ALPHA = (2 * L) ** 0.25
EPS = 1e-5
SLOT = 2048
PE_KEEPWARM = 3
NSLOT = 5

DEBUG = None


class Sem:
    def __init__(self, h):
        self.h = h
        self.count = 0


class Eng:
    def __init__(self, name, h, sem):
        self.name = name
        self.h = h
        self.sem = sem
        self.waited = {}


class Prog:
    def __init__(self, nc, es):
        self.nc = nc
        self.es = es
        self.lastw = {}
        self.readers = {}
        self.nsem = 0
        self.dma_sems = []

        def mk(name, h):
            return Eng(name, h, self.new_sem(name))

        self.pe = mk("pe", nc.tensor)
        self.dve = mk("dve", nc.vector)
        self.act = mk("act", nc.scalar)
        self.pool = mk("pool", nc.gpsimd)
        self.sp = mk("sp", nc.sync)
        self.out_toks = []

    def new_sem(self, name):
        self.nsem += 1
        sm = Sem(self.es.enter_context(self.nc.semaphore(f"s{self.nsem}_{name}")))
        self.dma_sems.append(sm)
        return sm

    def _wait(self, eng, tok):
        sem, val = tok
        if eng.waited.get(sem, 0) >= val:
            return
        eng.h.wait_ge(sem.h, val)
        eng.waited[sem] = val

    def _deps(self, R, W):
        deps = []
        for k in R:
            w = self.lastw.get(k)
            if w is not None:
                deps.append((w, "raw"))
        for k in W:
            w = self.lastw.get(k)
            if w is not None:
                deps.append((w, "waw"))
            for sem, val in self.readers.get(k, {}).items():
                deps.append(((sem, val), "war"))
        return deps

    def _record(self, tok, R, W):
        for k in W:
            self.lastw[k] = tok
            self.readers[k] = {}
        for k in R:
            d = self.readers.setdefault(k, {})
            if d.get(tok[0], 0) < tok[1]:
                d[tok[0]] = tok[1]

    def op(self, eng, fn, R=(), W=()):
        for tok, kind in self._deps(R, W):
            if tok[0] is eng.sem and kind != "raw":
                continue
            self._wait(eng, tok)
        ins = fn(eng.h)
        eng.sem.count += 1
        ins.then_inc(eng.sem.h, 1)
        tok = (eng.sem, eng.sem.count)
        self._record(tok, R, W)
        return tok

    def barrier(self):
        engs = [self.pe, self.dve, self.act, self.pool, self.sp]
        sems = [e.sem for e in engs] + list(self.dma_sems)
        for e in engs:
            for sm in sems:
                if sm is not e.sem and sm.count > 0:
                    self._wait(e, (sm, sm.count))

    def dma(self, q, sem, out, in_, R=(), W=(), is_output=False):
        for tok, kind in self._deps(R, W):
            self._wait(q, tok)
        if sem.count:
            self._wait(q, (sem, sem.count))
        ins = q.h.dma_start(out=out, in_=in_)
        sem.count += 16
        ins.then_inc(sem.h, 16)
        tok = (sem, sem.count)
        self._record(tok, R, W)
        if is_output:
            self.out_toks.append(tok)
        return tok


class PScope(ExitStack):
    def __init__(self, prog):
        super().__init__()
        self.prog = prog

    def __exit__(self, *exc):
        r = super().__exit__(*exc)
        if exc[0] is None:
            self.prog.barrier()
        return r


def _fm(v):
    return np.ascontiguousarray(np.asarray(v, np.float32).reshape(KC, 128).T)


def _wunits(w, col_groups):
    out = []
    for grp in col_groups:
        cols = np.concatenate([w[:, a:a + s] for a, s in grp], axis=1)
        out.append(cols.reshape(KC, 128, cols.shape[1]).transpose(1, 0, 2))
    return np.ascontiguousarray(np.stack(out, 0), dtype=np.float32)


A_IN = 2 * D
B_IN = 5 * D
C_IN = 2 * D
OFF_A = 0
OFF_B = A_IN
OFF_C = A_IN + B_IN
OFF_G = A_IN + B_IN + C_IN


def build_program(debug=None):
    nc = bass.Bass("TRN2", target_bir_lowering=False)

    def din(name, shape):
        return nc.dram_tensor(name, list(shape), F32, kind="ExternalInput").ap()

    x_d = din("x", [T, D])
    flags_d = din("flags", [128, 4])
    cond_d = din("cond", [128, KC])
    s0_d = din("s0", [L, 2, NH, 128, 128])
    lnin_d = din("lnin", [128, 2, KC])
    bada_d = din("bada", [L, 128, 48])
    pvec_d = din("pvec", [L, 128, 9, KC])
    lbraw_d = din("lbraw", [128, 2, KC])
    convw_d = din("convw", [L, 128, KC, CONV_K])
    sgug_d = din("sgug", [L, D])
    sgub_d = din("sgub", [L, D])
    sgubias_d = din("sgubias", [L, NH * 128])
    sguw_d = din("sguw", [L, 128, NH, 128])
    wada_d = din("wada", [L, 24, 128, KC, 256])
    winBq_d = din("winBq", [L, NH, 128, KC, 128])
    winBz_d = din("winBz", [L, NH, 128, KC, 256])
    winBi_d = din("winBi", [L, NH, 128, KC, 256])
    winA_d = din("winA", [L, 8, 128, KC, 256])
    winCv_d = din("winCv", [L, 4, 128, KC, 256])
    winCu_d = din("winCu", [L, 4, 128, KC, 256])
    winG_d = din("winG", [L, 3, 4, 128, KC, 256])
    wxo_d = din("wxo", [L, 3, 4, 128, KC, 256])
    wo_d = din("wo", [L, 4, 128, KC, 256])
    wffi_d = din("wffi", [L, 22, 128, KC, 256])
    wffo_d = din("wffo", [L, 8, 2, 128, 11, 128])

    y_d = nc.dram_tensor("y", [T, D], F32, kind="ExternalOutput").ap()
    st_d = nc.dram_tensor("st", [L, 2, NSEG, NH, 128, 128], F32, kind="ExternalOutput").ap()
    dbg_d = None
    if debug is not None:
        dbg_d = nc.dram_tensor("dbg", [128, debug[1]], F32, kind="ExternalOutput").ap()

    with ExitStack() as es:
        P = Prog(nc, es)
        block = es.enter_context(nc.Block())
        pe, dve, act, pool, sp = P.pe, P.dve, P.act, P.pool, P.sp

        def sb(name, shape, dt, stack=es):
            return stack.enter_context(nc.sbuf_tensor("sb_" + name, list(shape), dt))

        ps = [es.enter_context(nc.psum_tensor(f"ps{i}", [128, 512], F32)) for i in range(7)]
        psT = es.enter_context(nc.psum_tensor("psT", [128, 1024], BF16))
        PSK = [("ps", i) for i in range(7)]
        PSTK = ("ps", 7)

        X = sb("X", [128, KC, 2, HT], F32)
        wslots = [sb(f"wslot{i}", [128, SLOT], BF16) for i in range(NSLOT)]
        wsems = [P.new_sem(f"w{i}") for i in range(NSLOT)]
        ident_b = sb("ident_b", [128, 128], BF16)
        ident_f = sb("ident_f", [128, 128], F32)
        ones_f = sb("ones_f", [128, 128], F32)
        epsc = sb("epsc", [128, 1], F32)
        flags = sb("flags", [128, 4], F32)
        lnin = sb("lnin", [128, 2, KC], F32)
        scb = sb("scb", [128, KC], BF16)
        modSa = sb("modS", [128, L, 48], F32)
        modDa = sb("modD", [128, L, 4, KC], F32)
        badaa = sb("badaa", [128, L, 48], F32)
        pvec = sb("pvec", [128, 9, KC], F32)
        lbv = sb("lbv", [128, 3, KC], F32)
        lbraw = sb("lbraw", [128, 2, KC], F32)
        misc_sems = [P.new_sem(f"misc{i}") for i in range(8)]
        misc_i = [0]

        def msem():
            misc_i[0] += 1
            return misc_sems[misc_i[0] % len(misc_sems)]

        carry = flags[:, 0:1]
        peflag = flags[:, 1:2]

        def XK(kc, hf):
            return [("X", kc, hf, 0), ("X", kc, hf, 1)]

        class WS:
            units = []
            issued = 0
            cur = 0
            done = set()

        def ws_pump():
            while WS.issued < len(WS.units) and WS.issued <= WS.cur + NSLOT - 1:
                u = WS.issued
                if u >= NSLOT and (u - NSLOT) not in WS.done:
                    break
                ap_in, shp = WS.units[u]
                slot = u % NSLOT
                n = shp[0] * shp[1]
                ov = wslots[slot][:, 0:n].rearrange("p (k c) -> p k c", k=shp[0])
                P.dma(pool, wsems[slot], ov, ap_in, W=[("w", slot)])
                WS.issued += 1

        def ws_next(shp):
            u = WS.cur
            exp_ap, exp_shp = WS.units[u]
            assert exp_shp == shp, (u, exp_shp, shp)
            ws_pump()
            assert WS.issued > u, ("weight unit not issuable", u)
            WS.cur += 1
            ws_pump()
            slot = u % NSLOT
            n = shp[0] * shp[1]
            return wslots[slot][:, 0:n].rearrange("p (k c) -> p k c", k=shp[0]), ("w", slot), u

        def ws_done(u):
            WS.done.add(u)
            ws_pump()

        def unit_plan():
            for l in range(L):
                for u in range(24):
                    yield wada_d[l, u], (KC, 256)
            for l in range(L):
                for h in range(NH):
                    yield winBq_d[l, h], (KC, 128)
                    yield winBz_d[l, h], (KC, 256)
                    yield winBi_d[l, h], (KC, 256)
                for hf in range(2):
                    for br in (1, 2, 0):
                        if br == 2:
                            for u in range(4):
                                yield winCv_d[l, u], (KC, 256)
                            for u in range(4):
                                yield winCu_d[l, u], (KC, 256)
                        if br == 0:
                            for u in range(8):
                                yield winA_d[l, u], (KC, 256)
                        for u in range(4):
                            yield wxo_d[l, br, u], (KC, 256)
                            yield winG_d[l, br, u], (KC, 256)
                    for u in range(4):
                        yield wo_d[l, u], (KC, 256)
                for hf in range(2):
                    for u in range(22):
                        yield wffi_d[l, u], (KC, 256)
                    for n in range(8):
                        for q in range(2):
                            yield wffo_d[l, n, q], (11, 128)

        WS.units = list(unit_plan())
        if debug is not None and debug[0] in ("input",):
            WS.units = []

        def c_ident(h):
            h.memset(ident_f[:], 1.0)
            return h.affine_select(out=ident_f[:], in_=ident_f[:], pattern=[[-1, 128]],
                                   compare_op=ALU.is_equal, fill=0.0, base=0, channel_multiplier=1)
        P.op(pool, c_ident, W=["ident_f"])
        P.op(pool, lambda h: h.tensor_copy(out=ident_b[:], in_=ident_f[:]), R=["ident_f"], W=["ident_b"])
        P.op(pool, lambda h: h.memset(ones_f[:], 1.0), W=["ones_f"])
        P.op(pool, lambda h: h.memset(epsc[:], EPS), W=["epsc"])
        P.dma(sp, msem(), flags[:], flags_d, W=["flags"])
        P.dma(sp, msem(), lnin[:], lnin_d, W=["lnin"])
        P.dma(sp, msem(), lbraw[:], lbraw_d, W=["lbraw"])

        with PScope(P) as s_in:
            W2 = sb("W2", [128, 512], F32, s_in)
            PH = sb("PH", [128, 512], F32, s_in)
            pe_c = sb("pe_c", [128, 512], F32, s_in)
            jint = sb("jint", [128, 256], I32, s_in)
            pint = sb("pint", [128, 16], I32, s_in)
            pcol = sb("pcol", [128, 4], F32, s_in)
            rr = sb("rr", [128, 16], F32, s_in)
            argt = [sb(f"argt{i}", [128, 512], F32, s_in) for i in range(2)]
            xt = [sb(f"xt{i}", [128, D], F32, s_in) for i in range(2)]
            xsem = [P.new_sem(f"x{i}") for i in range(2)]
            stt = sb("stt", [128, 2, 8], F32, s_in)
            cond = sb("cond", [128, KC], F32, s_in)
            ctmp = sb("ctmp", [128, KC], F32, s_in)

            P.dma(sp, msem(), cond[:], cond_d, W=["cond"])
            P.op(pool, lambda h: h.iota(jint[:], pattern=[[1, 256]], base=0, channel_multiplier=0), W=["jint"])
            P.op(pool, lambda h: h.tensor_copy(out=W2[:, 0:256], in_=jint[:]), R=["jint"], W=["W2a"])
            P.op(act, lambda h: h.activation(out=W2[:, 0:256], in_=W2[:, 0:256], func=AF.Exp,
                                             scale=-math.log(10000.0) / 256.0), R=["W2a"], W=["W2a"])
            P.op(pool, lambda h: h.tensor_copy(out=W2[:, 256:512], in_=W2[:, 0:256]), R=["W2a"], W=["W2b"])
            P.op(pool, lambda h: h.memset(PH[:, 0:256], 0.0), W=["PHa"])
            P.op(pool, lambda h: h.memset(PH[:, 256:512], math.pi / 2), W=["PHb"])
            WK = ["W2a", "W2b", "PHa", "PHb"]
            P.op(pool, lambda h: h.iota(pint[:, 0:1], pattern=[[0, 1]], base=0, channel_multiplier=1), W=["pint0"])
            P.op(pool, lambda h: h.tensor_copy(out=pcol[:, 0:1], in_=pint[:, 0:1]), R=["pint0"], W=["pcol0"])
            P.op(pool, lambda h: h.tensor_single_scalar(out=pcol[:, 1:2], in_=pcol[:, 0:1], scalar=64.0, op=ALU.is_ge),
                 R=["pcol0"], W=["pcol1"])
            P.op(dve, lambda h: h.scalar_tensor_tensor(out=pcol[:, 2:3], in0=pcol[:, 1:2], scalar=-64.0,
                                                        in1=pcol[:, 0:1], op0=ALU.mult, op1=ALU.add),
                 R=["pcol0", "pcol1"], W=["pcol2"])
            P.op(pool, lambda h: h.iota(pint[:, 0:16], pattern=[[2, 16]], base=0, channel_multiplier=0),
                 R=["pcol0"], W=["pint0"])
            P.op(pool, lambda h: h.tensor_copy(out=rr[:], in_=pint[:, 0:16]), R=["pint0"], W=["rr"])
            P.op(pool, lambda h: h.tensor_scalar(out=rr[:], in0=rr[:], scalar1=pcol[:, 1:2], scalar2=None, op0=ALU.add),
                 R=["rr", "pcol1"], W=["rr"])

            nint = sb("nint", [128, 512], I32, s_in)
            nflt = sb("nflt", [128, 512], F32, s_in)

            def pe_table(dst, dkey, scal_ap, scal_keys, dst_flagmul, np_=128):
                nf, ni = nflt[0:np_, :], nint[0:np_, :]
                P.op(dve, lambda h: h.scalar_tensor_tensor(out=dst, in0=W2[0:np_, :], scalar=scal_ap, in1=PH[0:np_, :],
                                                           op0=ALU.mult, op1=ALU.add),
                     R=WK + scal_keys, W=[dkey])
                P.op(dve, lambda h: h.tensor_scalar(out=nf, in0=dst, scalar1=1.0 / (2 * math.pi), scalar2=0.5,
                                                    op0=ALU.mult, op1=ALU.add), R=[dkey], W=["nflt"])
                P.op(dve, lambda h: h.tensor_copy(out=ni, in_=nf), R=["nflt"], W=["nint"])
                P.op(dve, lambda h: h.tensor_copy(out=nf, in_=ni), R=["nint"], W=["nflt"])
                P.op(dve, lambda h: h.scalar_tensor_tensor(out=dst, in0=nf, scalar=-2 * math.pi, in1=dst,
                                                           op0=ALU.mult, op1=ALU.add), R=["nflt", dkey], W=[dkey])
                P.op(dve, lambda h: h.tensor_single_scalar(out=nf, in_=dst, scalar=-math.pi, op=ALU.is_lt),
                     R=[dkey], W=["nflt"])
                P.op(dve, lambda h: h.scalar_tensor_tensor(out=dst, in0=nf, scalar=2 * math.pi, in1=dst,
                                                           op0=ALU.mult, op1=ALU.add), R=["nflt", dkey], W=[dkey])
                P.op(dve, lambda h: h.tensor_scalar(out=dst, in0=dst, scalar1=math.pi, scalar2=-math.pi,
                                                    op0=ALU.min, op1=ALU.max), R=[dkey], W=[dkey])
                P.op(act, lambda h: h.activation(out=dst, in_=dst, func=AF.Sin), R=[dkey], W=[dkey])
                if dst_flagmul:
                    P.op(dve, lambda h: h.tensor_scalar(out=dst, in0=dst, scalar1=peflag, scalar2=None, op0=ALU.mult),
                         R=[dkey, "flags"], W=[dkey])

            negpi = sb("negpi", [128, 1], F32, s_in)
            P.op(pool, lambda h: h.memset(negpi[:], -math.pi), W=["negpi"])
            pe_table(pe_c[:], "pe_c", pcol[:, 2:3], ["pcol2"], True)
            Rtab = sb("Rtab", [32, 512], F32, s_in)
            Sel = sb("Sel", [32, 16, 128], F32, s_in)
            pe_table(Rtab[:], "Rtab", pcol[0:32, 0:1], ["pcol0"], False, np_=32)

            def c_sel(h):
                h.memset(Sel[:], 1.0)
                h.affine_select(out=Sel[:, :, 0:64], in_=Sel[:, :, 0:64], pattern=[[-2, 16], [0, 64]],
                                compare_op=ALU.is_equal, fill=0.0, base=0, channel_multiplier=1)
                return h.affine_select(out=Sel[:, :, 64:128], in_=Sel[:, :, 64:128], pattern=[[-2, 16], [0, 64]],
                                       compare_op=ALU.is_equal, fill=0.0, base=-1, channel_multiplier=1)
            P.op(pool, c_sel, W=["Sel"])

            P.op(act, lambda h: h.activation(out=ctmp[:], in_=cond[:], func=AF.Exp, scale=-1.0), R=["cond"], W=["ctmp"])
            P.op(dve, lambda h: h.tensor_scalar(out=ctmp[:], in0=ctmp[:], scalar1=1.0, scalar2=None, op0=ALU.add),
                 R=["ctmp"], W=["ctmp"])
            P.op(dve, lambda h: h.reciprocal(out=ctmp[:], in_=ctmp[:]), R=["ctmp"], W=["ctmp"])
            P.op(dve, lambda h: h.tensor_tensor(out=scb[:], in0=cond[:], in1=ctmp[:], op=ALU.mult),
                 R=["cond", "ctmp"], W=["scb"])

            for l_ in range(L):
                P.dma(sp, msem(), badaa[:, l_, :], bada_d[l_], W=["bada"])
            ada_list = [(l_, u) for l_ in range(L) for u in range(24)]

            def emit_ada(l_, u):
                wv, wk, wu = ws_next((KC, 256))

                def ada(h):
                    ins = None
                    for jj in range(2):
                        j = 48 * l_ + 2 * u + jj
                        for kc in range(KC):
                            ins = h.matmul(ps[6][:, j:j + 1], lhsT=wv[:, kc, jj * 128:(jj + 1) * 128],
                                           rhs=scb[:, kc:kc + 1], start=(kc == 0), stop=(kc == KC - 1))
                    return ins
                P.op(pe, ada, R=[wk, "scb"], W=[PSK[6]])
                ws_done(wu)

            for i in range(T // 128):
                for l_, u in ada_list[3 * i:3 * i + 3]:
                    if WS.units:
                        emit_ada(l_, u)
                s = i % 2
                hf, tc = divmod(i, 8)
                xk = ("xt", s)
                ak = ("argt", s)
                P.dma(sp, xsem[s], xt[s][:], x_d[i * 128:(i + 1) * 128, :], W=[xk])
                pbk = 2 + i % 2
                P.op(pe, lambda h: h.matmul(ps[pbk][:], lhsT=Sel[:, i, :], rhs=Rtab[:], start=True, stop=True),
                     R=["Sel", "Rtab"], W=[PSK[pbk]])
                P.op(dve, lambda h: h.scalar_tensor_tensor(out=xt[s][:, 0:512], in0=ps[pbk][:], scalar=peflag,
                                                           in1=xt[s][:, 0:512], op0=ALU.mult, op1=ALU.add),
                     R=[xk, "flags"], W=[PSK[pbk], xk])
                P.op(pool, lambda h: h.tensor_tensor(out=xt[s][:, 512:1024], in0=xt[s][:, 512:1024], in1=pe_c[:],
                                                     op=ALU.add), R=[xk, "pe_c"], W=[xk])

                def stats(h):
                    h.bn_stats(out=stt[:, 0, 0:6], in_=xt[s][:, 0:512])
                    return h.bn_stats(out=stt[:, 1, 0:6], in_=xt[s][:, 512:1024])
                P.op(dve, stats, R=[xk], W=["stt"])
                P.op(dve, lambda h: h.bn_aggr(out=stt[:, 0, 6:8], in_=stt[:, 0:2, 0:6]), R=["stt"], W=["mv"])
                P.op(act, lambda h: h.activation(out=stt[:, 1, 6:7], in_=stt[:, 0, 7:8], func=AF.Ln, bias=epsc[:, 0:1]),
                     R=["mv", "epsc"], W=["rstd"])
                P.op(act, lambda h: h.activation(out=stt[:, 1, 6:7], in_=stt[:, 1, 6:7], func=AF.Exp, scale=-0.5),
                     R=["rstd"], W=["rstd"])
                P.op(dve, lambda h: h.tensor_scalar(out=stt[:, 1, 7:8], in0=stt[:, 0, 6:7], scalar1=stt[:, 1, 6:7],
                                                    scalar2=-1.0, op0=ALU.mult, op1=ALU.mult),
                     R=["mv", "rstd"], W=["nmr"])
                P.op(dve, lambda h: h.tensor_scalar(out=xt[s][:], in0=xt[s][:], scalar1=stt[:, 1, 6:7],
                                                    scalar2=stt[:, 1, 7:8], op0=ALU.mult, op1=ALU.add),
                     R=[xk, "rstd", "nmr"], W=[xk])
                for half in range(2):
                    pb = ps[half]

                    def tr(h, half=half, pb=pb):
                        ins = None
                        for q in range(4):
                            kc = half * 4 + q
                            ins = h.transpose(pb[:, q * 128:(q + 1) * 128], xt[s][:, kc * 128:(kc + 1) * 128], ident_f[:])
                        return ins
                    P.op(pe, tr, R=[xk, "ident_f"], W=[PSK[half]])
                    for q in range(4):
                        kc = half * 4 + q
                        P.op(act, lambda h, kc=kc, q=q, pb=pb: h.activation(
                            out=X[:, kc, hf, tc * 128:(tc + 1) * 128], in_=pb[:, q * 128:(q + 1) * 128],
                            func=AF.Identity, scale=lnin[:, 0, kc:kc + 1], bias=lnin[:, 1, kc:kc + 1]),
                            R=["lnin"], W=[PSK[half]] + XK(kc, hf))

        if WS.units:
            P.op(dve, lambda h: h.tensor_tensor(out=modSa[:].rearrange("p l c -> p (l c)"), in0=ps[6][:, 0:96],
                                                in1=badaa[:].rearrange("p l c -> p (l c)"), op=ALU.add),
                 R=["bada"], W=[PSK[6], "mod"])
            for l_ in range(L):
                P.op(dve, lambda h: h.tensor_scalar(out=modDa[:, l_, 0, :], in0=modSa[:, l_, 8:16], scalar1=1.0, scalar2=None,
                                                    op0=ALU.add), R=["mod"], W=[("modD0", l_)])
                P.op(dve, lambda h: h.reciprocal(out=modDa[:, l_, 1, :], in_=modDa[:, l_, 0, :]), R=[("modD0", l_)], W=[("modD1", l_)])
                P.op(dve, lambda h: h.tensor_scalar(out=modDa[:, l_, 2, :], in0=modDa[:, l_, 1, :], scalar1=ALPHA, scalar2=None,
                                                    op0=ALU.mult), R=[("modD1", l_)], W=[("modD2", l_)])
                P.op(dve, lambda h: h.tensor_scalar(out=modDa[:, l_, 3, :], in0=modSa[:, l_, 32:40], scalar1=1.0, scalar2=None,
                                                    op0=ALU.add), R=["mod"], W=[("modD3", l_)])

        def alias_sync(new_keys, old_keys):
            merged = {}
            for k in old_keys:
                w = P.lastw.get(k)
                if w is not None:
                    merged[w[0]] = max(merged.get(w[0], 0), w[1])
                for sem, val in P.readers.get(k, {}).items():
                    merged[sem] = max(merged.get(sem, 0), val)
            for k in new_keys:
                P.lastw.pop(k, None)
                P.readers[k] = dict(merged)

        def rview(kc, hf):
            return X[:, kc, hf, 0:512].bitcast(BF16)

        def bview(kc, hf):
            return X[:, kc, hf, 512:1024].bitcast(BF16)

        def MK(kc, blk):
            return ("m", kc, blk)

        dbg_stop = [debug is not None and debug[0] == "input"]

        def dump(ap_list, keys):
            dsem = P.new_sem("dbg")
            off = 0
            for ap, k in zip(ap_list, keys):
                n = ap.shape[-1] if len(ap.shape) == 2 else int(np.prod(ap.shape[1:]))
                P.dma(sp if ap.dtype == F32 else pool, dsem, dbg_d[0:ap.shape[0], off:off + n], ap, R=k, is_output=True)
                off += n
            dbg_stop[0] = True

        def ln_feature_major(t_ap, t_keys, out_ap, out_keys, g_ap, b_ap, scratch, nb2=2, silu=False, gb_keys=("pvec",)):
            sqa, mean, rstd, t1a, sqb, t1b = scratch
            sqs = [sqa, sqb]
            t1s = [t1a, t1b]
            for b2 in range(nb2):
                def s1(h):
                    ins = None
                    for n in range(KC):
                        ins = h.matmul(ps[5][:], lhsT=ones_f[:], rhs=t_ap(n, b2), start=(n == 0), stop=(n == KC - 1))
                    return ins
                P.op(pe, s1, R=["ones_f"] + [k for n in range(KC) for k in t_keys(n, b2)], W=[PSK[5]])
                for n in range(KC):
                    sq = sqs[n % 2]
                    P.op(act, lambda h: h.activation(out=sq[:], in_=t_ap(n, b2), func=AF.Square),
                         R=t_keys(n, b2), W=[("lnsq", n % 2)])
                    P.op(pe, lambda h: h.matmul(ps[6][:], lhsT=ones_f[:], rhs=sq[:], start=(n == 0), stop=(n == KC - 1)),
                         R=["ones_f", ("lnsq", n % 2)], W=[PSK[6]])
                P.op(act, lambda h: h.activation(out=mean[:], in_=ps[5][:], func=AF.Identity, scale=1.0 / D),
                     W=[PSK[5], "lnmean"])
                P.op(dve, lambda h: h.tensor_tensor(out=t1a[:], in0=mean[:], in1=mean[:], op=ALU.mult),
                     R=["lnmean"], W=[("lnt1", 0)])
                P.op(dve, lambda h: h.scalar_tensor_tensor(out=rstd[:], in0=ps[6][:], scalar=1.0 / D, in1=t1a[:],
                                                           op0=ALU.mult, op1=ALU.subtract),
                     R=[("lnt1", 0)], W=[PSK[6], "lnrstd"])
                P.op(act, lambda h: h.activation(out=rstd[:], in_=rstd[:], func=AF.Ln, bias=epsc[:, 0:1]),
                     R=["lnrstd", "epsc"], W=["lnrstd"])
                P.op(act, lambda h: h.activation(out=rstd[:], in_=rstd[:], func=AF.Exp, scale=-0.5),
                     R=["lnrstd"], W=["lnrstd"])
                for n in range(KC):
                    t1 = t1s[n % 2]
                    tk = ("lnt1", n % 2)
                    P.op(dve, lambda h: h.tensor_tensor(out=t1[:], in0=t_ap(n, b2), in1=mean[:], op=ALU.subtract),
                         R=t_keys(n, b2) + ["lnmean"], W=[tk])
                    P.op(dve, lambda h: h.tensor_tensor(out=t1[:], in0=t1[:], in1=rstd[:], op=ALU.mult),
                         R=[tk, "lnrstd"], W=[tk])
                    P.op(act, lambda h: h.activation(out=out_ap(n, b2), in_=t1[:], func=(AF.Silu if silu else AF.Identity),
                                                     scale=g_ap(n), bias=b_ap(n)),
                         R=[tk] + list(gb_keys), W=out_keys(n, b2))

        maskF = sb("maskF", [32, 32], F32)
        maskB = sb("maskB", [32, 32], F32)

        def c_mask(h, mt, pat, cm):
            h.memset(mt[:], 1.0)
            return h.affine_select(out=mt[:], in_=mt[:], pattern=[[pat, 32]], compare_op=ALU.is_ge, fill=0.0,
                                   base=0, channel_multiplier=cm)
        P.op(pool, lambda h: c_mask(h, maskF, 1, -1), W=["maskF"])
        P.op(pool, lambda h: c_mask(h, maskB, -1, 1), W=["maskB"])


        NRB = 3
        ssems = [P.new_sem(f"S{i}") for i in range(NRB)]
        onec = sb("onec", [128, 1], F32)
        P.op(pool, lambda h: h.memset(onec[:], 1.0), W=["onec"])
        TriF = sb("TriF", [128, 128], F32)
        TriB = sb("TriB", [128, 128], F32)

        def c_tri(h, mt, pat, cm):
            h.memset(mt[:], 1.0)
            ins = h.affine_select(out=mt[:], in_=mt[:], pattern=[[pat, 128]], compare_op=ALU.is_ge, fill=0.0,
                                  base=0, channel_multiplier=cm)
            for c in range(4):
                blk_ = mt[:, c * CH:(c + 1) * CH]
                h.affine_select(out=blk_, in_=blk_, pattern=[[0, CH]], compare_op=ALU.is_ge, fill=0.0,
                                base=-c * CH, channel_multiplier=1)
                ins = h.affine_select(out=blk_, in_=blk_, pattern=[[0, CH]], compare_op=ALU.is_ge, fill=0.0,
                                      base=c * CH + CH - 1, channel_multiplier=-1)
            return ins
        P.op(pool, lambda h: c_tri(h, TriF, 1, -1), W=["Tri"])
        P.op(pool, lambda h: c_tri(h, TriB, -1, 1), W=["Tri"])

        for l in range(L):
            if dbg_stop[0]:
                break
            last_layer = (l == L - 1)
            with PScope(P) as s_mix:
                m = sb(f"m{l}", [128, KC, T], BF16, s_mix)
                P.dma(sp, msem(), pvec[:], pvec_d[l], W=["pvec"])

                modS = modSa[:, l, :]
                modD = modDa[:, l, :, :]
                MODK = ["mod", ("modD0", l), ("modD1", l), ("modD2", l), ("modD3", l)]
                sh1 = lambda kc: modSa[:, l, kc:kc + 1]
                g1 = lambda kc: modSa[:, l, 16 + kc:17 + kc]
                sh2 = lambda kc: modSa[:, l, 24 + kc:25 + kc]
                g2 = lambda kc: modSa[:, l, 40 + kc:41 + kc]
                opsc1 = lambda kc: modDa[:, l, 0, kc:kc + 1]
                inv1 = lambda kc: modDa[:, l, 1, kc:kc + 1]
                ainv1 = lambda kc: modDa[:, l, 2, kc:kc + 1]
                opsc2 = lambda kc: modDa[:, l, 3, kc:kc + 1]

                if l == 0:
                    P.op(pool, lambda h: h.memset(lbv[:, 0, :], 0.0), W=["lbv"])
                    P.op(pool, lambda h: h.memset(lbv[:, 1, :], 1.0), W=["lbv"])
                else:
                    P.op(dve, lambda h: h.tensor_tensor(out=lbv[:, 2, :], in0=lbraw[:, 0, :], in1=lbraw[:, 1, :], op=ALU.subtract),
                         R=["lbraw"], W=["lbv2"])
                    P.op(act, lambda h: h.activation(out=lbv[:, 2, :], in_=lbv[:, 2, :], func=AF.Exp), R=["lbv2"], W=["lbv2"])
                    P.op(dve, lambda h: h.tensor_scalar(out=lbv[:, 0, :], in0=lbv[:, 2, :], scalar1=1.0, scalar2=None, op0=ALU.add),
                         R=["lbv2"], W=["lbv"])
                    P.op(dve, lambda h: h.reciprocal(out=lbv[:, 0, :], in_=lbv[:, 0, :]), R=["lbv"], W=["lbv"])
                    P.op(dve, lambda h: h.tensor_tensor(out=lbv[:, 1, :], in0=lbv[:, 2, :], in1=lbv[:, 0, :], op=ALU.mult),
                         R=["lbv", "lbv2"], W=["lbv"])
                lb_ap = lambda hh: lbv[:, 0, hh:hh + 1]
                oml_ap = lambda hh: lbv[:, 1, hh:hh + 1]

                with PScope(P) as s_r:
                    rt32 = [sb(f"rt32_{l}_{i}", [128, HT], F32, s_r) for i in range(2)]
                    rtb = [sb(f"rtb_{l}_{i}", [128, HT], BF16, s_r) for i in range(2)]
                    cnt = 0
                    for kc in range(KC):
                        for hf in range(2):
                            sl = cnt % 2
                            cnt += 1
                            mk = [MK(kc, 2 * hf), MK(kc, 2 * hf + 1)]
                            P.op(act, lambda h: h.activation(out=m[:, kc, hf * HT:(hf + 1) * HT], in_=X[:, kc, hf, :],
                                                             func=AF.Identity, scale=opsc1(kc), bias=sh1(kc)),
                                 R=XK(kc, hf) + MODK, W=mk)
                            P.op(dve, lambda h: h.tensor_scalar(out=rt32[sl][:], in0=m[:, kc, hf * HT:(hf + 1) * HT],
                                                                scalar1=sh1(kc), scalar2=inv1(kc),
                                                                op0=ALU.subtract, op1=ALU.mult),
                                 R=mk + MODK, W=[("rt32", sl)])
                            P.op(dve, lambda h: h.tensor_tensor(out=rtb[sl][:], in0=X[:, kc, hf, :], in1=rt32[sl][:],
                                                                op=ALU.subtract),
                                 R=XK(kc, hf) + [("rt32", sl)], W=[("rtb", sl)])
                            P.op(act, lambda h: h.activation(out=rview(kc, hf), in_=rtb[sl][:], func=AF.Copy),
                                 R=[("rtb", sl)], W=[("X", kc, hf, 0)])

                if debug is not None and debug[0] == "mod" and l == debug[2]:
                    dump([modSa[:, l, :]] + [m[:, kc, :] for kc in range(KC)] + [rview(kc, hf) for kc in range(KC) for hf in range(2)],
                         [MODK] + [[MK(kc, b) for b in range(4)] for kc in range(KC)] +
                         [[("X", kc, hf, 0)] for kc in range(KC) for hf in range(2)])
                    break

                with PScope(P) as s_b:
                    NST = 2
                    ITs = [sb(f"IT{l}_{i}", [128, T], BF16, s_b) for i in range(NST)]
                    SGs = [sb(f"SG{l}_{i}", [128, T], BF16, s_b) for i in range(NST)]
                    OFs = [sb(f"OF{l}_{i}", [128, T], F32, s_b) for i in range(NST)]
                    DEC = sb(f"DEC{l}", [128, 3, 16], F32, s_b)
                    E1 = sb(f"E{l}", [128, BLK], F32, s_b)
                    NS1 = sb(f"NS{l}", [128, BLK], F32, s_b)
                    LFT1 = sb(f"LFT{l}", [128, BLK], F32, s_b)
                    EB1 = sb(f"EB{l}", [128, BLK], F32, s_b)
                    ENB1 = sb(f"ENB{l}", [128, BLK], F32, s_b)
                    GT1 = sb(f"GT{l}", [128, BLK], F32, s_b)
                    FE = sb(f"FE{l}", [128, BLK], F32, s_b)
                    FN = sb(f"FN{l}", [128, BLK], F32, s_b)
                    Qd = [sb(f"Qd{l}_{i}", [128, BLK], BF16, s_b) for i in range(3)]
                    Ktd = [sb(f"Ktd{l}_{i}", [128, BLK], BF16, s_b) for i in range(3)]
                    Khd = [sb(f"Khd{l}_{i}", [128, BLK], BF16, s_b) for i in range(2)]
                    KhTokd = [sb(f"KhTok{l}_{i}", [32, 16, 128], BF16, s_b) for i in range(2)]
                    VTokd = [sb(f"VTok{l}_{i}", [32, 16, 128], BF16, s_b) for i in range(2)]
                    MS = sb(f"MS{l}", [32, 16, 32], BF16, s_b)
                    RS = sb(f"RS{l}", [128, NRB, 5, 128], F32, s_b)
                    SB_ = sb(f"SBr{l}", [128, NRB, 5, 128], BF16, s_b)
                    psT3 = psT[:].rearrange("p (c v) -> p c v", v=128)
                    ps3b = ps[3][:].bitcast(BF16).rearrange("p (c v) -> p c v", v=128)

                    class SR:
                        rb = 0
                        ent = (0, 0)

                    def rb_next():
                        SR.rb = (SR.rb + 1) % NRB
                        return SR.rb

                    def RK(loc):
                        return ("RS", loc[0], loc[1])

                    def BK(loc):
                        return ("SBr", loc[0], loc[1])

                    def sweep_start(dirn, hh):
                        rb = rb_next()
                        P.dma(sp, ssems[rb], RS[:, rb, 0, :], s0_d[l, dirn, hh], W=[RK((rb, 0))])
                        P.op(act, lambda h: h.activation(out=SB_[:, rb, 0, :], in_=RS[:, rb, 0, :], func=AF.Copy),
                             R=[RK((rb, 0))], W=[BK((rb, 0))])
                        SR.ent = (rb, 0)

                    def emit_state(dirn, seg, hh):
                        e = SR.ent
                        P.dma(sp, ssems[e[0]], st_d[l, dirn, seg, hh], RS[:, e[0], e[1], :], R=[RK(e)], is_output=True)

                    tok_cnt = [0]

                    def run(*gens):
                        gens = [g for g in gens if g is not None]
                        while gens:
                            for g in list(gens):
                                try:
                                    next(g)
                                except StopIteration:
                                    gens.remove(g)
                            if PE_KEEPWARM:
                                def warm(h):
                                    ins = None
                                    for _ in range(PE_KEEPWARM):
                                        ins = h.matmul(ps[1][:], lhsT=m[:, 0, 0:128], rhs=m[:, 1, 0:512], start=True, stop=True)
                                    return ins
                                P.op(pe, warm, R=[MK(0, 0), MK(1, 0)], W=[PSK[1]])

                    def to_tok(src_fn, src_keys, dst, dst_key):
                        for rd in range(2):
                            pv, pk = (psT3, PSTK) if rd == 0 else (ps3b, PSK[3])

                            def tr(h):
                                ins = None
                                for q in range(8):
                                    ins = h.transpose(pv[0:32, q, :], src_fn(rd * 8 + q), ident_b[:])
                                return ins
                            P.op(pe, tr, R=src_keys + ["ident_b"], W=[pk])
                            yield
                            tok_cnt[0] += 1
                            if tok_cnt[0] % 2:
                                P.op(act, lambda h: h.activation(out=dst[:, rd * 8:(rd + 1) * 8, :], in_=pv[0:32, :, :],
                                                                 func=AF.Copy), W=[pk, dst_key + (rd,)])
                            else:
                                P.op(dve, lambda h: h.tensor_copy(out=dst[:, rd * 8:(rd + 1) * 8, :], in_=pv[0:32, :, :]),
                                     W=[pk, dst_key + (rd,)])
                            yield

                    def scan_block(dirn, hh, bi, k, kt_fn, q_fn, kq_keys, maskX, mask_key):
                        order = list(range(16)) if dirn == 0 else list(range(15, -1, -1))
                        tp_ = k % 2
                        KhTok, VTok = KhTokd[tp_], VTokd[tp_]
                        TOKK = [("KhTok", tp_, 0), ("KhTok", tp_, 1), ("VTok", tp_, 0), ("VTok", tp_, 1)]
                        obank, okey = ps[4], PSK[4]
                        dk = ("DEC", k % 3)

                        def scores(h):
                            ins = None
                            for cc in range(16):
                                ins = h.matmul(ps[3][0:32, cc * 32:(cc + 1) * 32], lhsT=kt_fn(cc), rhs=q_fn(cc),
                                               start=True, stop=True)
                            return ins
                        P.op(pe, scores, R=kq_keys, W=[PSK[3]])
                        yield
                        P.op(dve, lambda h: h.tensor_tensor(
                            out=MS[:], in0=ps[3][0:32, :].rearrange("p (c t) -> p c t", t=32),
                            in1=maskX[:].unsqueeze(1).to_broadcast([32, 16, 32]), op=ALU.mult),
                            R=[mask_key], W=[PSK[3], "MS"])
                        yield

                        def intra(h):
                            ins = None
                            for cc in range(16):
                                ins = h.matmul(obank[:, cc * 32:(cc + 1) * 32], lhsT=VTok[:, cc, :], rhs=MS[:, cc, :],
                                               start=(cc == 0), stop=False, skip_group_check=True)
                            return ins
                        P.op(pe, intra, R=["MS"] + TOKK[2:], W=[okey])
                        yield

                        def emit_dsm(rd):
                            chunks = order[4 * rd:4 * rd + 4]
                            bank = ps[5 + rd % 2]

                            def dsm(h):
                                ins = None
                                for j, cc in enumerate(chunks):
                                    ins = h.matmul(bank[:, j * 128:(j + 1) * 128], lhsT=KhTok[:, cc, :], rhs=VTok[:, cc, :],
                                                   start=True, stop=True)
                                return ins
                            P.op(pe, dsm, R=TOKK, W=[PSK[5 + rd % 2]])
                        emit_dsm(0)
                        yield
                        for rd in range(4):
                            chunks = order[4 * rd:4 * rd + 4]
                            bank = ps[5 + rd % 2]
                            bkey = PSK[5 + rd % 2]
                            c0 = 16 * bi + chunks[0]
                            if dirn == 0 and c0 % 8 == 0 and c0 > 0:
                                seg_done = c0 // 8 - 1
                            elif dirn == 1 and c0 % 8 == 7 and c0 < NCH - 1:
                                seg_done = (c0 + 1) // 8
                            else:
                                seg_done = None
                            rb = rb_next()
                            if seg_done is not None:
                                emit_state(dirn, seg_done, hh)
                                e = SR.ent
                                P.op(dve, lambda h: h.tensor_scalar(out=RS[:, rb, 0, :], in0=RS[:, e[0], e[1], :],
                                                                    scalar1=carry, scalar2=None, op0=ALU.mult),
                                     R=[RK(e), "flags"], W=[RK((rb, 0))])
                                P.op(act, lambda h: h.activation(out=SB_[:, rb, 0, :], in_=RS[:, rb, 0, :], func=AF.Copy),
                                     R=[RK((rb, 0))], W=[BK((rb, 0))])
                                SR.ent = (rb, 0)
                                yield
                            ents = []
                            for j, cc in enumerate(chunks):
                                e = SR.ent
                                ents.append(e)
                                P.op(dve, lambda h: h.scalar_tensor_tensor(
                                    out=RS[:, rb, j + 1, :], in0=RS[:, e[0], e[1], :], scalar=DEC[:, k % 3, cc:cc + 1],
                                    in1=bank[:, j * 128:(j + 1) * 128], op0=ALU.mult, op1=ALU.add),
                                    R=[RK(e), dk], W=[bkey, RK((rb, j + 1))])
                                SR.ent = (rb, j + 1)
                            yield
                            if rd < 3:
                                emit_dsm(rd + 1)
                            P.op(act, lambda h: h.activation(out=SB_[:, rb, 1:5, :], in_=RS[:, rb, 1:5, :], func=AF.Copy),
                                 R=[RK((rb, s_)) for s_ in range(1, 5)], W=[BK((rb, s_)) for s_ in range(1, 5)])
                            yield

                            def inter(h, chunks=chunks, ents=ents):
                                ins = None
                                for j, cc in enumerate(chunks):
                                    e = ents[j]
                                    ins = h.matmul(obank[:, cc * 32:(cc + 1) * 32], lhsT=SB_[:, e[0], e[1], :], rhs=q_fn(cc),
                                                   start=False, stop=True, skip_group_check=True)
                                return ins
                            P.op(pe, inter, R=[BK(e) for e in ents] + kq_keys, W=[okey])
                            yield

                    items = []
                    for hh in range(NH):
                        for bi in range(NBLK):
                            items.append((hh, 0, bi))
                        for bi in range(NBLK - 1, -1, -1):
                            items.append((hh, 1, bi))
                    wts = {}

                    def head_weights(hh):
                        if hh not in wts:
                            wts[hh] = [ws_next((KC, 128)), ws_next((KC, 256)), ws_next((KC, 256))]
                        return wts[hh]

                    def G_stage(k):
                        if k - 3 >= 0:
                            yield from Fin_stage(k - 3)
                        hh, dirn, bi = items[k]
                        (wq_, wqk, wqu), (wz_, wzk, wzu), (wi_, wik, wiu) = head_weights(hh)
                        st_ = hh % NST
                        IT, SG = ITs[st_], SGs[st_]
                        p3, par = k % 3, k % 2
                        MB = [MK(kc, bi) for kc in range(KC)]
                        sl = slice(bi * BLK, (bi + 1) * BLK)

                        def proj(bank, wv, c0):
                            def f(h):
                                ins = None
                                for kc in range(KC):
                                    ins = h.matmul(bank[:], lhsT=wv[:, kc, c0:c0 + 128], rhs=m[:, kc, bi * BLK:(bi + 1) * BLK],
                                                   start=(kc == 0), stop=(kc == KC - 1))
                                return ins
                            return f
                        P.op(pe, proj(ps[0], wz_, 128 * dirn), R=[wzk] + MB, W=[PSK[0]])
                        yield
                        P.op(act, lambda h: h.activation(out=E1[:], in_=ps[0][:], func=AF.Exp), W=[PSK[0], "E1"])
                        P.op(pe, proj(ps[2], wq_, 0), R=[wqk] + MB, W=[PSK[2]])
                        yield
                        P.op(act, lambda h: h.activation(out=NS1[:], in_=E1[:], func=AF.Ln, bias=onec[:, 0:1]),
                             R=["E1", "onec"], W=["NS1"])
                        yield
                        if l == 0:
                            P.op(dve, lambda h: h.tensor_tensor(out=E1[:], in0=ps[0][:], in1=NS1[:], op=ALU.subtract),
                                 R=["NS1", "E1"], W=[PSK[0], "E1"])
                            P.op(act, lambda h: h.activation(out=NS1[:], in_=NS1[:], func=AF.Exp, scale=-1.0),
                                 R=["NS1"], W=["NS1"])
                            yield
                        else:
                            P.op(act, lambda h: h.activation(out=NS1[:], in_=NS1[:], func=AF.Exp, scale=-1.0),
                                 R=["NS1"], W=["NS1"])
                            yield
                            P.op(dve, lambda h: h.tensor_tensor(out=E1[:], in0=E1[:], in1=NS1[:], op=ALU.mult),
                                 R=["E1", "NS1"], W=["E1"])
                            yield
                            P.op(act, lambda h: h.activation(out=NS1[:], in_=NS1[:], func=AF.Identity, scale=oml_ap(hh)),
                                 R=["NS1", "lbv"], W=["NS1"])
                            P.op(act, lambda h: h.activation(out=E1[:], in_=E1[:], func=AF.Ln, scale=oml_ap(hh),
                                                             bias=lb_ap(hh)), R=["E1", "lbv"], W=["E1"])
                            yield
                        def trl(h):
                            ins = None
                            for tl in range(4):
                                ins = h.transpose(ps[0][:, tl * 128:(tl + 1) * 128], E1[:, tl * 128:(tl + 1) * 128], ident_f[:])
                            return ins
                        P.op(pe, trl, R=["E1", "ident_f"], W=[PSK[0]])
                        yield
                        P.op(act, lambda h: h.activation(out=LFT1[:], in_=ps[0][:], func=AF.Copy), W=[PSK[0], "LFT1"])
                        yield

                        def cum(h):
                            ins = None
                            tri = TriF if dirn == 0 else TriB
                            for tl in range(4):
                                ins = h.matmul(ps[0][:, tl * 128:(tl + 1) * 128], lhsT=LFT1[:, tl * 128:(tl + 1) * 128],
                                               rhs=tri[:], start=True, stop=True)
                            return ins
                        P.op(pe, cum, R=["LFT1", "Tri"], W=[PSK[0]])
                        yield
                        P.op(act, lambda h: h.activation(out=EB1[:], in_=ps[0][:], func=AF.Exp), W=[PSK[0], "EB1"])
                        P.op(act, lambda h: h.activation(out=ENB1[:], in_=ps[0][:], func=AF.Exp, scale=-1.0), W=[PSK[0], "ENB1"])
                        yield
                        eb3 = EB1[:].rearrange("p (c j) -> p c j", j=CH)
                        ecol = (CH - 1) if dirn == 0 else 0
                        P.op(act, lambda h: h.activation(out=DEC[:, p3, :], in_=eb3[:, :, ecol], func=AF.Copy),
                             R=["EB1"], W=[("DEC", p3)])
                        P.op(dve, lambda h: h.tensor_tensor(out=Qd[p3][:], in0=ps[2][:], in1=EB1[:], op=ALU.mult),
                             R=["EB1"], W=[PSK[2], ("Qd", p3)])
                        yield
                        P.op(dve, lambda h: h.tensor_tensor(out=ENB1[:], in0=ENB1[:], in1=NS1[:], op=ALU.mult),
                             R=["ENB1", "NS1"], W=["ENB1"])
                        if dirn == 0:
                            P.op(pe, proj(ps[2], wi_, 0), R=[wik] + MB, W=[PSK[2]])
                        yield
                        P.op(act, lambda h: h.activation(out=Ktd[p3][:], in_=ENB1[:], func=AF.Copy), R=["ENB1"], W=[("Ktd", p3)])
                        yield
                        P.op(dve, lambda h: h.tensor_tensor(
                            out=Khd[par][:].rearrange("p (c j) -> p c j", j=CH),
                            in0=ENB1[:].rearrange("p (c j) -> p c j", j=CH),
                            in1=eb3[:, :, ecol:ecol + 1].to_broadcast([128, 16, CH]), op=ALU.mult),
                            R=["ENB1", "EB1"], W=[("Khd", par)])
                        yield
                        if dirn == 0:
                            P.op(act, lambda h: h.activation(out=IT[:, sl], in_=ps[2][:], func=AF.Copy), W=[PSK[2], ("IT", st_, bi)])
                            P.op(pe, proj(ps[2], wi_, 128), R=[wik] + MB, W=[PSK[2]])
                            yield
                            P.op(act, lambda h: h.activation(out=GT1[:], in_=ps[2][:], func=AF.Exp, scale=-1.0), W=[PSK[2], "GT1"])
                            yield
                            P.op(act, lambda h: h.activation(out=GT1[:], in_=GT1[:], func=AF.Ln, bias=onec[:, 0:1]),
                                 R=["GT1", "onec"], W=["GT1"])
                            yield
                            P.op(act, lambda h: h.activation(out=GT1[:], in_=GT1[:], func=AF.Exp, scale=-1.0), R=["GT1"], W=["GT1"])
                            yield
                            P.op(dve, lambda h: h.tensor_tensor(out=SG[:, sl], in0=ps[2][:], in1=GT1[:], op=ALU.mult),
                                 R=["GT1"], W=[PSK[2], ("SG", st_, bi)])
                            yield
                        if dirn == 0 and bi == NBLK - 1:
                            ws_done(wiu)
                        if dirn == 1 and bi == 0:
                            ws_done(wqu)
                            ws_done(wzu)

                    def T_stage(k):
                        hh, dirn, bi = items[k]
                        st_ = hh % NST
                        IT = ITs[st_]
                        par = k % 2
                        yield from to_tok(lambda cc: Khd[par][:, cc * CH:(cc + 1) * CH], [("Khd", par)], KhTokd[par], ("KhTok", par))
                        yield from to_tok(lambda cc: IT[:, bi * BLK + cc * CH: bi * BLK + (cc + 1) * CH], [("IT", st_, bi)],
                                          VTokd[par], ("VTok", par))

                    def S_stage(k):
                        hh, dirn, bi = items[k]
                        st_ = hh % NST
                        p3 = k % 3
                        sl = slice(bi * BLK, (bi + 1) * BLK)
                        if (dirn == 0 and bi == 0) or (dirn == 1 and bi == NBLK - 1):
                            sweep_start(dirn, hh)
                        yield from scan_block(dirn, hh, bi, k, lambda cc: Ktd[p3][:, cc * CH:(cc + 1) * CH],
                                              lambda cc: Qd[p3][:, cc * CH:(cc + 1) * CH],
                                              [("Ktd", p3), ("Qd", p3)], maskF if dirn == 0 else maskB,
                                              "maskF" if dirn == 0 else "maskB")
                        if dirn == 0:
                            P.op(act, lambda h: h.activation(out=OFs[st_][:, sl], in_=ps[4][:], func=AF.Copy),
                                 W=[PSK[4], ("OF", st_, bi)])
                            yield
                            if bi == NBLK - 1:
                                emit_state(0, NSEG - 1, hh)
                        else:
                            P.op(dve, lambda h: h.tensor_tensor(out=FE[:], in0=ps[4][:], in1=OFs[st_][:, sl], op=ALU.add),
                                 R=[("OF", st_, bi)], W=[PSK[4], "FE"])
                            yield
                            if bi == 0:
                                emit_state(1, 0, hh)

                    def Fin_stage(k):
                        hh, dirn, bi = items[k]
                        if dirn == 0:
                            return
                        yield
                        st_ = hh % NST
                        sl = slice(bi * BLK, (bi + 1) * BLK)
                        hf, b2 = divmod(bi, 2)
                        P.op(act, lambda h: h.activation(out=FN[:], in_=FE[:], func=AF.Square), R=["FE"], W=["FN"])
                        yield
                        P.op(pe, lambda h: h.matmul(ps[2][:], lhsT=ones_f[:], rhs=FN[:], start=True, stop=True),
                             R=["ones_f", "FN"], W=[PSK[2]])
                        yield
                        P.op(act, lambda h: h.activation(out=FN[:], in_=ps[2][:], func=AF.Ln, scale=1.0 / 128, bias=epsc[:, 0:1]),
                             R=["epsc"], W=[PSK[2], "FN"])
                        yield
                        P.op(act, lambda h: h.activation(out=FN[:], in_=FN[:], func=AF.Exp, scale=-0.5), R=["FN"], W=["FN"])
                        yield
                        P.op(dve, lambda h: h.tensor_tensor(out=FE[:], in0=FE[:], in1=FN[:], op=ALU.mult), R=["FE", "FN"], W=["FE"])
                        yield
                        P.op(dve, lambda h: h.scalar_tensor_tensor(
                            out=bview(hh, hf)[:, b2 * BLK:(b2 + 1) * BLK], in0=FE[:], scalar=pvec[:, 3, hh:hh + 1],
                            in1=SGs[st_][:, sl], op0=ALU.mult, op1=ALU.mult),
                            R=["FE", "pvec", ("SG", st_, bi)], W=[("X", hh, hf, 1)])
                        yield

                    NI = len(items)
                    stage = lambda fn, k: fn(k) if 0 <= k < NI else None
                    for step in range(-2, NI + 1):
                        run(stage(S_stage, step), stage(T_stage, step + 1), stage(G_stage, step + 2),
                            stage(Fin_stage, step - 1) if step + 2 >= NI else None)

                if debug is not None and debug[0] == "hgrn" and l == debug[2]:
                    dump([bview(kc, hf) for kc in range(KC) for hf in range(2)],
                         [[("X", kc, hf, 1)] for kc in range(KC) for hf in range(2)])
                    break
                with PScope(P) as s_h:
                    z = sb(f"z{l}", [128, KC, HT], F32, s_h)
                    actb = sb(f"actb{l}", [128, KC, HT], BF16, s_h)
                    sgt = [sb(f"sgt{l}_{i}", [128, BLK], F32, s_h) for i in range(2)]
                    tmp2 = sb(f"tmp2{l}", [128, BLK], F32, s_h)
                    actflat = actb[:].rearrange("p k t -> p (k t)")
                    VN = lambda tile: actflat[:, tile * 1024:(tile + 1) * 1024]
                    cact = actflat.rearrange("p (tile g c) -> p g tile c", tile=8, g=8)
                    CK_all = [("act", tile, g) for tile in range(8) for g in range(8)]
                    AK_all = [("acta", j, b2) for j in range(KC) for b2 in range(2)]
                    ZBK_all = [("zb", kc, b2) for kc in range(KC) for b2 in range(2)]
                    ycnt = [0]

                    def yproj(hf, br, rhs_fn, rhs_keys, first):
                        for u in range(4):
                            wx, wxk, wxu = ws_next((KC, 256))
                            wg, wgk, wgu = ws_next((KC, 256))
                            for nn in range(2):
                                n = 2 * u + nn
                                for b2 in range(2):
                                    blk = 2 * hf + b2
                                    pp = ycnt[0] % 2
                                    ycnt[0] += 1
                                    py, pg = ps[2 * pp], ps[2 * pp + 1]
                                    pyk, pgk = PSK[2 * pp], PSK[2 * pp + 1]

                                    def fy(h):
                                        ins = None
                                        for kc in range(KC):
                                            ins = h.matmul(py[:], lhsT=wx[:, kc, nn * 128:(nn + 1) * 128], rhs=rhs_fn(kc, b2),
                                                           start=(kc == 0), stop=(kc == KC - 1))
                                        return ins
                                    P.op(pe, fy, R=[wxk] + [k for kc in range(KC) for k in rhs_keys(kc, b2)], W=[pyk])

                                    def fg(h):
                                        ins = None
                                        for kc in range(KC):
                                            ins = h.matmul(pg[:], lhsT=wg[:, kc, nn * 128:(nn + 1) * 128],
                                                           rhs=m[:, kc, blk * BLK:(blk + 1) * BLK],
                                                           start=(kc == 0), stop=(kc == KC - 1))
                                        return ins
                                    P.op(pe, fg, R=[wgk] + [MK(kc, blk) for kc in range(KC)], W=[pgk])
                                    P.op(act, lambda h: h.activation(out=sgt[pp][:], in_=pg[:], func=AF.Sigmoid),
                                         W=[pgk, ("sgt", pp)])
                                    zsl = z[:, n, b2 * BLK:(b2 + 1) * BLK]
                                    if first:
                                        P.op(dve, lambda h: h.tensor_tensor(out=zsl, in0=py[:], in1=sgt[pp][:], op=ALU.mult),
                                             R=[("sgt", pp)], W=[pyk, ("z", n, b2)])
                                    else:
                                        P.op(dve, lambda h: h.tensor_tensor(out=tmp2[:], in0=py[:], in1=sgt[pp][:], op=ALU.mult),
                                             R=[("sgt", pp)], W=[pyk, "tmp2"])
                                        P.op(pool, lambda h: h.tensor_tensor(out=zsl, in0=zsl, in1=tmp2[:], op=ALU.add),
                                             R=["tmp2", ("z", n, b2)], W=[("z", n, b2)])
                            ws_done(wxu)
                            ws_done(wgu)

                    for hf in range(2):
                        h0 = hf * HT
                        yproj(hf, 1, lambda kc, b2: bview(kc, hf)[:, b2 * BLK:(b2 + 1) * BLK],
                              lambda kc, b2: [("X", kc, hf, 1)], True)
                        if debug is not None and debug[0] == "yb" and l == debug[2] and hf == debug[3]:
                            dump([z[:, n, :] for n in range(KC)], [[("z", n, 0), ("z", n, 1)] for n in range(KC)])
                            break

                        alias_sync(CK_all, ZBK_all + AK_all)
                        with PScope(P) as s_c:
                            gam_bc = sb(f"gam{l}{hf}", [128, D], F32, s_c)
                            BIAS = sb(f"BIAS{l}{hf}", [128, NH, 128], F32, s_c)
                            wTb = sb(f"wTb{l}{hf}", [128, NH, 128], BF16, s_c)
                            ones_b = sb(f"onesb{l}{hf}", [128, 128], BF16, s_c)
                            stc = sb(f"stc{l}{hf}", [128, 2, 8], F32, s_c)
                            ctmp2 = sb(f"ctmp2{l}{hf}", [128, BLK], F32, s_c)
                            s_bs = ExitStack()
                            bs_bc = sb(f"bsbc{l}{hf}", [128, NH, 128], F32, s_bs)
                            P.dma(sp, msem(), gam_bc[:], sgug_d[l:l + 1, :].to_broadcast([128, D]), W=["gam_bc"])
                            P.dma(sp, msem(), bs_bc[:].rearrange("p g t -> p (g t)"),
                                  sgubias_d[l:l + 1, :].to_broadcast([128, NH * 128]), W=["bs_bc"])
                            P.dma(pool, msem(), wTb[:], sguw_d[l], W=["wTb"])
                            P.op(pool, lambda h: h.memset(ones_b[:], 1.0), W=["ones_b"])
                            for gq in range(2):
                                def rs(h):
                                    ins = None
                                    for q in range(4):
                                        ins = h.matmul(ps[gq][:, q * 128:(q + 1) * 128], lhsT=ones_b[:], rhs=wTb[:, gq * 4 + q, :],
                                                       start=True, stop=True)
                                    return ins
                                P.op(pe, rs, R=["ones_b", "wTb"], W=[PSK[gq]])
                                for q in range(4):
                                    g = gq * 4 + q
                                    P.op(dve, lambda h: h.scalar_tensor_tensor(
                                        out=BIAS[:, g, :], in0=ps[gq][:, q * 128:(q + 1) * 128], scalar=pvec[:, 8, g:g + 1],
                                        in1=bs_bc[:, g, :], op0=ALU.mult, op1=ALU.add),
                                        R=["pvec", "bs_bc"], W=[PSK[gq], "BIAS"])
                            s_bs.close()
                            P.barrier()
                            vn32 = sb(f"vn32{l}{hf}", [128, D], F32, s_c)
                            wvs = [ws_next((KC, 256)) for _ in range(4)]
                            for ti in range(8):
                                t0 = h0 + ti * 128
                                pb0 = 2 * (ti % 2)
                                for q, (wv, wvk, _u) in enumerate(wvs):
                                    def fv(h, wv=wv, q=q):
                                        ins = None
                                        for kc in range(KC):
                                            ins = h.matmul(ps[pb0 + q // 2][:, (q % 2) * 256:(q % 2 + 1) * 256], lhsT=m[:, kc, t0:t0 + 128],
                                                           rhs=wv[:, kc, :], start=(kc == 0), stop=(kc == KC - 1))
                                        return ins
                                    P.op(pe, fv, R=[wvk] + [MK(kc, t0 // BLK) for kc in range(KC)], W=[PSK[pb0 + q // 2]])

                                def stats(h):
                                    h.bn_stats(out=stc[:, 0, 0:6], in_=ps[pb0][:])
                                    return h.bn_stats(out=stc[:, 1, 0:6], in_=ps[pb0 + 1][:])
                                P.op(dve, stats, W=[PSK[pb0], PSK[pb0 + 1], "stc"])
                                P.op(dve, lambda h: h.bn_aggr(out=stc[:, 0, 6:8], in_=stc[:, 0:2, 0:6]), R=["stc"], W=["cmv"])
                                P.op(act, lambda h: h.activation(out=stc[:, 1, 6:7], in_=stc[:, 0, 7:8], func=AF.Ln, bias=epsc[:, 0:1]),
                                     R=["cmv", "epsc"], W=["crstd"])
                                P.op(act, lambda h: h.activation(out=stc[:, 1, 6:7], in_=stc[:, 1, 6:7], func=AF.Exp, scale=-0.5),
                                     R=["crstd"], W=["crstd"])
                                P.op(dve, lambda h: h.tensor_scalar(out=stc[:, 1, 7:8], in0=stc[:, 0, 6:7], scalar1=stc[:, 1, 6:7],
                                                                    scalar2=-1.0, op0=ALU.mult, op1=ALU.mult),
                                     R=["cmv", "crstd"], W=["cnmr"])
                                for q in range(2):
                                    P.op(act, lambda h: h.activation(out=vn32[:, q * 512:(q + 1) * 512], in_=ps[pb0 + q][:], func=AF.Identity,
                                                                     scale=stc[:, 1, 6:7], bias=stc[:, 1, 7:8]),
                                         R=["crstd", "cnmr"], W=[PSK[pb0 + q], "vn32"])
                                P.op(dve, lambda h: h.tensor_tensor(out=VN(ti), in0=vn32[:], in1=gam_bc[:], op=ALU.mult),
                                     R=["vn32", "gam_bc"], W=[("act", ti, g) for g in range(8)])
                            for _wv, _wk, _u in wvs:
                                ws_done(_u)
                            for uq in range(4):
                                wu_, wuk, wuu = ws_next((KC, 256))
                                for gg in range(2):
                                    g = uq * 2 + gg
                                    for b2 in range(2):
                                        blk = 2 * hf + b2

                                        def fu(h):
                                            ins = None
                                            for kc in range(KC):
                                                ins = h.matmul(ps[2][:], lhsT=wu_[:, kc, gg * 128:(gg + 1) * 128],
                                                               rhs=m[:, kc, blk * BLK:(blk + 1) * BLK],
                                                               start=(kc == 0), stop=(kc == KC - 1))
                                            return ins
                                        P.op(pe, fu, R=[wuk] + [MK(kc, blk) for kc in range(KC)], W=[PSK[2]])

                                        def fm_(h):
                                            ins = None
                                            for tl in range(4):
                                                ti = b2 * 4 + tl
                                                ins = h.matmul(ps[3][:, tl * 128:(tl + 1) * 128], lhsT=VN(ti)[:, g * 128:(g + 1) * 128],
                                                               rhs=wTb[:, g, :], start=True, stop=True)
                                            return ins
                                        ck = [("act", b2 * 4 + tl, g) for tl in range(4)]
                                        P.op(pe, fm_, R=ck + ["wTb"], W=[PSK[3]])
                                        P.op(dve, lambda h: h.tensor_tensor(
                                            out=ctmp2[:].rearrange("p (a t) -> p a t", t=128),
                                            in0=ps[3][:].rearrange("p (a t) -> p a t", t=128),
                                            in1=BIAS[:, g, :].unsqueeze(1).to_broadcast([128, 4, 128]), op=ALU.add),
                                            R=["BIAS"], W=[PSK[3], "ctmp2"])
                                        P.op(dve, lambda h: h.tensor_tensor(
                                            out=cact[:, g, b2 * 4:(b2 + 1) * 4, :],
                                            in0=ps[2][:].rearrange("p (a t) -> p a t", t=128),
                                            in1=ctmp2[:].rearrange("p (a t) -> p a t", t=128), op=ALU.mult),
                                            R=["ctmp2"], W=[PSK[2]] + ck)
                                ws_done(wuu)
                        if debug is not None and debug[0] == "cact" and l == debug[2] and hf == debug[3]:
                            dump([actb[:, kc, :] for kc in range(KC)], [CK_all for kc in range(KC)])
                            break
                        yproj(hf, 2, lambda kc, b2: cact[:, kc, b2 * 4:(b2 + 1) * 4, :],
                              lambda kc, b2: [("act", b2 * 4 + tl, kc) for tl in range(4)], False)

                        alias_sync(AK_all, CK_all)
                        with PScope(P) as s_a:
                            hp = [sb(f"hp{l}{hf}{i}", [128, 4, SEGP], BF16, s_a) for i in range(2)]
                            DG = sb(f"DG{l}{hf}", [128, CONV_K, 128], BF16, s_a)
                            convw = sb(f"convw{l}{hf}", [128, KC, CONV_K], F32, s_a)
                            P.dma(sp, msem(), convw[:], convw_d[l], W=["convw"])
                            co32 = sb(f"co32{l}{hf}", [128, BLK], F32, s_a)
                            sq32 = sb(f"sq32{l}{hf}", [128, BLK], F32, s_a)
                            cmean = [sb(f"cmean{l}{hf}{i}", [128, BLK], F32, s_a) for i in range(2)]
                            crstd = [sb(f"crstd{l}{hf}{i}", [128, BLK], F32, s_a) for i in range(2)]
                            for u in range(8):
                                wa, wak, wau = ws_next((KC, 256))
                                for jj in range(1):
                                    j = u
                                    hs = j % 2
                                    hpk = ("hp", hs)
                                    hpt = hp[hs]
                                    if hf == 0:
                                        P.op(pool, lambda h: h.memset(hpt[:, 0, 0:HALO], 0.0), W=[hpk])
                                    else:
                                        P.op(pool, lambda h: h.memset(hpt[:, 3, SEG + HALO:SEGP], 0.0), W=[hpk])

                                    def vg(tok0, ntok, pv, pg):
                                        def f(h):
                                            ins = None
                                            for kc in range(KC):
                                                h.matmul(pv[:, 0:ntok], lhsT=wa[:, kc, jj * 128:(jj + 1) * 128],
                                                         rhs=m[:, kc, tok0:tok0 + ntok], start=(kc == 0), stop=(kc == KC - 1))
                                            for kc in range(KC):
                                                ins = h.matmul(pg[:, 0:ntok], lhsT=wa[:, kc, 128 + jj * 128:128 + (jj + 1) * 128],
                                                               rhs=m[:, kc, tok0:tok0 + ntok], start=(kc == 0), stop=(kc == KC - 1))
                                            return ins
                                        return f

                                    def halo(dst, pv, c0):
                                        P.op(dve, lambda h: h.scalar_tensor_tensor(out=dst, in0=pv[:, c0:c0 + HALO], scalar=carry,
                                                                                   in1=sgt[0][:, c0:c0 + HALO],
                                                                                   op0=ALU.mult, op1=ALU.mult),
                                             R=[("sgt", 0), "flags"], W=[PSK[0], hpk])

                                    for b2 in range(2):
                                        blk = 2 * hf + b2
                                        P.op(pe, vg(blk * BLK, BLK, ps[0], ps[1]), R=[wak] + [MK(kc, blk) for kc in range(KC)],
                                             W=[PSK[0], PSK[1]])
                                        P.op(act, lambda h: h.activation(out=sgt[0][:], in_=ps[1][:], func=AF.Sigmoid),
                                             W=[PSK[1], ("sgt", 0)])
                                        P.op(dve, lambda h: h.tensor_tensor(
                                            out=hpt[:, 2 * b2:2 * b2 + 2, HALO:HALO + SEG],
                                            in0=ps[0][:].rearrange("p (s t) -> p s t", t=SEG),
                                            in1=sgt[0][:].rearrange("p (s t) -> p s t", t=SEG), op=ALU.mult),
                                            R=[("sgt", 0)], W=[PSK[0], hpk])
                                        halo(hpt[:, 2 * b2 + 1, 0:HALO], ps[0], SEG - HALO)
                                        halo(hpt[:, 2 * b2, SEG + HALO:SEGP], ps[0], SEG)
                                        if b2 == 0:
                                            halo(hpt[:, 2, 0:HALO], ps[0], BLK - HALO)
                                        else:
                                            halo(hpt[:, 1, SEG + HALO:SEGP], ps[0], 0)
                                    if hf == 0:
                                        P.op(pe, vg(HT, HALO, ps[0], ps[1]), R=[wak] + [MK(kc, 2) for kc in range(KC)],
                                             W=[PSK[0], PSK[1]])
                                        P.op(act, lambda h: h.activation(out=sgt[0][:, 0:HALO], in_=ps[1][:, 0:HALO], func=AF.Sigmoid),
                                             W=[PSK[1], ("sgt", 0)])
                                        halo(hpt[:, 3, SEG + HALO:SEGP], ps[0], 0)
                                    else:
                                        P.op(pe, vg(HT - HALO, HALO, ps[0], ps[1]), R=[wak] + [MK(kc, 1) for kc in range(KC)],
                                             W=[PSK[0], PSK[1]])
                                        P.op(act, lambda h: h.activation(out=sgt[0][:, 0:HALO], in_=ps[1][:, 0:HALO], func=AF.Sigmoid),
                                             W=[PSK[1], ("sgt", 0)])
                                        halo(hpt[:, 0, 0:HALO], ps[0], 0)
                                    P.op(pool, lambda h: h.tensor_tensor(
                                        out=DG[:], in0=ident_b[:].unsqueeze(1).to_broadcast([128, CONV_K, 128]),
                                        in1=convw[:, j, :].unsqueeze(2).to_broadcast([128, CONV_K, 128]), op=ALU.mult),
                                        R=["ident_b", "convw"], W=["DG"])
                                    for b2 in range(2):
                                        def cv(h):
                                            ins = None
                                            for k in range(CONV_K):
                                                ins = h.matmul(ps[2][:].rearrange("p (s t) -> p s t", t=SEG), lhsT=DG[:, k, :],
                                                               rhs=hpt[:, 2 * b2:2 * b2 + 2, k:k + SEG],
                                                               start=(k == 0), stop=(k == CONV_K - 1))
                                            return ins
                                        P.op(pe, cv, R=["DG", hpk], W=[PSK[2]])
                                        P.op(act, lambda h: h.activation(out=co32[:], in_=ps[2][:], func=AF.Identity,
                                                                         bias=pvec[:, 0, j:j + 1], scale=1.0),
                                             R=["pvec"], W=[PSK[2], "co32"])
                                        P.op(pool, lambda h: h.tensor_copy(out=actb[:, j, b2 * BLK:(b2 + 1) * BLK], in_=co32[:]),
                                             R=["co32"], W=[("acta", j, b2)])
                                        P.op(dve, lambda h: h.tensor_tensor(out=sq32[:], in0=co32[:], in1=co32[:], op=ALU.mult),
                                             R=["co32"], W=["sq32"])
                                        P.op(pe, lambda h: h.matmul(ps[3 + b2][:], lhsT=ones_f[:], rhs=co32[:], start=(j == 0),
                                                                    stop=(j == KC - 1)), R=["ones_f", "co32"], W=[PSK[3 + b2]])
                                        P.op(pe, lambda h: h.matmul(ps[5 + b2][:], lhsT=ones_f[:], rhs=sq32[:], start=(j == 0),
                                                                    stop=(j == KC - 1)), R=["ones_f", "sq32"], W=[PSK[5 + b2]])
                                ws_done(wau)
                            for b2 in range(2):
                                P.op(act, lambda h: h.activation(out=cmean[b2][:], in_=ps[3 + b2][:], func=AF.Identity, scale=1.0 / D),
                                     W=[PSK[3 + b2], ("cmean", b2)])
                                P.op(dve, lambda h: h.tensor_tensor(out=sq32[:], in0=cmean[b2][:], in1=cmean[b2][:], op=ALU.mult),
                                     R=[("cmean", b2)], W=["sq32"])
                                P.op(dve, lambda h: h.scalar_tensor_tensor(out=crstd[b2][:], in0=ps[5 + b2][:], scalar=1.0 / D,
                                                                           in1=sq32[:], op0=ALU.mult, op1=ALU.subtract),
                                     R=["sq32"], W=[PSK[5 + b2], ("crstd", b2)])
                                P.op(act, lambda h: h.activation(out=crstd[b2][:], in_=crstd[b2][:], func=AF.Ln, bias=epsc[:, 0:1]),
                                     R=[("crstd", b2), "epsc"], W=[("crstd", b2)])
                                P.op(act, lambda h: h.activation(out=crstd[b2][:], in_=crstd[b2][:], func=AF.Exp, scale=-0.5),
                                     R=[("crstd", b2)], W=[("crstd", b2)])
                            for b2 in range(2):
                                for j in range(KC):
                                    asl = actb[:, j, b2 * BLK:(b2 + 1) * BLK]
                                    nb_, nk_ = (co32, "co32") if j % 2 == 0 else (sq32, "sq32")
                                    P.op(dve, lambda h: h.tensor_tensor(out=nb_[:], in0=asl, in1=cmean[b2][:], op=ALU.subtract),
                                         R=[("acta", j, b2), ("cmean", b2)], W=[nk_])
                                    P.op(dve, lambda h: h.tensor_tensor(out=nb_[:], in0=nb_[:], in1=crstd[b2][:], op=ALU.mult),
                                         R=[nk_, ("crstd", b2)], W=[nk_])
                                    P.op(act, lambda h: h.activation(out=asl, in_=nb_[:], func=AF.Silu,
                                                                     scale=pvec[:, 1, j:j + 1], bias=pvec[:, 2, j:j + 1]),
                                         R=[nk_, "pvec"], W=[("acta", j, b2)])
                        if debug is not None and debug[0] == "aact" and l == debug[2] and hf == debug[3]:
                            dump([actb[:, kc, :] for kc in range(KC)], [AK_all for kc in range(KC)])
                            break
                        yproj(hf, 0, lambda kc, b2: actb[:, kc, b2 * BLK:(b2 + 1) * BLK],
                              lambda kc, b2: [("acta", kc, b2)], False)
                        if debug is not None and debug[0] == "z" and l == debug[2] and hf == debug[3]:
                            dump([z[:, n, :] for n in range(KC)], [[("z", n, 0), ("z", n, 1)] for n in range(KC)])
                            break

                        alias_sync(ZBK_all, AK_all)
                        for kc in range(KC):
                            if kc % 2 == 0:
                                P.op(act, lambda h: h.activation(out=actb[:, kc, :], in_=z[:, kc, :], func=AF.Copy),
                                     R=[("z", kc, 0), ("z", kc, 1)], W=[("zb", kc, 0), ("zb", kc, 1)])
                            else:
                                P.op(dve, lambda h: h.tensor_copy(out=actb[:, kc, :], in_=z[:, kc, :]),
                                     R=[("z", kc, 0), ("z", kc, 1)], W=[("zb", kc, 0), ("zb", kc, 1)])
                        for u in range(4):
                            wo_, wok, wou = ws_next((KC, 256))
                            for nn in range(2):
                                n = 2 * u + nn
                                for b2 in range(2):
                                    blk = 2 * hf + b2
                                    pp = ycnt[0] % 2
                                    ycnt[0] += 1
                                    pm = ps[pp]

                                    def fo(h):
                                        ins = None
                                        for kc in range(KC):
                                            ins = h.matmul(pm[:], lhsT=wo_[:, kc, nn * 128:(nn + 1) * 128],
                                                           rhs=actb[:, kc, b2 * BLK:(b2 + 1) * BLK],
                                                           start=(kc == 0), stop=(kc == KC - 1))
                                        return ins
                                    P.op(pe, fo, R=[wok] + [("zb", kc, b2) for kc in range(KC)], W=[PSK[pp]])
                                    P.op(dve, lambda h: h.tensor_scalar(out=tmp2[:], in0=m[:, n, blk * BLK:(blk + 1) * BLK],
                                                                        scalar1=sh1(n), scalar2=ainv1(n),
                                                                        op0=ALU.subtract, op1=ALU.mult),
                                         R=[MK(n, blk)] + MODK, W=["tmp2"])
                                    P.op(dve, lambda h: h.scalar_tensor_tensor(out=tmp2[:], in0=rview(n, hf)[:, b2 * BLK:(b2 + 1) * BLK],
                                                                               scalar=ALPHA, in1=tmp2[:], op0=ALU.mult, op1=ALU.add),
                                         R=[("X", n, hf, 0), "tmp2"], W=["tmp2"])
                                    P.op(dve, lambda h: h.scalar_tensor_tensor(out=z[:, n, b2 * BLK:(b2 + 1) * BLK], in0=pm[:],
                                                                               scalar=g1(n), in1=tmp2[:], op0=ALU.mult, op1=ALU.add),
                                         R=["tmp2"] + MODK, W=[PSK[pp], ("z", n, b2)])
                            ws_done(wou)
                        with PScope(P) as s_ln:
                            lnsc = [sb(f"lnsc{l}{hf}_{i}", [128, BLK], F32, s_ln) for i in range(6)]
                            ln_feature_major(lambda n, b2: z[:, n, b2 * BLK:(b2 + 1) * BLK], lambda n, b2: [("z", n, b2)],
                                             lambda n, b2: X[:, n, hf, b2 * BLK:(b2 + 1) * BLK], lambda n, b2: XK(n, hf),
                                             lambda n: pvec[:, 4, n:n + 1], lambda n: pvec[:, 5, n:n + 1], lnsc)
                    if dbg_stop[0]:
                        break
                if debug is not None and debug[0] == "x1" and l == debug[2]:
                    dump([X[:, kc, hf, :] for kc in range(KC) for hf in range(2)], [XK(kc, hf) for kc in range(KC) for hf in range(2)])
                    break

            with PScope(P) as s_f:
                m2 = sb(f"m2{l}", [128, KC, HT], BF16, s_f)
                hid = sb(f"hid{l}", [128, FKC, HT], BF16, s_f)
                tb = sb(f"tb{l}", [128, KC, HT], F32, s_f)
                sgf = [sb(f"sgf{l}_{i}", [128, BLK], F32, s_f) for i in range(2)]
                lnsc2 = [sb(f"lnsc2{l}_{i}", [128, BLK], F32, s_f) for i in range(6)]
                fcnt = 0
                for hf in range(2):
                    for kc in range(KC):
                        P.op(act, lambda h: h.activation(out=m2[:, kc, :], in_=X[:, kc, hf, :], func=AF.Identity,
                                                         scale=opsc2(kc), bias=sh2(kc)),
                             R=XK(kc, hf) + MODK, W=[("m2", kc, 0), ("m2", kc, 1)])
                    for u in range(22):
                        wf, wfk, wfu = ws_next((KC, 256))
                        for jj in range(1):
                            j = u
                            for b2 in range(2):
                                pp = fcnt % 2
                                fcnt += 1
                                pa, pb_ = ps[2 * pp], ps[2 * pp + 1]

                                def fh(h):
                                    ins = None
                                    for kc in range(KC):
                                        h.matmul(pa[:], lhsT=wf[:, kc, jj * 128:(jj + 1) * 128], rhs=m2[:, kc, b2 * BLK:(b2 + 1) * BLK],
                                                 start=(kc == 0), stop=(kc == KC - 1))
                                    for kc in range(KC):
                                        ins = h.matmul(pb_[:], lhsT=wf[:, kc, 128 + jj * 128:128 + (jj + 1) * 128],
                                                       rhs=m2[:, kc, b2 * BLK:(b2 + 1) * BLK], start=(kc == 0), stop=(kc == KC - 1))
                                    return ins
                                P.op(pe, fh, R=[wfk] + [("m2", kc, b2) for kc in range(KC)], W=[PSK[2 * pp], PSK[2 * pp + 1]])
                                P.op(act, lambda h: h.activation(out=sgf[pp][:], in_=pa[:], func=AF.Silu), W=[PSK[2 * pp], ("sgf", pp)])
                                P.op(dve, lambda h: h.tensor_tensor(out=hid[:, j, b2 * BLK:(b2 + 1) * BLK], in0=pb_[:], in1=sgf[pp][:],
                                                                    op=ALU.mult), R=[("sgf", pp)], W=[PSK[2 * pp + 1], ("hid", j, b2)])
                        ws_done(wfu)
                    for n in range(KC):
                        w2a, w2ak, w2au = ws_next((11, 128))
                        w2b, w2bk, w2bu = ws_next((11, 128))
                        for b2 in range(2):
                            pp = fcnt % 2
                            fcnt += 1
                            pf = ps[pp]

                            def fo2(h):
                                ins = None
                                for kc in range(FKC):
                                    ins = h.matmul(pf[:], lhsT=(w2a if kc < 11 else w2b)[:, kc % 11, :], rhs=hid[:, kc, b2 * BLK:(b2 + 1) * BLK],
                                                   start=(kc == 0), stop=(kc == FKC - 1))
                                return ins
                            P.op(pe, fo2, R=[w2ak, w2bk] + [("hid", kc, b2) for kc in range(FKC)], W=[PSK[pp]])
                            P.op(act, lambda h: h.activation(out=sgf[pp][:], in_=pf[:], func=AF.Identity, scale=g2(n)),
                                 R=MODK, W=[PSK[pp], ("sgf", pp)])
                            P.op(dve, lambda h: h.scalar_tensor_tensor(out=tb[:, n, b2 * BLK:(b2 + 1) * BLK],
                                                                       in0=X[:, n, hf, b2 * BLK:(b2 + 1) * BLK], scalar=ALPHA,
                                                                       in1=sgf[pp][:], op0=ALU.mult, op1=ALU.add),
                                 R=XK(n, hf) + [("sgf", pp)], W=[("tb", n, b2)])
                        ws_done(w2au)
                        ws_done(w2bu)
                    ln_feature_major(lambda n, b2: tb[:, n, b2 * BLK:(b2 + 1) * BLK], lambda n, b2: [("tb", n, b2)],
                                     lambda n, b2: X[:, n, hf, b2 * BLK:(b2 + 1) * BLK], lambda n, b2: XK(n, hf),
                                     lambda n: pvec[:, 6, n:n + 1], lambda n: pvec[:, 7, n:n + 1], lnsc2)
            if debug is not None and debug[0] == "x2" and l == debug[2]:
                dump([X[:, kc, hf, :] for kc in range(KC) for hf in range(2)], [XK(kc, hf) for kc in range(KC) for hf in range(2)])
                break

        if not dbg_stop[0] and (debug is None or debug[0] == "full"):
            with PScope(P) as s_o:
                yt = [sb(f"yt{i}", [128, D], F32, s_o) for i in range(2)]
                ysem = [P.new_sem(f"y{i}") for i in range(2)]
                for i in range(T // 128):
                    s = i % 2
                    hf, tc = divmod(i, 8)
                    for half in range(2):
                        pb = ps[half]

                        def tr(h):
                            ins = None
                            for q in range(4):
                                kc = half * 4 + q
                                ins = h.transpose(pb[:, q * 128:(q + 1) * 128], X[:, kc, hf, tc * 128:(tc + 1) * 128], ident_f[:])
                            return ins
                        P.op(pe, tr, R=[k for q in range(4) for k in XK(half * 4 + q, hf)] + ["ident_f"], W=[PSK[half]])
                        P.op(act if half == 0 else dve,
                             (lambda h: h.activation(out=yt[s][:, 0:512], in_=pb[:], func=AF.Copy)) if half == 0 else
                             (lambda h: h.tensor_copy(out=yt[s][:, 512:1024], in_=pb[:])),
                             W=[PSK[half], ("yt", s, half)])
                    P.dma(sp, ysem[s], y_d[i * 128:(i + 1) * 128, :], yt[s][:], R=[("yt", s, 0), ("yt", s, 1)], is_output=True)

        if debug is not None and debug[0] == "input":
            dsem = P.new_sem("dbg")
            for kc in range(KC):
                for hf in range(2):
                    P.dma(sp, dsem, dbg_d[:, (kc * 2 + hf) * HT:(kc * 2 + hf + 1) * HT], X[:, kc, hf, :],
                          R=XK(kc, hf), is_output=True)

        for tok in P.out_toks:
            P._wait(sp, tok)
    return nc


def _prep_shared(inp):
    f = lambda k: np.asarray(inp[k], np.float32)
    w_in = f("w_in")
    sh = {}
    sh["lnin"] = np.ascontiguousarray(np.stack([_fm(f("ln_in_g")), _fm(f("ln_in_b"))], 1))
    sh["bada"] = np.ascontiguousarray(f("b_ada").reshape(L, 48, 128).transpose(0, 2, 1))
    pv = []
    for l in range(L):
        rows = [f("conv_b")[l], f("conv_ln_g")[l], f("conv_ln_b")[l], f("hgrn_norm_g")[l], f("ln1_g")[l],
                f("ln1_b")[l], f("ln2_g")[l], f("ln2_b")[l], f("sgu_ln_b")[l]]
        pv.append(np.stack([_fm(r) for r in rows], 1))
    sh["pvec"] = np.ascontiguousarray(np.stack(pv, 0))
    sh["lbraw"] = np.ascontiguousarray(np.stack([_fm(f("hgrn_lb")[0]), _fm(f("hgrn_lb")[1])], 1))
    sh["convw"] = np.ascontiguousarray(f("conv_w").transpose(0, 2, 1).reshape(L, KC, 128, CONV_K).transpose(0, 2, 1, 3))
    sh["sgug"] = f("sgu_ln_g")
    sh["sgub"] = f("sgu_ln_b")
    sh["sgubias"] = np.ascontiguousarray(f("sgu_b").reshape(L, NH * 128))
    sh["sguw"] = np.ascontiguousarray(f("sgu_w").transpose(0, 3, 1, 2))
    wada, winBq, winBz, winBi, winA, winCv, winCu, winG, wxo, wo, wffi, wffo = ([] for _ in range(12))
    for l in range(L):
        wi = w_in[l]
        wada.append(_wunits(f("w_ada")[l], [[(u * 256, 256)] for u in range(24)]))
        winBq.append(_wunits(wi, [[(OFF_B + 0 * D + h * 128, 128)] for h in range(NH)]))
        winBz.append(_wunits(wi, [[(OFF_B + g * D + h * 128, 128) for g in (1, 2)] for h in range(NH)]))
        winBi.append(_wunits(wi, [[(OFF_B + g * D + h * 128, 128) for g in (3, 4)] for h in range(NH)]))
        winA.append(_wunits(wi, [[(OFF_A + u * 128, 128), (OFF_A + D + u * 128, 128)] for u in range(8)]))
        winCu.append(_wunits(wi, [[(OFF_C + u * 256, 256)] for u in range(4)]))
        winCv.append(_wunits(wi, [[(OFF_C + D + u * 256, 256)] for u in range(4)]))
        winG.append(np.stack([_wunits(wi, [[(OFF_G + g * D + u * 256, 256)] for u in range(4)]) for g in range(3)], 0))
        wxo.append(np.stack([_wunits(f(k)[l], [[(u * 256, 256)] for u in range(4)])
                             for k in ("w_a_out", "w_b_out", "w_c_out")], 0))
        wo.append(_wunits(f("w_o")[l], [[(u * 256, 256)] for u in range(4)]))
        wf = f("w_ffn_in")[l]
        wffi.append(_wunits(wf, [[(u * 128, 128), (FF + u * 128, 128)] for u in range(22)]))
        w2 = f("w_ffn_out")[l]
        wffo.append(np.ascontiguousarray(
            np.stack([w2[:, n * 128:(n + 1) * 128].reshape(2, 11, 128, 128).transpose(0, 2, 1, 3) for n in range(8)], 0)))
    sh["wada"] = np.stack(wada, 0)
    sh["winBq"] = np.stack(winBq, 0)
    sh["winBz"] = np.stack(winBz, 0)
    sh["winBi"] = np.stack(winBi, 0)
    sh["winA"] = np.stack(winA, 0)
    sh["winCv"] = np.stack(winCv, 0)
    sh["winCu"] = np.stack(winCu, 0)
    sh["winG"] = np.stack(winG, 0)
    sh["wxo"] = np.stack(wxo, 0)
    sh["wo"] = np.stack(wo, 0)
    sh["wffi"] = np.stack(wffi, 0)
    sh["wffo"] = np.stack(wffo, 0)
    return {k: np.ascontiguousarray(v, dtype=np.float32) for k, v in sh.items()}


def _prep_cores(inp):
    xp = np.asarray(inp["x_prompt"], np.float32)
    xs_ = np.asarray(inp["x_sample"], np.float32)
    st = np.asarray(inp["state_hgrn"], np.float32)
    c = np.asarray(inp["c"], np.float32)
    cc = np.asarray(inp["c_ctx"], np.float32)
    cores = []
    for i in range(8):
        d = {}
        fl = np.zeros((128, 4), np.float32)
        if i < 4:
            d["x"] = np.ascontiguousarray(xs_[i])
            fl[:, 0] = 1.0
            fl[:, 1] = 1.0
            d["cond"] = _fm(c[i])
            d["s0"] = np.ascontiguousarray(st[i])
        else:
            blk = xp[4 * (i - 4):4 * (i - 4) + 4].reshape(4 * SEG, D)
            d["x"] = np.ascontiguousarray(np.concatenate([blk, blk], 0))
            d["cond"] = _fm(cc)
            d["s0"] = np.zeros((L, 2, NH, 128, 128), np.float32)
        d["flags"] = fl
        cores.append(d)
    return cores


def kernel(**inputs):
    shared = _prep_shared(inputs)
    cores = _prep_cores(inputs)
    nc = build_program()
    in_maps = [dict(shared, **c) for c in cores]
    res = run_bass_kernel_spmd(nc, in_maps, core_ids=list(range(8)))
    y_prompt = np.zeros((16, SEG, D), np.float32)
    y_sample = np.zeros((4, T, D), np.float32)
    new_state = np.zeros((16, L, 2, NH, 128, 128), np.float32)
    for i, r in enumerate(res.results):
        if i < 4:
            y_sample[i] = r["y"]
        else:
            y_prompt[4 * (i - 4):4 * (i - 4) + 4] = r["y"][:4 * SEG].reshape(4, SEG, D)
            stt = r["st"]
            new_state[4 * (i - 4):4 * (i - 4) + 4] = stt[:, :, 0:4].transpose(2, 0, 1, 3, 4, 5)
    return (y_prompt, y_sample, new_state)
```

```python
import math
from contextlib import ExitStack

import numpy as np
import concourse.bass as bass
import concourse.mybir as mybir
from concourse.bass_utils import run_bass_kernel_spmd

F32 = mybir.dt.float32
BF16 = mybir.dt.bfloat16
I32 = mybir.dt.int32
AF = mybir.ActivationFunctionType
ALU = mybir.AluOpType

D = 1024
KC = 8
T = 2048
HT = 1024
BLK = 512
NBLK = 4
NSEG = 8
SEG = 256
CH = 32
NCH = 64
FF = 2816
FKC = 22
L = 2
NH = 8
CONV_K = 31
HALO = 15
SEGP = SEG + 2 * HALO
ALPHA = (2 * L) ** 0.25
EPS = 1e-5
SLOT = 2048
PE_KEEPWARM = 2
NSLOT = 5

DEBUG = None


class Sem:
    def __init__(self, h):
        self.h = h
        self.count = 0


class Eng:
    def __init__(self, name, h, sem):
        self.name = name
        self.h = h
        self.sem = sem
        self.waited = {}


class Prog:
    def __init__(self, nc, es):
        self.nc = nc
        self.es = es
        self.lastw = {}
        self.readers = {}
        self.nsem = 0
        self.dma_sems = []

        def mk(name, h):
            return Eng(name, h, self.new_sem(name))

        self.pe = mk("pe", nc.tensor)
        self.dve = mk("dve", nc.vector)
        self.act = mk("act", nc.scalar)
        self.pool = mk("pool", nc.gpsimd)
        self.sp = mk("sp", nc.sync)
        self.out_toks = []

    def new_sem(self, name):
        self.nsem += 1
        sm = Sem(self.es.enter_context(self.nc.semaphore(f"s{self.nsem}_{name}")))
        self.dma_sems.append(sm)
        return sm

    def _wait(self, eng, tok):
        sem, val = tok
        if eng.waited.get(sem, 0) >= val:
            return
        eng.h.wait_ge(sem.h, val)
        eng.waited[sem] = val

    def _deps(self, R, W):
        deps = []
        for k in R:
            w = self.lastw.get(k)
            if w is not None:
                deps.append((w, "raw"))
        for k in W:
            w = self.lastw.get(k)
            if w is not None:
                deps.append((w, "waw"))
            for sem, val in self.readers.get(k, {}).items():
                deps.append(((sem, val), "war"))
        return deps

    def _record(self, tok, R, W):
        for k in W:
            self.lastw[k] = tok
            self.readers[k] = {}
        for k in R:
            d = self.readers.setdefault(k, {})
            if d.get(tok[0], 0) < tok[1]:
                d[tok[0]] = tok[1]

    def op(self, eng, fn, R=(), W=()):
        for tok, kind in self._deps(R, W):
            if tok[0] is eng.sem and kind != "raw":
                continue
            self._wait(eng, tok)
        ins = fn(eng.h)
        eng.sem.count += 1
        ins.then_inc(eng.sem.h, 1)
        tok = (eng.sem, eng.sem.count)
        self._record(tok, R, W)
        return tok

    def barrier(self):
        engs = [self.pe, self.dve, self.act, self.pool, self.sp]
        sems = [e.sem for e in engs] + list(self.dma_sems)
        for e in engs:
            for sm in sems:
                if sm is not e.sem and sm.count > 0:
                    self._wait(e, (sm, sm.count))

    def dma(self, q, sem, out, in_, R=(), W=(), is_output=False):
        for tok, kind in self._deps(R, W):
            self._wait(q, tok)
        if sem.count:
            self._wait(q, (sem, sem.count))
        ins = q.h.dma_start(out=out, in_=in_)
        sem.count += 16
        ins.then_inc(sem.h, 16)
        tok = (sem, sem.count)
        self._record(tok, R, W)
        if is_output:
            self.out_toks.append(tok)
        return tok


class PScope(ExitStack):
    def __init__(self, prog):
        super().__init__()
        self.prog = prog

    def __exit__(self, *exc):
        r = super().__exit__(*exc)
        if exc[0] is None:
            self.prog.barrier()
        return r


def _fm(v):
    return np.ascontiguousarray(np.asarray(v, np.float32).reshape(KC, 128).T)


def _wunits(w, col_groups):
    out = []
    for grp in col_groups:
        cols = np.concatenate([w[:, a:a + s] for a, s in grp], axis=1)
        out.append(cols.reshape(KC, 128, cols.shape[1]).transpose(1, 0, 2))
    return np.ascontiguousarray(np.stack(out, 0), dtype=np.float32)


A_IN = 2 * D
B_IN = 5 * D
C_IN = 2 * D
OFF_A = 0
OFF_B = A_IN
OFF_C = A_IN + B_IN
OFF_G = A_IN + B_IN + C_IN


def build_program(debug=None):
    nc = bass.Bass("TRN2", target_bir_lowering=False)

    def din(name, shape):
        return nc.dram_tensor(name, list(shape), F32, kind="ExternalInput").ap()

    x_d = din("x", [T, D])
    flags_d = din("flags", [128, 4])
    cond_d = din("cond", [128, KC])
    s0_d = din("s0", [L, 2, NH, 128, 128])
    lnin_d = din("lnin", [128, 2, KC])
    bada_d = din("bada", [L, 128, 48])
    pvec_d = din("pvec", [L, 128, 9, KC])
    lbraw_d = din("lbraw", [128, 2, KC])
    convw_d = din("convw", [L, 128, KC, CONV_K])
    sgug_d = din("sgug", [L, D])
    sgub_d = din("sgub", [L, D])
    sgubias_d = din("sgubias", [L, NH * 128])
    sguw_d = din("sguw", [L, 128, NH, 128])
    wada_d = din("wada", [L, 24, 128, KC, 256])
    winBq_d = din("winBq", [L, NH, 128, KC, 128])
    winBz_d = din("winBz", [L, NH, 128, KC, 256])
    winBi_d = din("winBi", [L, NH, 128, KC, 256])
    winA_d = din("winA", [L, 8, 128, KC, 256])
    winCv_d = din("winCv", [L, 4, 128, KC, 256])
    winCu_d = din("winCu", [L, 4, 128, KC, 256])
    winG_d = din("winG", [L, 3, 4, 128, KC, 256])
    wxo_d = din("wxo", [L, 3, 4, 128, KC, 256])
    wo_d = din("wo", [L, 4, 128, KC, 256])
    wffi_d = din("wffi", [L, 22, 128, KC, 256])
    wffo_d = din("wffo", [L, 8, 2, 128, 11, 128])

    y_d = nc.dram_tensor("y", [T, D], F32, kind="ExternalOutput").ap()
    st_d = nc.dram_tensor("st", [L, 2, NSEG, NH, 128, 128], F32, kind="ExternalOutput").ap()
    dbg_d = None
    if debug is not None:
        dbg_d = nc.dram_tensor("dbg", [128, debug[1]], F32, kind="ExternalOutput").ap()

    with ExitStack() as es:
        P = Prog(nc, es)
        block = es.enter_context(nc.Block())
        pe, dve, act, pool, sp = P.pe, P.dve, P.act, P.pool, P.sp

        def sb(name, shape, dt, stack=es):
            return stack.enter_context(nc.sbuf_tensor("sb_" + name, list(shape), dt))

        ps = [es.enter_context(nc.psum_tensor(f"ps{i}", [128, 512], F32)) for i in range(7)]
        psT = es.enter_context(nc.psum_tensor("psT", [128, 1024], BF16))
        PSK = [("ps", i) for i in range(7)]
        PSTK = ("ps", 7)

        X = sb("X", [128, KC, 2, HT], F32)
        wslots = [sb(f"wslot{i}", [128, SLOT], BF16) for i in range(NSLOT)]
        wsems = [P.new_sem(f"w{i}") for i in range(NSLOT)]
        ident_b = sb("ident_b", [128, 128], BF16)
        ident_f = sb("ident_f", [128, 128], F32)
        ones_f = sb("ones_f", [128, 128], F32)
        epsc = sb("epsc", [128, 1], F32)
        flags = sb("flags", [128, 4], F32)
        lnin = sb("lnin", [128, 2, KC], F32)
        scb = sb("scb", [128, KC], BF16)
        modSa = sb("modS", [128, L, 48], F32)
        modDa = sb("modD", [128, L, 4, KC], F32)
        badaa = sb("badaa", [128, L, 48], F32)
        pvec = sb("pvec", [128, 9, KC], F32)
        lbv = sb("lbv", [128, 3, KC], F32)
        lbraw = sb("lbraw", [128, 2, KC], F32)
        misc_sems = [P.new_sem(f"misc{i}") for i in range(8)]
        misc_i = [0]

        def msem():
            misc_i[0] += 1
            return misc_sems[misc_i[0] % len(misc_sems)]

        carry = flags[:, 0:1]
        peflag = flags[:, 1:2]

        def XK(kc, hf):
            return [("X", kc, hf, 0), ("X", kc, hf, 1)]

        class WS:
            units = []
            issued = 0
            cur = 0
            done = set()

        def ws_pump():
            while WS.issued < len(WS.units) and WS.issued <= WS.cur + NSLOT - 1:
                u = WS.issued
                if u >= NSLOT and (u - NSLOT) not in WS.done:
                    break
                ap_in, shp = WS.units[u]
                slot = u % NSLOT
                n = shp[0] * shp[1]
                ov = wslots[slot][:, 0:n].rearrange("p (k c) -> p k c", k=shp[0])
                P.dma(pool, wsems[slot], ov, ap_in, W=[("w", slot)])
                WS.issued += 1

        def ws_next(shp):
            u = WS.cur
            exp_ap, exp_shp = WS.units[u]
            assert exp_shp == shp, (u, exp_shp, shp)
            ws_pump()
            assert WS.issued > u, ("weight unit not issuable", u)
            WS.cur += 1
            ws_pump()
            slot = u % NSLOT
            n = shp[0] * shp[1]
            return wslots[slot][:, 0:n].rearrange("p (k c) -> p k c", k=shp[0]), ("w", slot), u

        def ws_done(u):
            WS.done.add(u)
            ws_pump()

        def unit_plan():
            for l in range(L):
                for u in range(24):
                    yield wada_d[l, u], (KC, 256)
            for l in range(L):
                for h in range(NH):
                    yield winBq_d[l, h], (KC, 128)
                    yield winBz_d[l, h], (KC, 256)
                    yield winBi_d[l, h], (KC, 256)
                for hf in range(2):
                    for br in (1, 2, 0):
                        if br == 2:
                            for u in range(4):
                                yield winCv_d[l, u], (KC, 256)
                            for u in range(4):
                                yield winCu_d[l, u], (KC, 256)
                        if br == 0:
                            for u in range(8):
                                yield winA_d[l, u], (KC, 256)
                        for u in range(4):
                            yield wxo_d[l, br, u], (KC, 256)
                            yield winG_d[l, br, u], (KC, 256)
                    for u in range(4):
                        yield wo_d[l, u], (KC, 256)
                for hf in range(2):
                    for u in range(22):
                        yield wffi_d[l, u], (KC, 256)
                    for n in range(8):
                        for q in range(2):
                            yield wffo_d[l, n, q], (11, 128)

        WS.units = list(unit_plan())
        if debug is not None and debug[0] in ("input",):
            WS.units = []

        def c_ident(h):
            h.memset(ident_f[:], 1.0)
            return h.affine_select(out=ident_f[:], in_=ident_f[:], pattern=[[-1, 128]],
                                   compare_op=ALU.is_equal, fill=0.0, base=0, channel_multiplier=1)
        P.op(pool, c_ident, W=["ident_f"])
        P.op(pool, lambda h: h.tensor_copy(out=ident_b[:], in_=ident_f[:]), R=["ident_f"], W=["ident_b"])
        P.op(pool, lambda h: h.memset(ones_f[:], 1.0), W=["ones_f"])
        P.op(pool, lambda h: h.memset(epsc[:], EPS), W=["epsc"])
        P.dma(sp, msem(), flags[:], flags_d, W=["flags"])
        P.dma(sp, msem(), lnin[:], lnin_d, W=["lnin"])
        P.dma(sp, msem(), lbraw[:], lbraw_d, W=["lbraw"])

        with PScope(P) as s_in:
            W2 = sb("W2", [128, 512], F32, s_in)
            PH = sb("PH", [128, 512], F32, s_in)
            pe_c = sb("pe_c", [128, 512], F32, s_in)
            jint = sb("jint", [128, 256], I32, s_in)
            pint = sb("pint", [128, 16], I32, s_in)
            pcol = sb("pcol", [128, 4], F32, s_in)
            rr = sb("rr", [128, 16], F32, s_in)
            argt = [sb(f"argt{i}", [128, 512], F32, s_in) for i in range(2)]
            xt = [sb(f"xt{i}", [128, D], F32, s_in) for i in range(2)]
            xsem = [P.new_sem(f"x{i}") for i in range(2)]
            stt = sb("stt", [128, 2, 8], F32, s_in)
            cond = sb("cond", [128, KC], F32, s_in)
            ctmp = sb("ctmp", [128, KC], F32, s_in)

            P.dma(sp, msem(), cond[:], cond_d, W=["cond"])
            P.op(pool, lambda h: h.iota(jint[:], pattern=[[1, 256]], base=0, channel_multiplier=0), W=["jint"])
            P.op(pool, lambda h: h.tensor_copy(out=W2[:, 0:256], in_=jint[:]), R=["jint"], W=["W2a"])
            P.op(act, lambda h: h.activation(out=W2[:, 0:256], in_=W2[:, 0:256], func=AF.Exp,
                                             scale=-math.log(10000.0) / 256.0), R=["W2a"], W=["W2a"])
            P.op(pool, lambda h: h.tensor_copy(out=W2[:, 256:512], in_=W2[:, 0:256]), R=["W2a"], W=["W2b"])
            P.op(pool, lambda h: h.memset(PH[:, 0:256], 0.0), W=["PHa"])
            P.op(pool, lambda h: h.memset(PH[:, 256:512], math.pi / 2), W=["PHb"])
            WK = ["W2a", "W2b", "PHa", "PHb"]
            P.op(pool, lambda h: h.iota(pint[:, 0:1], pattern=[[0, 1]], base=0, channel_multiplier=1), W=["pint0"])
            P.op(pool, lambda h: h.tensor_copy(out=pcol[:, 0:1], in_=pint[:, 0:1]), R=["pint0"], W=["pcol0"])
            P.op(pool, lambda h: h.tensor_single_scalar(out=pcol[:, 1:2], in_=pcol[:, 0:1], scalar=64.0, op=ALU.is_ge),
                 R=["pcol0"], W=["pcol1"])
            P.op(dve, lambda h: h.scalar_tensor_tensor(out=pcol[:, 2:3], in0=pcol[:, 1:2], scalar=-64.0,
                                                        in1=pcol[:, 0:1], op0=ALU.mult, op1=ALU.add),
                 R=["pcol0", "pcol1"], W=["pcol2"])
            P.op(pool, lambda h: h.iota(pint[:, 0:16], pattern=[[2, 16]], base=0, channel_multiplier=0),
                 R=["pcol0"], W=["pint0"])
            P.op(pool, lambda h: h.tensor_copy(out=rr[:], in_=pint[:, 0:16]), R=["pint0"], W=["rr"])
            P.op(pool, lambda h: h.tensor_scalar(out=rr[:], in0=rr[:], scalar1=pcol[:, 1:2], scalar2=None, op0=ALU.add),
                 R=["rr", "pcol1"], W=["rr"])

            nint = sb("nint", [128, 512], I32, s_in)
            nflt = sb("nflt", [128, 512], F32, s_in)

            def pe_table(dst, dkey, scal_ap, scal_keys, dst_flagmul, np_=128):
                nf, ni = nflt[0:np_, :], nint[0:np_, :]
                P.op(dve, lambda h: h.scalar_tensor_tensor(out=dst, in0=W2[0:np_, :], scalar=scal_ap, in1=PH[0:np_, :],
                                                           op0=ALU.mult, op1=ALU.add),
                     R=WK + scal_keys, W=[dkey])
                P.op(dve, lambda h: h.tensor_scalar(out=nf, in0=dst, scalar1=1.0 / (2 * math.pi), scalar2=0.5,
                                                    op0=ALU.mult, op1=ALU.add), R=[dkey], W=["nflt"])
                P.op(dve, lambda h: h.tensor_copy(out=ni, in_=nf), R=["nflt"], W=["nint"])
                P.op(dve, lambda h: h.tensor_copy(out=nf, in_=ni), R=["nint"], W=["nflt"])
                P.op(dve, lambda h: h.scalar_tensor_tensor(out=dst, in0=nf, scalar=-2 * math.pi, in1=dst,
                                                           op0=ALU.mult, op1=ALU.add), R=["nflt", dkey], W=[dkey])
                P.op(dve, lambda h: h.tensor_single_scalar(out=nf, in_=dst, scalar=-math.pi, op=ALU.is_lt),
                     R=[dkey], W=["nflt"])
                P.op(dve, lambda h: h.scalar_tensor_tensor(out=dst, in0=nf, scalar=2 * math.pi, in1=dst,
                                                           op0=ALU.mult, op1=ALU.add), R=["nflt", dkey], W=[dkey])
                P.op(dve, lambda h: h.tensor_scalar(out=dst, in0=dst, scalar1=math.pi, scalar2=-math.pi,
                                                    op0=ALU.min, op1=ALU.max), R=[dkey], W=[dkey])
                P.op(act, lambda h: h.activation(out=dst, in_=dst, func=AF.Sin), R=[dkey], W=[dkey])
                if dst_flagmul:
                    P.op(dve, lambda h: h.tensor_scalar(out=dst, in0=dst, scalar1=peflag, scalar2=None, op0=ALU.mult),
                         R=[dkey, "flags"], W=[dkey])

            negpi = sb("negpi", [128, 1], F32, s_in)
            P.op(pool, lambda h: h.memset(negpi[:], -math.pi), W=["negpi"])
            pe_table(pe_c[:], "pe_c", pcol[:, 2:3], ["pcol2"], True)
            Rtab = sb("Rtab", [32, 512], F32, s_in)
            Sel = sb("Sel", [32, 16, 128], F32, s_in)
            pe_table(Rtab[:], "Rtab", pcol[0:32, 0:1], ["pcol0"], False, np_=32)

            def c_sel(h):
                h.memset(Sel[:], 1.0)
                h.affine_select(out=Sel[:, :, 0:64], in_=Sel[:, :, 0:64], pattern=[[-2, 16], [0, 64]],
                                compare_op=ALU.is_equal, fill=0.0, base=0, channel_multiplier=1)
                return h.affine_select(out=Sel[:, :, 64:128], in_=Sel[:, :, 64:128], pattern=[[-2, 16], [0, 64]],
                                       compare_op=ALU.is_equal, fill=0.0, base=-1, channel_multiplier=1)
            P.op(pool, c_sel, W=["Sel"])

            P.op(act, lambda h: h.activation(out=ctmp[:], in_=cond[:], func=AF.Exp, scale=-1.0), R=["cond"], W=["ctmp"])
            P.op(dve, lambda h: h.tensor_scalar(out=ctmp[:], in0=ctmp[:], scalar1=1.0, scalar2=None, op0=ALU.add),
                 R=["ctmp"], W=["ctmp"])
            P.op(dve, lambda h: h.reciprocal(out=ctmp[:], in_=ctmp[:]), R=["ctmp"], W=["ctmp"])
            P.op(dve, lambda h: h.tensor_tensor(out=scb[:], in0=cond[:], in1=ctmp[:], op=ALU.mult),
                 R=["cond", "ctmp"], W=["scb"])

            for l_ in range(L):
                P.dma(sp, msem(), badaa[:, l_, :], bada_d[l_], W=["bada"])
            ada_list = [(l_, u) for l_ in range(L) for u in range(24)]

            def emit_ada(l_, u):
                wv, wk, wu = ws_next((KC, 256))

                def ada(h):
                    ins = None
                    for jj in range(2):
                        j = 48 * l_ + 2 * u + jj
                        for kc in range(KC):
                            ins = h.matmul(ps[6][:, j:j + 1], lhsT=wv[:, kc, jj * 128:(jj + 1) * 128],
                                           rhs=scb[:, kc:kc + 1], start=(kc == 0), stop=(kc == KC - 1))
                    return ins
                P.op(pe, ada, R=[wk, "scb"], W=[PSK[6]])
                ws_done(wu)

            for i in range(T // 128):
                for l_, u in ada_list[3 * i:3 * i + 3]:
                    if WS.units:
                        emit_ada(l_, u)
                s = i % 2
                hf, tc = divmod(i, 8)
                xk = ("xt", s)
                ak = ("argt", s)
                P.dma(sp, xsem[s], xt[s][:], x_d[i * 128:(i + 1) * 128, :], W=[xk])
                pbk = 2 + i % 2
                P.op(pe, lambda h: h.matmul(ps[pbk][:], lhsT=Sel[:, i, :], rhs=Rtab[:], start=True, stop=True),
                     R=["Sel", "Rtab"], W=[PSK[pbk]])
                P.op(dve, lambda h: h.scalar_tensor_tensor(out=xt[s][:, 0:512], in0=ps[pbk][:], scalar=peflag,
                                                           in1=xt[s][:, 0:512], op0=ALU.mult, op1=ALU.add),
                     R=[xk, "flags"], W=[PSK[pbk], xk])
                P.op(pool, lambda h: h.tensor_tensor(out=xt[s][:, 512:1024], in0=xt[s][:, 512:1024], in1=pe_c[:],
                                                     op=ALU.add), R=[xk, "pe_c"], W=[xk])

                def stats(h):
                    h.bn_stats(out=stt[:, 0, 0:6], in_=xt[s][:, 0:512])
                    return h.bn_stats(out=stt[:, 1, 0:6], in_=xt[s][:, 512:1024])
                P.op(dve, stats, R=[xk], W=["stt"])
                P.op(dve, lambda h: h.bn_aggr(out=stt[:, 0, 6:8], in_=stt[:, 0:2, 0:6]), R=["stt"], W=["mv"])
                P.op(act, lambda h: h.activation(out=stt[:, 1, 6:7], in_=stt[:, 0, 7:8], func=AF.Ln, bias=epsc[:, 0:1]),
                     R=["mv", "epsc"], W=["rstd"])
                P.op(act, lambda h: h.activation(out=stt[:, 1, 6:7], in_=stt[:, 1, 6:7], func=AF.Exp, scale=-0.5),
                     R=["rstd"], W=["rstd"])
                P.op(dve, lambda h: h.tensor_scalar(out=stt[:, 1, 7:8], in0=stt[:, 0, 6:7], scalar1=stt[:, 1, 6:7],
                                                    scalar2=-1.0, op0=ALU.mult, op1=ALU.mult),
                     R=["mv", "rstd"], W=["nmr"])
                P.op(dve, lambda h: h.tensor_scalar(out=xt[s][:], in0=xt[s][:], scalar1=stt[:, 1, 6:7],
                                                    scalar2=stt[:, 1, 7:8], op0=ALU.mult, op1=ALU.add),
                     R=[xk, "rstd", "nmr"], W=[xk])
                for half in range(2):
                    pb = ps[half]

                    def tr(h, half=half, pb=pb):
                        ins = None
                        for q in range(4):
                            kc = half * 4 + q
                            ins = h.transpose(pb[:, q * 128:(q + 1) * 128], xt[s][:, kc * 128:(kc + 1) * 128], ident_f[:])
                        return ins
                    P.op(pe, tr, R=[xk, "ident_f"], W=[PSK[half]])
                    for q in range(4):
                        kc = half * 4 + q
                        P.op(act, lambda h, kc=kc, q=q, pb=pb: h.activation(
                            out=X[:, kc, hf, tc * 128:(tc + 1) * 128], in_=pb[:, q * 128:(q + 1) * 128],
                            func=AF.Identity, scale=lnin[:, 0, kc:kc + 1], bias=lnin[:, 1, kc:kc + 1]),
                            R=["lnin"], W=[PSK[half]] + XK(kc, hf))

        if WS.units:
            P.op(dve, lambda h: h.tensor_tensor(out=modSa[:].rearrange("p l c -> p (l c)"), in0=ps[6][:, 0:96],
                                                in1=badaa[:].rearrange("p l c -> p (l c)"), op=ALU.add),
                 R=["bada"], W=[PSK[6], "mod"])
            for l_ in range(L):
                P.op(dve, lambda h: h.tensor_scalar(out=modDa[:, l_, 0, :], in0=modSa[:, l_, 8:16], scalar1=1.0, scalar2=None,
                                                    op0=ALU.add), R=["mod"], W=[("modD0", l_)])
                P.op(dve, lambda h: h.reciprocal(out=modDa[:, l_, 1, :], in_=modDa[:, l_, 0, :]), R=[("modD0", l_)], W=[("modD1", l_)])
                P.op(dve, lambda h: h.tensor_scalar(out=modDa[:, l_, 2, :], in0=modDa[:, l_, 1, :], scalar1=ALPHA, scalar2=None,
                                                    op0=ALU.mult), R=[("modD1", l_)], W=[("modD2", l_)])
                P.op(dve, lambda h: h.tensor_scalar(out=modDa[:, l_, 3, :], in0=modSa[:, l_, 32:40], scalar1=1.0, scalar2=None,
                                                    op0=ALU.add), R=["mod"], W=[("modD3", l_)])

        def alias_sync(new_keys, old_keys):
            merged = {}
            for k in old_keys:
                w = P.lastw.get(k)
                if w is not None:
                    merged[w[0]] = max(merged.get(w[0], 0), w[1])
                for sem, val in P.readers.get(k, {}).items():
                    merged[sem] = max(merged.get(sem, 0), val)
            for k in new_keys:
                P.lastw.pop(k, None)
                P.readers[k] = dict(merged)

        def rview(kc, hf):
            return X[:, kc, hf, 0:512].bitcast(BF16)

        def bview(kc, hf):
            return X[:, kc, hf, 512:1024].bitcast(BF16)

        def MK(kc, blk):
            return ("m", kc, blk)

        dbg_stop = [debug is not None and debug[0] == "input"]

        def dump(ap_list, keys):
            dsem = P.new_sem("dbg")
            off = 0
            for ap, k in zip(ap_list, keys):
                n = ap.shape[-1] if len(ap.shape) == 2 else int(np.prod(ap.shape[1:]))
                P.dma(sp if ap.dtype == F32 else pool, dsem, dbg_d[0:ap.shape[0], off:off + n], ap, R=k, is_output=True)
                off += n
            dbg_stop[0] = True

        def ln_feature_major(t_ap, t_keys, out_ap, out_keys, g_ap, b_ap, scratch, nb2=2, silu=False, gb_keys=("pvec",)):
            sqa, mean, rstd, t1a, sqb, t1b = scratch
            sqs = [sqa, sqb]
            t1s = [t1a, t1b]
            for b2 in range(nb2):
                def s1(h):
                    ins = None
                    for n in range(KC):
                        ins = h.matmul(ps[5][:], lhsT=ones_f[:], rhs=t_ap(n, b2), start=(n == 0), stop=(n == KC - 1))
                    return ins
                P.op(pe, s1, R=["ones_f"] + [k for n in range(KC) for k in t_keys(n, b2)], W=[PSK[5]])
                for n in range(KC):
                    sq = sqs[n % 2]
                    P.op(act, lambda h: h.activation(out=sq[:], in_=t_ap(n, b2), func=AF.Square),
                         R=t_keys(n, b2), W=[("lnsq", n % 2)])
                    P.op(pe, lambda h: h.matmul(ps[6][:], lhsT=ones_f[:], rhs=sq[:], start=(n == 0), stop=(n == KC - 1)),
                         R=["ones_f", ("lnsq", n % 2)], W=[PSK[6]])
                P.op(act, lambda h: h.activation(out=mean[:], in_=ps[5][:], func=AF.Identity, scale=1.0 / D),
                     W=[PSK[5], "lnmean"])
                P.op(dve, lambda h: h.tensor_tensor(out=t1a[:], in0=mean[:], in1=mean[:], op=ALU.mult),
                     R=["lnmean"], W=[("lnt1", 0)])
                P.op(dve, lambda h: h.scalar_tensor_tensor(out=rstd[:], in0=ps[6][:], scalar=1.0 / D, in1=t1a[:],
                                                           op0=ALU.mult, op1=ALU.subtract),
                     R=[("lnt1", 0)], W=[PSK[6], "lnrstd"])
                P.op(act, lambda h: h.activation(out=rstd[:], in_=rstd[:], func=AF.Ln, bias=epsc[:, 0:1]),
                     R=["lnrstd", "epsc"], W=["lnrstd"])
                P.op(act, lambda h: h.activation(out=rstd[:], in_=rstd[:], func=AF.Exp, scale=-0.5),
                     R=["lnrstd"], W=["lnrstd"])
                for n in range(KC):
                    t1 = t1s[n % 2]
                    tk = ("lnt1", n % 2)
                    P.op(dve, lambda h: h.tensor_tensor(out=t1[:], in0=t_ap(n, b2), in1=mean[:], op=ALU.subtract),
                         R=t_keys(n, b2) + ["lnmean"], W=[tk])
                    P.op(dve, lambda h: h.tensor_tensor(out=t1[:], in0=t1[:], in1=rstd[:], op=ALU.mult),
                         R=[tk, "lnrstd"], W=[tk])
                    P.op(act, lambda h: h.activation(out=out_ap(n, b2), in_=t1[:], func=(AF.Silu if silu else AF.Identity),
                                                     scale=g_ap(n), bias=b_ap(n)),
                         R=[tk] + list(gb_keys), W=out_keys(n, b2))

        maskF = sb("maskF", [32, 32], F32)
        maskB = sb("maskB", [32, 32], F32)

        def c_mask(h, mt, pat, cm):
            h.memset(mt[:], 1.0)
            return h.affine_select(out=mt[:], in_=mt[:], pattern=[[pat, 32]], compare_op=ALU.is_ge, fill=0.0,
                                   base=0, channel_multiplier=cm)
        P.op(pool, lambda h: c_mask(h, maskF, 1, -1), W=["maskF"])
        P.op(pool, lambda h: c_mask(h, maskB, -1, 1), W=["maskB"])


        NRB = 3
        ssems = [P.new_sem(f"S{i}") for i in range(NRB)]
        onec = sb("onec", [128, 1], F32)
        P.op(pool, lambda h: h.memset(onec[:], 1.0), W=["onec"])
        TriF = sb("TriF", [128, 128], F32)
        TriB = sb("TriB", [128, 128], F32)

        def c_tri(h, mt, pat, cm):
            h.memset(mt[:], 1.0)
            ins = h.affine_select(out=mt[:], in_=mt[:], pattern=[[pat, 128]], compare_op=ALU.is_ge, fill=0.0,
                                  base=0, channel_multiplier=cm)
            for c in range(4):
                blk_ = mt[:, c * CH:(c + 1) * CH]
                h.affine_select(out=blk_, in_=blk_, pattern=[[0, CH]], compare_op=ALU.is_ge, fill=0.0,
                                base=-c * CH, channel_multiplier=1)
                ins = h.affine_select(out=blk_, in_=blk_, pattern=[[0, CH]], compare_op=ALU.is_ge, fill=0.0,
                                      base=c * CH + CH - 1, channel_multiplier=-1)
            return ins
        P.op(pool, lambda h: c_tri(h, TriF, 1, -1), W=["Tri"])
        P.op(pool, lambda h: c_tri(h, TriB, -1, 1), W=["Tri"])

        for l in range(L):
            if dbg_stop[0]:
                break
            last_layer = (l == L - 1)
            with PScope(P) as s_mix:
                m = sb(f"m{l}", [128, KC, T], BF16, s_mix)
                P.dma(sp, msem(), pvec[:], pvec_d[l], W=["pvec"])

                modS = modSa[:, l, :]
                modD = modDa[:, l, :, :]
                MODK = ["mod", ("modD0", l), ("modD1", l), ("modD2", l), ("modD3", l)]
                sh1 = lambda kc: modSa[:, l, kc:kc + 1]
                g1 = lambda kc: modSa[:, l, 16 + kc:17 + kc]
                sh2 = lambda kc: modSa[:, l, 24 + kc:25 + kc]
                g2 = lambda kc: modSa[:, l, 40 + kc:41 + kc]
                opsc1 = lambda kc: modDa[:, l, 0, kc:kc + 1]
                inv1 = lambda kc: modDa[:, l, 1, kc:kc + 1]
                ainv1 = lambda kc: modDa[:, l, 2, kc:kc + 1]
                opsc2 = lambda kc: modDa[:, l, 3, kc:kc + 1]

                if l == 0:
                    P.op(pool, lambda h: h.memset(lbv[:, 0, :], 0.0), W=["lbv"])
                    P.op(pool, lambda h: h.memset(lbv[:, 1, :], 1.0), W=["lbv"])
                else:
                    P.op(dve, lambda h: h.tensor_tensor(out=lbv[:, 2, :], in0=lbraw[:, 0, :], in1=lbraw[:, 1, :], op=ALU.subtract),
                         R=["lbraw"], W=["lbv2"])
                    P.op(act, lambda h: h.activation(out=lbv[:, 2, :], in_=lbv[:, 2, :], func=AF.Exp), R=["lbv2"], W=["lbv2"])
                    P.op(dve, lambda h: h.tensor_scalar(out=lbv[:, 0, :], in0=lbv[:, 2, :], scalar1=1.0, scalar2=None, op0=ALU.add),
                         R=["lbv2"], W=["lbv"])
                    P.op(dve, lambda h: h.reciprocal(out=lbv[:, 0, :], in_=lbv[:, 0, :]), R=["lbv"], W=["lbv"])
                    P.op(dve, lambda h: h.tensor_tensor(out=lbv[:, 1, :], in0=lbv[:, 2, :], in1=lbv[:, 0, :], op=ALU.mult),
                         R=["lbv", "lbv2"], W=["lbv"])
                lb_ap = lambda hh: lbv[:, 0, hh:hh + 1]
                oml_ap = lambda hh: lbv[:, 1, hh:hh + 1]

                with PScope(P) as s_r:
                    rt32 = [sb(f"rt32_{l}_{i}", [128, HT], F32, s_r) for i in range(2)]
                    rtb = [sb(f"rtb_{l}_{i}", [128, HT], BF16, s_r) for i in range(2)]
                    cnt = 0
                    for kc in range(KC):
                        for hf in range(2):
                            sl = cnt % 2
                            cnt += 1
                            mk = [MK(kc, 2 * hf), MK(kc, 2 * hf + 1)]
                            P.op(act, lambda h: h.activation(out=m[:, kc, hf * HT:(hf + 1) * HT], in_=X[:, kc, hf, :],
                                                             func=AF.Identity, scale=opsc1(kc), bias=sh1(kc)),
                                 R=XK(kc, hf) + MODK, W=mk)
                            P.op(dve, lambda h: h.tensor_scalar(out=rt32[sl][:], in0=m[:, kc, hf * HT:(hf + 1) * HT],
                                                                scalar1=sh1(kc), scalar2=inv1(kc),
                                                                op0=ALU.subtract, op1=ALU.mult),
                                 R=mk + MODK, W=[("rt32", sl)])
                            P.op(dve, lambda h: h.tensor_tensor(out=rtb[sl][:], in0=X[:, kc, hf, :], in1=rt32[sl][:],
                                                                op=ALU.subtract),
                                 R=XK(kc, hf) + [("rt32", sl)], W=[("rtb", sl)])
                            P.op(act, lambda h: h.activation(out=rview(kc, hf), in_=rtb[sl][:], func=AF.Copy),
                                 R=[("rtb", sl)], W=[("X", kc, hf, 0)])

                if debug is not None and debug[0] == "mod" and l == debug[2]:
                    dump([modSa[:, l, :]] + [m[:, kc, :] for kc in range(KC)] + [rview(kc, hf) for kc in range(KC) for hf in range(2)],
                         [MODK] + [[MK(kc, b) for b in range(4)] for kc in range(KC)] +
                         [[("X", kc, hf, 0)] for kc in range(KC) for hf in range(2)])
                    break

                with PScope(P) as s_b:
                    NST = 2
                    ITs = [sb(f"IT{l}_{i}", [128, T], BF16, s_b) for i in range(NST)]
                    SGs = [sb(f"SG{l}_{i}", [128, T], BF16, s_b) for i in range(NST)]
                    OFs = [sb(f"OF{l}_{i}", [128, T], F32, s_b) for i in range(NST)]
                    DEC = sb(f"DEC{l}", [128, 3, 16], F32, s_b)
                    E1 = sb(f"E{l}", [128, BLK], F32, s_b)
                    NS1 = sb(f"NS{l}", [128, BLK], F32, s_b)
                    LFT1 = sb(f"LFT{l}", [128, BLK], F32, s_b)
                    EB1 = sb(f"EB{l}", [128, BLK], F32, s_b)
                    ENB1 = sb(f"ENB{l}", [128, BLK], F32, s_b)
                    GT1 = sb(f"GT{l}", [128, BLK], F32, s_b)
                    FE = sb(f"FE{l}", [128, BLK], F32, s_b)
                    FN = sb(f"FN{l}", [128, BLK], F32, s_b)
                    Qd = [sb(f"Qd{l}_{i}", [128, BLK], BF16, s_b) for i in range(3)]
                    Ktd = [sb(f"Ktd{l}_{i}", [128, BLK], BF16, s_b) for i in range(3)]
                    Khd = [sb(f"Khd{l}_{i}", [128, BLK], BF16, s_b) for i in range(2)]
                    KhTokd = [sb(f"KhTok{l}_{i}", [32, 16, 128], BF16, s_b) for i in range(2)]
                    VTokd = [sb(f"VTok{l}_{i}", [32, 16, 128], BF16, s_b) for i in range(2)]
                    MS = sb(f"MS{l}", [32, 16, 32], BF16, s_b)
                    RS = sb(f"RS{l}", [128, NRB, 5, 128], F32, s_b)
                    SB_ = sb(f"SBr{l}", [128, NRB, 5, 128], BF16, s_b)
                    psT3 = psT[:].rearrange("p (c v) -> p c v", v=128)
                    ps3b = ps[3][:].bitcast(BF16).rearrange("p (c v) -> p c v", v=128)

                    class SR:
                        rb = 0
                        ent = (0, 0)

                    def rb_next():
                        SR.rb = (SR.rb + 1) % NRB
                        return SR.rb

                    def RK(loc):
                        return ("RS", loc[0], loc[1])

                    def BK(loc):
                        return ("SBr", loc[0], loc[1])

                    def sweep_start(dirn, hh):
                        rb = rb_next()
                        P.dma(sp, ssems[rb], RS[:, rb, 0, :], s0_d[l, dirn, hh], W=[RK((rb, 0))])
                        P.op(act, lambda h: h.activation(out=SB_[:, rb, 0, :], in_=RS[:, rb, 0, :], func=AF.Copy),
                             R=[RK((rb, 0))], W=[BK((rb, 0))])
                        SR.ent = (rb, 0)

                    def emit_state(dirn, seg, hh):
                        e = SR.ent
                        P.dma(sp, ssems[e[0]], st_d[l, dirn, seg, hh], RS[:, e[0], e[1], :], R=[RK(e)], is_output=True)

                    tok_cnt = [0]

                    def run(*gens):
                        gens = [g for g in gens if g is not None]
                        while gens:
                            for g in list(gens):
                                try:
                                    next(g)
                                except StopIteration:
                                    gens.remove(g)
                            if PE_KEEPWARM:
                                def warm(h):
                                    ins = None
                                    for _ in range(PE_KEEPWARM):
                                        ins = h.matmul(ps[1][:], lhsT=m[:, 0, 0:128], rhs=m[:, 1, 0:512], start=True, stop=True)
                                    return ins
                                P.op(pe, warm, R=[MK(0, 0), MK(1, 0)], W=[PSK[1]])

                    def to_tok(src_fn, src_keys, dst, dst_key):
                        for rd in range(2):
                            pv, pk = (psT3, PSTK) if rd == 0 else (ps3b, PSK[3])

                            def tr(h):
                                ins = None
                                for q in range(8):
                                    ins = h.transpose(pv[0:32, q, :], src_fn(rd * 8 + q), ident_b[:])
                                return ins
                            P.op(pe, tr, R=src_keys + ["ident_b"], W=[pk])
                            yield
                            tok_cnt[0] += 1
                            if tok_cnt[0] % 2:
                                P.op(act, lambda h: h.activation(out=dst[:, rd * 8:(rd + 1) * 8, :], in_=pv[0:32, :, :],
                                                                 func=AF.Copy), W=[pk, dst_key + (rd,)])
                            else:
                                P.op(dve, lambda h: h.tensor_copy(out=dst[:, rd * 8:(rd + 1) * 8, :], in_=pv[0:32, :, :]),
                                     W=[pk, dst_key + (rd,)])
                            yield

                    def scan_block(dirn, hh, bi, k, kt_fn, q_fn, kq_keys, maskX, mask_key):
                        order = list(range(16)) if dirn == 0 else list(range(15, -1, -1))
                        tp_ = k % 2
                        KhTok, VTok = KhTokd[tp_], VTokd[tp_]
                        TOKK = [("KhTok", tp_, 0), ("KhTok", tp_, 1), ("VTok", tp_, 0), ("VTok", tp_, 1)]
                        obank, okey = ps[4], PSK[4]
                        dk = ("DEC", k % 3)

                        def scores(h):
                            ins = None
                            for cc in range(16):
                                ins = h.matmul(ps[3][0:32, cc * 32:(cc + 1) * 32], lhsT=kt_fn(cc), rhs=q_fn(cc),
                                               start=True, stop=True)
                            return ins
                        P.op(pe, scores, R=kq_keys, W=[PSK[3]])
                        yield
                        P.op(dve, lambda h: h.tensor_tensor(
                            out=MS[:], in0=ps[3][0:32, :].rearrange("p (c t) -> p c t", t=32),
                            in1=maskX[:].unsqueeze(1).to_broadcast([32, 16, 32]), op=ALU.mult),
                            R=[mask_key], W=[PSK[3], "MS"])
                        yield

                        def intra(h):
                            ins = None
                            for cc in range(16):
                                ins = h.matmul(obank[:, cc * 32:(cc + 1) * 32], lhsT=VTok[:, cc, :], rhs=MS[:, cc, :],
                                               start=(cc == 0), stop=False, skip_group_check=True)
                            return ins
                        P.op(pe, intra, R=["MS"] + TOKK[2:], W=[okey])
                        yield

                        def emit_dsm(rd):
                            chunks = order[4 * rd:4 * rd + 4]
                            bank = ps[5 + rd % 2]

                            def dsm(h):
                                ins = None
                                for j, cc in enumerate(chunks):
                                    ins = h.matmul(bank[:, j * 128:(j + 1) * 128], lhsT=KhTok[:, cc, :], rhs=VTok[:, cc, :],
                                                   start=True, stop=True)
                                return ins
                            P.op(pe, dsm, R=TOKK, W=[PSK[5 + rd % 2]])
                        emit_dsm(0)
                        yield
                        for rd in range(4):
                            chunks = order[4 * rd:4 * rd + 4]
                            bank = ps[5 + rd % 2]
                            bkey = PSK[5 + rd % 2]
                            c0 = 16 * bi + chunks[0]
                            if dirn == 0 and c0 % 8 == 0 and c0 > 0:
                                seg_done = c0 // 8 - 1
                            elif dirn == 1 and c0 % 8 == 7 and c0 < NCH - 1:
                                seg_done = (c0 + 1) // 8
                            else:
                                seg_done = None
                            rb = rb_next()
                            if seg_done is not None:
                                emit_state(dirn, seg_done, hh)
                                e = SR.ent
                                P.op(dve, lambda h: h.tensor_scalar(out=RS[:, rb, 0, :], in0=RS[:, e[0], e[1], :],
                                                                    scalar1=carry, scalar2=None, op0=ALU.mult),
                                     R=[RK(e), "flags"], W=[RK((rb, 0))])
                                P.op(act, lambda h: h.activation(out=SB_[:, rb, 0, :], in_=RS[:, rb, 0, :], func=AF.Copy),
                                     R=[RK((rb, 0))], W=[BK((rb, 0))])
                                SR.ent = (rb, 0)
                                yield
                            ents = []
                            for j, cc in enumerate(chunks):
                                e = SR.ent
                                ents.append(e)
                                P.op(dve, lambda h: h.scalar_tensor_tensor(
                                    out=RS[:, rb, j + 1, :], in0=RS[:, e[0], e[1], :], scalar=DEC[:, k % 3, cc:cc + 1],
                                    in1=bank[:, j * 128:(j + 1) * 128], op0=ALU.mult, op1=ALU.add),
                                    R=[RK(e), dk], W=[bkey, RK((rb, j + 1))])
                                SR.ent = (rb, j + 1)
                            yield
                            if rd < 3:
                                emit_dsm(rd + 1)
                            P.op(act, lambda h: h.activation(out=SB_[:, rb, 1:5, :], in_=RS[:, rb, 1:5, :], func=AF.Copy),
                                 R=[RK((rb, s_)) for s_ in range(1, 5)], W=[BK((rb, s_)) for s_ in range(1, 5)])
                            yield

                            def inter(h, chunks=chunks, ents=ents):
                                ins = None
                                for j, cc in enumerate(chunks):
                                    e = ents[j]
                                    ins = h.matmul(obank[:, cc * 32:(cc + 1) * 32], lhsT=SB_[:, e[0], e[1], :], rhs=q_fn(cc),
                                                   start=False, stop=True, skip_group_check=True)
                                return ins
                            P.op(pe, inter, R=[BK(e) for e in ents] + kq_keys, W=[okey])
                            yield

                    items = []
                    for hh in range(NH):
                        for bi in range(NBLK):
                            items.append((hh, 0, bi))
                        for bi in range(NBLK - 1, -1, -1):
                            items.append((hh, 1, bi))
                    wts = {}

                    def head_weights(hh):
                        if hh not in wts:
                            wts[hh] = [ws_next((KC, 128)), ws_next((KC, 256)), ws_next((KC, 256))]
                        return wts[hh]

                    def G_stage(k):
                        if k - 3 >= 0:
                            yield from Fin_stage(k - 3)
                        hh, dirn, bi = items[k]
                        (wq_, wqk, wqu), (wz_, wzk, wzu), (wi_, wik, wiu) = head_weights(hh)
                        st_ = hh % NST
                        IT, SG = ITs[st_], SGs[st_]
                        p3, par = k % 3, k % 2
                        MB = [MK(kc, bi) for kc in range(KC)]
                        sl = slice(bi * BLK, (bi + 1) * BLK)

                        def proj(bank, wv, c0):
                            def f(h):
                                ins = None
                                for kc in range(KC):
                                    ins = h.matmul(bank[:], lhsT=wv[:, kc, c0:c0 + 128], rhs=m[:, kc, bi * BLK:(bi + 1) * BLK],
                                                   start=(kc == 0), stop=(kc == KC - 1))
                                return ins
                            return f
                        P.op(pe, proj(ps[0], wz_, 128 * dirn), R=[wzk] + MB, W=[PSK[0]])
                        yield
                        P.op(act, lambda h: h.activation(out=E1[:], in_=ps[0][:], func=AF.Exp), W=[PSK[0], "E1"])
                        P.op(pe, proj(ps[2], wq_, 0), R=[wqk] + MB, W=[PSK[2]])
                        yield
                        P.op(act, lambda h: h.activation(out=NS1[:], in_=E1[:], func=AF.Ln, bias=onec[:, 0:1]),
                             R=["E1", "onec"], W=["NS1"])
                        yield
                        if l == 0:
                            P.op(dve, lambda h: h.tensor_tensor(out=E1[:], in0=ps[0][:], in1=NS1[:], op=ALU.subtract),
                                 R=["NS1", "E1"], W=[PSK[0], "E1"])
                            P.op(act, lambda h: h.activation(out=NS1[:], in_=NS1[:], func=AF.Exp, scale=-1.0),
                                 R=["NS1"], W=["NS1"])
                            yield
                        else:
                            P.op(act, lambda h: h.activation(out=NS1[:], in_=NS1[:], func=AF.Exp, scale=-1.0),
                                 R=["NS1"], W=["NS1"])
                            yield
                            P.op(dve, lambda h: h.tensor_tensor(out=E1[:], in0=E1[:], in1=NS1[:], op=ALU.mult),
                                 R=["E1", "NS1"], W=["E1"])
                            yield
                            P.op(act, lambda h: h.activation(out=NS1[:], in_=NS1[:], func=AF.Identity, scale=oml_ap(hh)),
                                 R=["NS1", "lbv"], W=["NS1"])
                            P.op(act, lambda h: h.activation(out=E1[:], in_=E1[:], func=AF.Ln, scale=oml_ap(hh),
                                                             bias=lb_ap(hh)), R=["E1", "lbv"], W=["E1"])
                            yield
                        def trl(h):
                            ins = None
                            for tl in range(4):
                                ins = h.transpose(ps[0][:, tl * 128:(tl + 1) * 128], E1[:, tl * 128:(tl + 1) * 128], ident_f[:])
                            return ins
                        P.op(pe, trl, R=["E1", "ident_f"], W=[PSK[0]])
                        yield
                        P.op(act, lambda h: h.activation(out=LFT1[:], in_=ps[0][:], func=AF.Copy), W=[PSK[0], "LFT1"])
                        yield

                        def cum(h):
                            ins = None
                            tri = TriF if dirn == 0 else TriB
                            for tl in range(4):
                                ins = h.matmul(ps[0][:, tl * 128:(tl + 1) * 128], lhsT=LFT1[:, tl * 128:(tl + 1) * 128],
                                               rhs=tri[:], start=True, stop=True)
                            return ins
                        P.op(pe, cum, R=["LFT1", "Tri"], W=[PSK[0]])
                        yield
                        P.op(act, lambda h: h.activation(out=EB1[:], in_=ps[0][:], func=AF.Exp), W=[PSK[0], "EB1"])
                        P.op(act, lambda h: h.activation(out=ENB1[:], in_=ps[0][:], func=AF.Exp, scale=-1.0), W=[PSK[0], "ENB1"])
                        yield
                        eb3 = EB1[:].rearrange("p (c j) -> p c j", j=CH)
                        ecol = (CH - 1) if dirn == 0 else 0
                        P.op(act, lambda h: h.activation(out=DEC[:, p3, :], in_=eb3[:, :, ecol], func=AF.Copy),
                             R=["EB1"], W=[("DEC", p3)])
                        P.op(dve, lambda h: h.tensor_tensor(out=Qd[p3][:], in0=ps[2][:], in1=EB1[:], op=ALU.mult),
                             R=["EB1"], W=[PSK[2], ("Qd", p3)])
                        yield
                        P.op(dve, lambda h: h.tensor_tensor(out=ENB1[:], in0=ENB1[:], in1=NS1[:], op=ALU.mult),
                             R=["ENB1", "NS1"], W=["ENB1"])
                        if dirn == 0:
                            P.op(pe, proj(ps[2], wi_, 0), R=[wik] + MB, W=[PSK[2]])
                        yield
                        P.op(act, lambda h: h.activation(out=Ktd[p3][:], in_=ENB1[:], func=AF.Copy), R=["ENB1"], W=[("Ktd", p3)])
                        yield
                        P.op(dve, lambda h: h.tensor_tensor(
                            out=Khd[par][:].rearrange("p (c j) -> p c j", j=CH),
                            in0=ENB1[:].rearrange("p (c j) -> p c j", j=CH),
                            in1=eb3[:, :, ecol:ecol + 1].to_broadcast([128, 16, CH]), op=ALU.mult),
                            R=["ENB1", "EB1"], W=[("Khd", par)])
                        yield
                        if dirn == 0:
                            P.op(act, lambda h: h.activation(out=IT[:, sl], in_=ps[2][:], func=AF.Copy), W=[PSK[2], ("IT", st_, bi)])
                            P.op(pe, proj(ps[2], wi_, 128), R=[wik] + MB, W=[PSK[2]])
                            yield
                            P.op(act, lambda h: h.activation(out=GT1[:], in_=ps[2][:], func=AF.Exp, scale=-1.0), W=[PSK[2], "GT1"])
                            yield
                            P.op(act, lambda h: h.activation(out=GT1[:], in_=GT1[:], func=AF.Ln, bias=onec[:, 0:1]),
                                 R=["GT1", "onec"], W=["GT1"])
                            yield
                            P.op(act, lambda h: h.activation(out=GT1[:], in_=GT1[:], func=AF.Exp, scale=-1.0), R=["GT1"], W=["GT1"])
                            yield
                            P.op(dve, lambda h: h.tensor_tensor(out=SG[:, sl], in0=ps[2][:], in1=GT1[:], op=ALU.mult),
                                 R=["GT1"], W=[PSK[2], ("SG", st_, bi)])
                            yield
                        if dirn == 0 and bi == NBLK - 1:
                            ws_done(wiu)
                        if dirn == 1 and bi == 0:
                            ws_done(wqu)
                            ws_done(wzu)

                    def T_stage(k):
                        hh, dirn, bi = items[k]
                        st_ = hh % NST
                        IT = ITs[st_]
                        par = k % 2
                        yield from to_tok(lambda cc: Khd[par][:, cc * CH:(cc + 1) * CH], [("Khd", par)], KhTokd[par], ("KhTok", par))
                        yield from to_tok(lambda cc: IT[:, bi * BLK + cc * CH: bi * BLK + (cc + 1) * CH], [("IT", st_, bi)],
                                          VTokd[par], ("VTok", par))

                    def S_stage(k):
                        hh, dirn, bi = items[k]
                        st_ = hh % NST
                        p3 = k % 3
                        sl = slice(bi * BLK, (bi + 1) * BLK)
                        if (dirn == 0 and bi == 0) or (dirn == 1 and bi == NBLK - 1):
                            sweep_start(dirn, hh)
                        yield from scan_block(dirn, hh, bi, k, lambda cc: Ktd[p3][:, cc * CH:(cc + 1) * CH],
                                              lambda cc: Qd[p3][:, cc * CH:(cc + 1) * CH],
                                              [("Ktd", p3), ("Qd", p3)], maskF if dirn == 0 else maskB,
                                              "maskF" if dirn == 0 else "maskB")
                        if dirn == 0:
                            P.op(act, lambda h: h.activation(out=OFs[st_][:, sl], in_=ps[4][:], func=AF.Copy),
                                 W=[PSK[4], ("OF", st_, bi)])
                            yield
                            if bi == NBLK - 1:
                                emit_state(0, NSEG - 1, hh)
                        else:
                            P.op(dve, lambda h: h.tensor_tensor(out=FE[:], in0=ps[4][:], in1=OFs[st_][:, sl], op=ALU.add),
                                 R=[("OF", st_, bi)], W=[PSK[4], "FE"])
                            yield
                            if bi == 0:
                                emit_state(1, 0, hh)

                    def Fin_stage(k):
                        hh, dirn, bi = items[k]
                        if dirn == 0:
                            return
                        yield
                        st_ = hh % NST
                        sl = slice(bi * BLK, (bi + 1) * BLK)
                        hf, b2 = divmod(bi, 2)
                        P.op(act, lambda h: h.activation(out=FN[:], in_=FE[:], func=AF.Square), R=["FE"], W=["FN"])
                        yield
                        P.op(pe, lambda h: h.matmul(ps[2][:], lhsT=ones_f[:], rhs=FN[:], start=True, stop=True),
                             R=["ones_f", "FN"], W=[PSK[2]])
                        yield
                        P.op(act, lambda h: h.activation(out=FN[:], in_=ps[2][:], func=AF.Ln, scale=1.0 / 128, bias=epsc[:, 0:1]),
                             R=["epsc"], W=[PSK[2], "FN"])
                        yield
                        P.op(act, lambda h: h.activation(out=FN[:], in_=FN[:], func=AF.Exp, scale=-0.5), R=["FN"], W=["FN"])
                        yield
                        P.op(dve, lambda h: h.tensor_tensor(out=FE[:], in0=FE[:], in1=FN[:], op=ALU.mult), R=["FE", "FN"], W=["FE"])
                        yield
                        P.op(dve, lambda h: h.scalar_tensor_tensor(
                            out=bview(hh, hf)[:, b2 * BLK:(b2 + 1) * BLK], in0=FE[:], scalar=pvec[:, 3, hh:hh + 1],
                            in1=SGs[st_][:, sl], op0=ALU.mult, op1=ALU.mult),
                            R=["FE", "pvec", ("SG", st_, bi)], W=[("X", hh, hf, 1)])
                        yield

                    NI = len(items)
                    stage = lambda fn, k: fn(k) if 0 <= k < NI else None
                    for step in range(-2, NI + 1):
                        run(stage(S_stage, step), stage(T_stage, step + 1), stage(G_stage, step + 2),
                            stage(Fin_stage, step - 1) if step + 2 >= NI else None)

                if debug is not None and debug[0] == "hgrn" and l == debug[2]:
                    dump([bview(kc, hf) for kc in range(KC) for hf in range(2)],
                         [[("X", kc, hf, 1)] for kc in range(KC) for hf in range(2)])
                    break
                with PScope(P) as s_h:
                    z = sb(f"z{l}", [128, KC, HT], F32, s_h)
                    actb = sb(f"actb{l}", [128, KC, HT], BF16, s_h)
                    sgt = [sb(f"sgt{l}_{i}", [128, BLK], F32, s_h) for i in range(2)]
                    tmp2 = sb(f"tmp2{l}", [128, BLK], F32, s_h)
                    actflat = actb[:].rearrange("p k t -> p (k t)")
                    VN = lambda tile: actflat[:, tile * 1024:(tile + 1) * 1024]
                    cact = actflat.rearrange("p (tile g c) -> p g tile c", tile=8, g=8)
                    CK_all = [("act", tile, g) for tile in range(8) for g in range(8)]
                    AK_all = [("acta", j, b2) for j in range(KC) for b2 in range(2)]
                    ZBK_all = [("zb", kc, b2) for kc in range(KC) for b2 in range(2)]
                    ycnt = [0]

                    def yproj(hf, br, rhs_fn, rhs_keys, first):
                        for u in range(4):
                            wx, wxk, wxu = ws_next((KC, 256))
                            wg, wgk, wgu = ws_next((KC, 256))
                            for nn in range(2):
                                n = 2 * u + nn
                                for b2 in range(2):
                                    blk = 2 * hf + b2
                                    pp = ycnt[0] % 2
                                    ycnt[0] += 1
                                    py, pg = ps[2 * pp], ps[2 * pp + 1]
                                    pyk, pgk = PSK[2 * pp], PSK[2 * pp + 1]

                                    def fy(h):
                                        ins = None
                                        for kc in range(KC):
                                            ins = h.matmul(py[:], lhsT=wx[:, kc, nn * 128:(nn + 1) * 128], rhs=rhs_fn(kc, b2),
                                                           start=(kc == 0), stop=(kc == KC - 1))
                                        return ins
                                    P.op(pe, fy, R=[wxk] + [k for kc in range(KC) for k in rhs_keys(kc, b2)], W=[pyk])

                                    def fg(h):
                                        ins = None
                                        for kc in range(KC):
                                            ins = h.matmul(pg[:], lhsT=wg[:, kc, nn * 128:(nn + 1) * 128],
                                                           rhs=m[:, kc, blk * BLK:(blk + 1) * BLK],
                                                           start=(kc == 0), stop=(kc == KC - 1))
                                        return ins
                                    P.op(pe, fg, R=[wgk] + [MK(kc, blk) for kc in range(KC)], W=[pgk])
                                    P.op(act, lambda h: h.activation(out=sgt[pp][:], in_=pg[:], func=AF.Sigmoid),
                                         W=[pgk, ("sgt", pp)])
                                    zsl = z[:, n, b2 * BLK:(b2 + 1) * BLK]
                                    if first:
                                        P.op(dve, lambda h: h.tensor_tensor(out=zsl, in0=py[:], in1=sgt[pp][:], op=ALU.mult),
                                             R=[("sgt", pp)], W=[pyk, ("z", n, b2)])
                                    else:
                                        P.op(dve, lambda h: h.tensor_tensor(out=tmp2[:], in0=py[:], in1=sgt[pp][:], op=ALU.mult),
                                             R=[("sgt", pp)], W=[pyk, "tmp2"])
                                        P.op(pool, lambda h: h.tensor_tensor(out=zsl, in0=zsl, in1=tmp2[:], op=ALU.add),
                                             R=["tmp2", ("z", n, b2)], W=[("z", n, b2)])
                            ws_done(wxu)
                            ws_done(wgu)

                    for hf in range(2):
                        h0 = hf * HT
                        yproj(hf, 1, lambda kc, b2: bview(kc, hf)[:, b2 * BLK:(b2 + 1) * BLK],
                              lambda kc, b2: [("X", kc, hf, 1)], True)
                        if debug is not None and debug[0] == "yb" and l == debug[2] and hf == debug[3]:
                            dump([z[:, n, :] for n in range(KC)], [[("z", n, 0), ("z", n, 1)] for n in range(KC)])
                            break

                        alias_sync(CK_all, ZBK_all + AK_all)
                        with PScope(P) as s_c:
                            gam_bc = sb(f"gam{l}{hf}", [128, D], F32, s_c)
                            BIAS = sb(f"BIAS{l}{hf}", [128, NH, 128], F32, s_c)
                            wTb = sb(f"wTb{l}{hf}", [128, NH, 128], BF16, s_c)
                            ones_b = sb(f"onesb{l}{hf}", [128, 128], BF16, s_c)
                            stc = sb(f"stc{l}{hf}", [128, 2, 8], F32, s_c)
                            ctmp2 = sb(f"ctmp2{l}{hf}", [128, BLK], F32, s_c)
                            s_bs = ExitStack()
                            bs_bc = sb(f"bsbc{l}{hf}", [128, NH, 128], F32, s_bs)
                            P.dma(sp, msem(), gam_bc[:], sgug_d[l:l + 1, :].to_broadcast([128, D]), W=["gam_bc"])
                            P.dma(sp, msem(), bs_bc[:].rearrange("p g t -> p (g t)"),
                                  sgubias_d[l:l + 1, :].to_broadcast([128, NH * 128]), W=["bs_bc"])
                            P.dma(pool, msem(), wTb[:], sguw_d[l], W=["wTb"])
                            P.op(pool, lambda h: h.memset(ones_b[:], 1.0), W=["ones_b"])
                            for gq in range(2):
                                def rs(h):
                                    ins = None
                                    for q in range(4):
                                        ins = h.matmul(ps[gq][:, q * 128:(q + 1) * 128], lhsT=ones_b[:], rhs=wTb[:, gq * 4 + q, :],
                                                       start=True, stop=True)
                                    return ins
                                P.op(pe, rs, R=["ones_b", "wTb"], W=[PSK[gq]])
                                for q in range(4):
                                    g = gq * 4 + q
                                    P.op(dve, lambda h: h.scalar_tensor_tensor(
                                        out=BIAS[:, g, :], in0=ps[gq][:, q * 128:(q + 1) * 128], scalar=pvec[:, 8, g:g + 1],
                                        in1=bs_bc[:, g, :], op0=ALU.mult, op1=ALU.add),
                                        R=["pvec", "bs_bc"], W=[PSK[gq], "BIAS"])
                            s_bs.close()
                            P.barrier()
                            vn32 = sb(f"vn32{l}{hf}", [128, D], F32, s_c)
                            wvs = [ws_next((KC, 256)) for _ in range(4)]
                            for ti in range(8):
                                t0 = h0 + ti * 128
                                pb0 = 2 * (ti % 2)
                                for q, (wv, wvk, _u) in enumerate(wvs):
                                    def fv(h, wv=wv, q=q):
                                        ins = None
                                        for kc in range(KC):
                                            ins = h.matmul(ps[pb0 + q // 2][:, (q % 2) * 256:(q % 2 + 1) * 256], lhsT=m[:, kc, t0:t0 + 128],
                                                           rhs=wv[:, kc, :], start=(kc == 0), stop=(kc == KC - 1))
                                        return ins
                                    P.op(pe, fv, R=[wvk] + [MK(kc, t0 // BLK) for kc in range(KC)], W=[PSK[pb0 + q // 2]])

                                def stats(h):
                                    h.bn_stats(out=stc[:, 0, 0:6], in_=ps[pb0][:])
                                    return h.bn_stats(out=stc[:, 1, 0:6], in_=ps[pb0 + 1][:])
                                P.op(dve, stats, W=[PSK[pb0], PSK[pb0 + 1], "stc"])
                                P.op(dve, lambda h: h.bn_aggr(out=stc[:, 0, 6:8], in_=stc[:, 0:2, 0:6]), R=["stc"], W=["cmv"])
                                P.op(act, lambda h: h.activation(out=stc[:, 1, 6:7], in_=stc[:, 0, 7:8], func=AF.Ln, bias=epsc[:, 0:1]),
                                     R=["cmv", "epsc"], W=["crstd"])
                                P.op(act, lambda h: h.activation(out=stc[:, 1, 6:7], in_=stc[:, 1, 6:7], func=AF.Exp, scale=-0.5),
                                     R=["crstd"], W=["crstd"])
                                P.op(dve, lambda h: h.tensor_scalar(out=stc[:, 1, 7:8], in0=stc[:, 0, 6:7], scalar1=stc[:, 1, 6:7],
                                                                    scalar2=-1.0, op0=ALU.mult, op1=ALU.mult),
                                     R=["cmv", "crstd"], W=["cnmr"])
                                for q in range(2):
                                    P.op(act, lambda h: h.activation(out=vn32[:, q * 512:(q + 1) * 512], in_=ps[pb0 + q][:], func=AF.Identity,
                                                                     scale=stc[:, 1, 6:7], bias=stc[:, 1, 7:8]),
                                         R=["crstd", "cnmr"], W=[PSK[pb0 + q], "vn32"])
                                P.op(dve, lambda h: h.tensor_tensor(out=VN(ti), in0=vn32[:], in1=gam_bc[:], op=ALU.mult),
                                     R=["vn32", "gam_bc"], W=[("act", ti, g) for g in range(8)])
                            for _wv, _wk, _u in wvs:
                                ws_done(_u)
                            for uq in range(4):
                                wu_, wuk, wuu = ws_next((KC, 256))
                                for gg in range(2):
                                    g = uq * 2 + gg
                                    for b2 in range(2):
                                        blk = 2 * hf + b2

                                        def fu(h):
                                            ins = None
                                            for kc in range(KC):
                                                ins = h.matmul(ps[2][:], lhsT=wu_[:, kc, gg * 128:(gg + 1) * 128],
                                                               rhs=m[:, kc, blk * BLK:(blk + 1) * BLK],
                                                               start=(kc == 0), stop=(kc == KC - 1))
                                            return ins
                                        P.op(pe, fu, R=[wuk] + [MK(kc, blk) for kc in range(KC)], W=[PSK[2]])

                                        def fm_(h):
                                            ins = None
                                            for tl in range(4):
                                                ti = b2 * 4 + tl
                                                ins = h.matmul(ps[3][:, tl * 128:(tl + 1) * 128], lhsT=VN(ti)[:, g * 128:(g + 1) * 128],
                                                               rhs=wTb[:, g, :], start=True, stop=True)
                                            return ins
                                        ck = [("act", b2 * 4 + tl, g) for tl in range(4)]
                                        P.op(pe, fm_, R=ck + ["wTb"], W=[PSK[3]])
                                        P.op(dve, lambda h: h.tensor_tensor(
                                            out=ctmp2[:].rearrange("p (a t) -> p a t", t=128),
                                            in0=ps[3][:].rearrange("p (a t) -> p a t", t=128),
                                            in1=BIAS[:, g, :].unsqueeze(1).to_broadcast([128, 4, 128]), op=ALU.add),
                                            R=["BIAS"], W=[PSK[3], "ctmp2"])
                                        P.op(dve, lambda h: h.tensor_tensor(
                                            out=cact[:, g, b2 * 4:(b2 + 1) * 4, :],
                                            in0=ps[2][:].rearrange("p (a t) -> p a t", t=128),
                                            in1=ctmp2[:].rearrange("p (a t) -> p a t", t=128), op=ALU.mult),
                                            R=["ctmp2"], W=[PSK[2]] + ck)
                                ws_done(wuu)
                        if debug is not None and debug[0] == "cact" and l == debug[2] and hf == debug[3]:
                            dump([actb[:, kc, :] for kc in range(KC)], [CK_all for kc in range(KC)])
                            break
                        yproj(hf, 2, lambda kc, b2: cact[:, kc, b2 * 4:(b2 + 1) * 4, :],
                              lambda kc, b2: [("act", b2 * 4 + tl, kc) for tl in range(4)], False)

                        alias_sync(AK_all, CK_all)
                        with PScope(P) as s_a:
                            hp = [sb(f"hp{l}{hf}{i}", [128, 4, SEGP], BF16, s_a) for i in range(2)]
                            DG = sb(f"DG{l}{hf}", [128, CONV_K, 128], BF16, s_a)
                            convw = sb(f"convw{l}{hf}", [128, KC, CONV_K], F32, s_a)
                            P.dma(sp, msem(), convw[:], convw_d[l], W=["convw"])
                            co32 = sb(f"co32{l}{hf}", [128, BLK], F32, s_a)
                            sq32 = sb(f"sq32{l}{hf}", [128, BLK], F32, s_a)
                            cmean = [sb(f"cmean{l}{hf}{i}", [128, BLK], F32, s_a) for i in range(2)]
                            crstd = [sb(f"crstd{l}{hf}{i}", [128, BLK], F32, s_a) for i in range(2)]
                            for u in range(8):
                                wa, wak, wau = ws_next((KC, 256))
                                for jj in range(1):
                                    j = u
                                    hs = j % 2
                                    hpk = ("hp", hs)
                                    hpt = hp[hs]
                                    if hf == 0:
                                        P.op(pool, lambda h: h.memset(hpt[:, 0, 0:HALO], 0.0), W=[hpk])
                                    else:
                                        P.op(pool, lambda h: h.memset(hpt[:, 3, SEG + HALO:SEGP], 0.0), W=[hpk])

                                    def vg(tok0, ntok, pv, pg):
                                        def f(h):
                                            ins = None
                                            for kc in range(KC):
                                                h.matmul(pv[:, 0:ntok], lhsT=wa[:, kc, jj * 128:(jj + 1) * 128],
                                                         rhs=m[:, kc, tok0:tok0 + ntok], start=(kc == 0), stop=(kc == KC - 1))
                                            for kc in range(KC):
                                                ins = h.matmul(pg[:, 0:ntok], lhsT=wa[:, kc, 128 + jj * 128:128 + (jj + 1) * 128],
                                                               rhs=m[:, kc, tok0:tok0 + ntok], start=(kc == 0), stop=(kc == KC - 1))
                                            return ins
                                        return f

                                    def halo(dst, pv, c0):
                                        P.op(dve, lambda h: h.scalar_tensor_tensor(out=dst, in0=pv[:, c0:c0 + HALO], scalar=carry,
                                                                                   in1=sgt[0][:, c0:c0 + HALO],
                                                                                   op0=ALU.mult, op1=ALU.mult),
                                             R=[("sgt", 0), "flags"], W=[PSK[0], hpk])

                                    for b2 in range(2):
                                        blk = 2 * hf + b2
                                        P.op(pe, vg(blk * BLK, BLK, ps[0], ps[1]), R=[wak] + [MK(kc, blk) for kc in range(KC)],
                                             W=[PSK[0], PSK[1]])
                                        P.op(act, lambda h: h.activation(out=sgt[0][:], in_=ps[1][:], func=AF.Sigmoid),
                                             W=[PSK[1], ("sgt", 0)])
                                        P.op(dve, lambda h: h.tensor_tensor(
                                            out=hpt[:, 2 * b2:2 * b2 + 2, HALO:HALO + SEG],
                                            in0=ps[0][:].rearrange("p (s t) -> p s t", t=SEG),
                                            in1=sgt[0][:].rearrange("p (s t) -> p s t", t=SEG), op=ALU.mult),
                                            R=[("sgt", 0)], W=[PSK[0], hpk])
                                        halo(hpt[:, 2 * b2 + 1, 0:HALO], ps[0], SEG - HALO)
                                        halo(hpt[:, 2 * b2, SEG + HALO:SEGP], ps[0], SEG)
                                        if b2 == 0:
                                            halo(hpt[:, 2, 0:HALO], ps[0], BLK - HALO)
                                        else:
                                            halo(hpt[:, 1, SEG + HALO:SEGP], ps[0], 0)
                                    if hf == 0:
                                        P.op(pe, vg(HT, HALO, ps[0], ps[1]), R=[wak] + [MK(kc, 2) for kc in range(KC)],
                                             W=[PSK[0], PSK[1]])
                                        P.op(act, lambda h: h.activation(out=sgt[0][:, 0:HALO], in_=ps[1][:, 0:HALO], func=AF.Sigmoid),
                                             W=[PSK[1], ("sgt", 0)])
                                        halo(hpt[:, 3, SEG + HALO:SEGP], ps[0], 0)
                                    else:
                                        P.op(pe, vg(HT - HALO, HALO, ps[0], ps[1]), R=[wak] + [MK(kc, 1) for kc in range(KC)],
                                             W=[PSK[0], PSK[1]])
                                        P.op(act, lambda h: h.activation(out=sgt[0][:, 0:HALO], in_=ps[1][:, 0:HALO], func=AF.Sigmoid),
                                             W=[PSK[1], ("sgt", 0)])
                                        halo(hpt[:, 0, 0:HALO], ps[0], 0)
                                    P.op(pool, lambda h: h.tensor_tensor(
                                        out=DG[:], in0=ident_b[:].unsqueeze(1).to_broadcast([128, CONV_K, 128]),
                                        in1=convw[:, j, :].unsqueeze(2).to_broadcast([128, CONV_K, 128]), op=ALU.mult),
                                        R=["ident_b", "convw"], W=["DG"])
                                    for b2 in range(2):
                                        def cv(h):
                                            ins = None
                                            for k in range(CONV_K):
                                                ins = h.matmul(ps[2][:].rearrange("p (s t) -> p s t", t=SEG), lhsT=DG[:, k, :],
                                                               rhs=hpt[:, 2 * b2:2 * b2 + 2, k:k + SEG],
                                                               start=(k == 0), stop=(k == CONV_K - 1))
                                            return ins
                                        P.op(pe, cv, R=["DG", hpk], W=[PSK[2]])
                                        P.op(act, lambda h: h.activation(out=co32[:], in_=ps[2][:], func=AF.Identity,
                                                                         bias=pvec[:, 0, j:j + 1], scale=1.0),
                                             R=["pvec"], W=[PSK[2], "co32"])
                                        P.op(pool, lambda h: h.tensor_copy(out=actb[:, j, b2 * BLK:(b2 + 1) * BLK], in_=co32[:]),
                                             R=["co32"], W=[("acta", j, b2)])
                                        P.op(dve, lambda h: h.tensor_tensor(out=sq32[:], in0=co32[:], in1=co32[:], op=ALU.mult),
                                             R=["co32"], W=["sq32"])
                                        P.op(pe, lambda h: h.matmul(ps[3 + b2][:], lhsT=ones_f[:], rhs=co32[:], start=(j == 0),
                                                                    stop=(j == KC - 1)), R=["ones_f", "co32"], W=[PSK[3 + b2]])
                                        P.op(pe, lambda h: h.matmul(ps[5 + b2][:], lhsT=ones_f[:], rhs=sq32[:], start=(j == 0),
                                                                    stop=(j == KC - 1)), R=["ones_f", "sq32"], W=[PSK[5 + b2]])
                                ws_done(wau)
                            for b2 in range(2):
                                P.op(act, lambda h: h.activation(out=cmean[b2][:], in_=ps[3 + b2][:], func=AF.Identity, scale=1.0 / D),
                                     W=[PSK[3 + b2], ("cmean", b2)])
                                P.op(dve, lambda h: h.tensor_tensor(out=sq32[:], in0=cmean[b2][:], in1=cmean[b2][:], op=ALU.mult),
                                     R=[("cmean", b2)], W=["sq32"])
                                P.op(dve, lambda h: h.scalar_tensor_tensor(out=crstd[b2][:], in0=ps[5 + b2][:], scalar=1.0 / D,
                                                                           in1=sq32[:], op0=ALU.mult, op1=ALU.subtract),
                                     R=["sq32"], W=[PSK[5 + b2], ("crstd", b2)])
                                P.op(act, lambda h: h.activation(out=crstd[b2][:], in_=crstd[b2][:], func=AF.Ln, bias=epsc[:, 0:1]),
                                     R=[("crstd", b2), "epsc"], W=[("crstd", b2)])
                                P.op(act, lambda h: h.activation(out=crstd[b2][:], in_=crstd[b2][:], func=AF.Exp, scale=-0.5),
                                     R=[("crstd", b2)], W=[("crstd", b2)])
                            for b2 in range(2):
                                for j in range(KC):
                                    asl = actb[:, j, b2 * BLK:(b2 + 1) * BLK]
                                    nb_, nk_ = (co32, "co32") if j % 2 == 0 else (sq32, "sq32")
                                    P.op(dve, lambda h: h.tensor_tensor(out=nb_[:], in0=asl, in1=cmean[b2][:], op=ALU.subtract),
                                         R=[("acta", j, b2), ("cmean", b2)], W=[nk_])
                                    P.op(dve, lambda h: h.tensor_tensor(out=nb_[:], in0=nb_[:], in1=crstd[b2][:], op=ALU.mult),
                                         R=[nk_, ("crstd", b2)], W=[nk_])
                                    P.op(act, lambda h: h.activation(out=asl, in_=nb_[:], func=AF.Silu,
                                                                     scale=pvec[:, 1, j:j + 1], bias=pvec[:, 2, j:j + 1]),
                                         R=[nk_, "pvec"], W=[("acta", j, b2)])
                        if debug is not None and debug[0] == "aact" and l == debug[2] and hf == debug[3]:
                            dump([actb[:, kc, :] for kc in range(KC)], [AK_all for kc in range(KC)])
                            break
                        yproj(hf, 0, lambda kc, b2: actb[:, kc, b2 * BLK:(b2 + 1) * BLK],
                              lambda kc, b2: [("acta", kc, b2)], False)
                        if debug is not None and debug[0] == "z" and l == debug[2] and hf == debug[3]:
                            dump([z[:, n, :] for n in range(KC)], [[("z", n, 0), ("z", n, 1)] for n in range(KC)])
                            break

                        alias_sync(ZBK_all, AK_all)
                        for kc in range(KC):
                            if kc % 2 == 0:
                                P.op(act, lambda h: h.activation(out=actb[:, kc, :], in_=z[:, kc, :], func=AF.Copy),
                                     R=[("z", kc, 0), ("z", kc, 1)], W=[("zb", kc, 0), ("zb", kc, 1)])
                            else:
                                P.op(dve, lambda h: h.tensor_copy(out=actb[:, kc, :], in_=z[:, kc, :]),
                                     R=[("z", kc, 0), ("z", kc, 1)], W=[("zb", kc, 0), ("zb", kc, 1)])
                        for u in range(4):
                            wo_, wok, wou = ws_next((KC, 256))
                            for nn in range(2):
                                n = 2 * u + nn
                                for b2 in range(2):
                                    blk = 2 * hf + b2
                                    pp = ycnt[0] % 2
                                    ycnt[0] += 1
                                    pm = ps[pp]

                                    def fo(h):
                                        ins = None
                                        for kc in range(KC):
                                            ins = h.matmul(pm[:], lhsT=wo_[:, kc, nn * 128:(nn + 1) * 128],
                                                           rhs=actb[:, kc, b2 * BLK:(b2 + 1) * BLK],
                                                           start=(kc == 0), stop=(kc == KC - 1))
                                        return ins
                                    P.op(pe, fo, R=[wok] + [("zb", kc, b2) for kc in range(KC)], W=[PSK[pp]])
                                    P.op(dve, lambda h: h.tensor_scalar(out=tmp2[:], in0=m[:, n, blk * BLK:(blk + 1) * BLK],
                                                                        scalar1=sh1(n), scalar2=ainv1(n),
                                                                        op0=ALU.subtract, op1=ALU.mult),
                                         R=[MK(n, blk)] + MODK, W=["tmp2"])
                                    P.op(dve, lambda h: h.scalar_tensor_tensor(out=tmp2[:], in0=rview(n, hf)[:, b2 * BLK:(b2 + 1) * BLK],
                                                                               scalar=ALPHA, in1=tmp2[:], op0=ALU.mult, op1=ALU.add),
                                         R=[("X", n, hf, 0), "tmp2"], W=["tmp2"])
                                    P.op(dve, lambda h: h.scalar_tensor_tensor(out=z[:, n, b2 * BLK:(b2 + 1) * BLK], in0=pm[:],
                                                                               scalar=g1(n), in1=tmp2[:], op0=ALU.mult, op1=ALU.add),
                                         R=["tmp2"] + MODK, W=[PSK[pp], ("z", n, b2)])
                            ws_done(wou)
                        with PScope(P) as s_ln:
                            lnsc = [sb(f"lnsc{l}{hf}_{i}", [128, BLK], F32, s_ln) for i in range(6)]
                            ln_feature_major(lambda n, b2: z[:, n, b2 * BLK:(b2 + 1) * BLK], lambda n, b2: [("z", n, b2)],
                                             lambda n, b2: X[:, n, hf, b2 * BLK:(b2 + 1) * BLK], lambda n, b2: XK(n, hf),
                                             lambda n: pvec[:, 4, n:n + 1], lambda n: pvec[:, 5, n:n + 1], lnsc)
                    if dbg_stop[0]:
                        break
                if debug is not None and debug[0] == "x1" and l == debug[2]:
                    dump([X[:, kc, hf, :] for kc in range(KC) for hf in range(2)], [XK(kc, hf) for kc in range(KC) for hf in range(2)])
                    break

            with PScope(P) as s_f:
                m2 = sb(f"m2{l}", [128, KC, HT], BF16, s_f)
                hid = sb(f"hid{l}", [128, FKC, HT], BF16, s_f)
                tb = sb(f"tb{l}", [128, KC, HT], F32, s_f)
                sgf = [sb(f"sgf{l}_{i}", [128, BLK], F32, s_f) for i in range(2)]
                lnsc2 = [sb(f"lnsc2{l}_{i}", [128, BLK], F32, s_f) for i in range(6)]
                fcnt = 0
                for hf in range(2):
                    for kc in range(KC):
                        P.op(act, lambda h: h.activation(out=m2[:, kc, :], in_=X[:, kc, hf, :], func=AF.Identity,
                                                         scale=opsc2(kc), bias=sh2(kc)),
                             R=XK(kc, hf) + MODK, W=[("m2", kc, 0), ("m2", kc, 1)])
                    for u in range(22):
                        wf, wfk, wfu = ws_next((KC, 256))
                        for jj in range(1):
                            j = u
                            for b2 in range(2):
                                pp = fcnt % 2
                                fcnt += 1
                                pa, pb_ = ps[2 * pp], ps[2 * pp + 1]

                                def fh(h):
                                    ins = None
                                    for kc in range(KC):
                                        h.matmul(pa[:], lhsT=wf[:, kc, jj * 128:(jj + 1) * 128], rhs=m2[:, kc, b2 * BLK:(b2 + 1) * BLK],
                                                 start=(kc == 0), stop=(kc == KC - 1))
                                    for kc in range(KC):
                                        ins = h.matmul(pb_[:], lhsT=wf[:, kc, 128 + jj * 128:128 + (jj + 1) * 128],
                                                       rhs=m2[:, kc, b2 * BLK:(b2 + 1) * BLK], start=(kc == 0), stop=(kc == KC - 1))
                                    return ins
                                P.op(pe, fh, R=[wfk] + [("m2", kc, b2) for kc in range(KC)], W=[PSK[2 * pp], PSK[2 * pp + 1]])
                                P.op(act, lambda h: h.activation(out=sgf[pp][:], in_=pa[:], func=AF.Silu), W=[PSK[2 * pp], ("sgf", pp)])
                                P.op(dve, lambda h: h.tensor_tensor(out=hid[:, j, b2 * BLK:(b2 + 1) * BLK], in0=pb_[:], in1=sgf[pp][:],
                                                                    op=ALU.mult), R=[("sgf", pp)], W=[PSK[2 * pp + 1], ("hid", j, b2)])
                        ws_done(wfu)
                    for n in range(KC):
                        w2a, w2ak, w2au = ws_next((11, 128))
                        w2b, w2bk, w2bu = ws_next((11, 128))
                        for b2 in range(2):
                            pp = fcnt % 2
                            fcnt += 1
                            pf = ps[pp]

                            def fo2(h):
                                ins = None
                                for kc in range(FKC):
                                    ins = h.matmul(pf[:], lhsT=(w2a if kc < 11 else w2b)[:, kc % 11, :], rhs=hid[:, kc, b2 * BLK:(b2 + 1) * BLK],
                                                   start=(kc == 0), stop=(kc == FKC - 1))
                                return ins
                            P.op(pe, fo2, R=[w2ak, w2bk] + [("hid", kc, b2) for kc in range(FKC)], W=[PSK[pp]])
                            P.op(act, lambda h: h.activation(out=sgf[pp][:], in_=pf[:], func=AF.Identity, scale=g2(n)),
                                 R=MODK, W=[PSK[pp], ("sgf", pp)])
                            P.op(dve, lambda h: h.scalar_tensor_tensor(out=tb[:, n, b2 * BLK:(b2 + 1) * BLK],
                                                                       in0=X[:, n, hf, b2 * BLK:(b2 + 1) * BLK], scalar=ALPHA,
                                                                       in1=sgf[pp][:], op0=ALU.mult, op1=ALU.add),
                                 R=XK(n, hf) + [("sgf", pp)], W=[("tb", n, b2)])
                        ws_done(w2au)
                        ws_done(w2bu)
                    ln_feature_major(lambda n, b2: tb[:, n, b2 * BLK:(b2 + 1) * BLK], lambda n, b2: [("tb", n, b2)],
                                     lambda n, b2: X[:, n, hf, b2 * BLK:(b2 + 1) * BLK], lambda n, b2: XK(n, hf),
                                     lambda n: pvec[:, 6, n:n + 1], lambda n: pvec[:, 7, n:n + 1], lnsc2)
            if debug is not None and debug[0] == "x2" and l == debug[2]:
                dump([X[:, kc, hf, :] for kc in range(KC) for hf in range(2)], [XK(kc, hf) for kc in range(KC) for hf in range(2)])
                break

        if not dbg_stop[0] and (debug is None or debug[0] == "full"):
            with PScope(P) as s_o:
                yt = [sb(f"yt{i}", [128, D], F32, s_o) for i in range(2)]
                ysem = [P.new_sem(f"y{i}") for i in range(2)]
                for i in range(T // 128):
                    s = i % 2
                    hf, tc = divmod(i, 8)
                    for half in range(2):
                        pb = ps[half]

                        def tr(h):
                            ins = None
                            for q in range(4):
                                kc = half * 4 + q
                                ins = h.transpose(pb[:, q * 128:(q + 1) * 128], X[:, kc, hf, tc * 128:(tc + 1) * 128], ident_f[:])
                            return ins
                        P.op(pe, tr, R=[k for q in range(4) for k in XK(half * 4 + q, hf)] + ["ident_f"], W=[PSK[half]])
                        P.op(act if half == 0 else dve,
                             (lambda h: h.activation(out=yt[s][:, 0:512], in_=pb[:], func=AF.Copy)) if half == 0 else
                             (lambda h: h.tensor_copy(out=yt[s][:, 512:1024], in_=pb[:])),
                             W=[PSK[half], ("yt", s, half)])
                    P.dma(sp, ysem[s], y_d[i * 128:(i + 1) * 128, :], yt[s][:], R=[("yt", s, 0), ("yt", s, 1)], is_output=True)

        if debug is not None and debug[0] == "input":
            dsem = P.new_sem("dbg")
            for kc in range(KC):
                for hf in range(2):
                    P.dma(sp, dsem, dbg_d[:, (kc * 2 + hf) * HT:(kc * 2 + hf + 1) * HT], X[:, kc, hf, :],
                          R=XK(kc, hf), is_output=True)

        for tok in P.out_toks:
            P._wait(sp, tok)
    return nc


def _prep_shared(inp):
    f = lambda k: np.asarray(inp[k], np.float32)
    w_in = f("w_in")
    sh = {}
    sh["lnin"] = np.ascontiguousarray(np.stack([_fm(f("ln_in_g")), _fm(f("ln_in_b"))], 1))
    sh["bada"] = np.ascontiguousarray(f("b_ada").reshape(L, 48, 128).transpose(0, 2, 1))
    pv = []
    for l in range(L):
        rows = [f("conv_b")[l], f("conv_ln_g")[l], f("conv_ln_b")[l], f("hgrn_norm_g")[l], f("ln1_g")[l],
                f("ln1_b")[l], f("ln2_g")[l], f("ln2_b")[l], f("sgu_ln_b")[l]]
        pv.append(np.stack([_fm(r) for r in rows], 1))
    sh["pvec"] = np.ascontiguousarray(np.stack(pv, 0))
    sh["lbraw"] = np.ascontiguousarray(np.stack([_fm(f("hgrn_lb")[0]), _fm(f("hgrn_lb")[1])], 1))
    sh["convw"] = np.ascontiguousarray(f("conv_w").transpose(0, 2, 1).reshape(L, KC, 128, CONV_K).transpose(0, 2, 1, 3))
    sh["sgug"] = f("sgu_ln_g")
    sh["sgub"] = f("sgu_ln_b")
    sh["sgubias"] = np.ascontiguousarray(f("sgu_b").reshape(L, NH * 128))
    sh["sguw"] = np.ascontiguousarray(f("sgu_w").transpose(0, 3, 1, 2))
    wada, winBq, winBz, winBi, winA, winCv, winCu, winG, wxo, wo, wffi, wffo = ([] for _ in range(12))
    for l in range(L):
        wi = w_in[l]
        wada.append(_wunits(f("w_ada")[l], [[(u * 256, 256)] for u in range(24)]))
        winBq.append(_wunits(wi, [[(OFF_B + 0 * D + h * 128, 128)] for h in range(NH)]))
        winBz.append(_wunits(wi, [[(OFF_B + g * D + h * 128, 128) for g in (1, 2)] for h in range(NH)]))
        winBi.append(_wunits(wi, [[(OFF_B + g * D + h * 128, 128) for g in (3, 4)] for h in range(NH)]))
        winA.append(_wunits(wi, [[(OFF_A + u * 128, 128), (OFF_A + D + u * 128, 128)] for u in range(8)]))
        winCu.append(_wunits(wi, [[(OFF_C + u * 256, 256)] for u in range(4)]))
        winCv.append(_wunits(wi, [[(OFF_C + D + u * 256, 256)] for u in range(4)]))
        winG.append(np.stack([_wunits(wi, [[(OFF_G + g * D + u * 256, 256)] for u in range(4)]) for g in range(3)], 0))
        wxo.append(np.stack([_wunits(f(k)[l], [[(u * 256, 256)] for u in range(4)])
                             for k in ("w_a_out", "w_b_out", "w_c_out")], 0))
        wo.append(_wunits(f("w_o")[l], [[(u * 256, 256)] for u in range(4)]))
        wf = f("w_ffn_in")[l]
        wffi.append(_wunits(wf, [[(u * 128, 128), (FF + u * 128, 128)] for u in range(22)]))
        w2 = f("w_ffn_out")[l]
        wffo.append(np.ascontiguousarray(
            np.stack([w2[:, n * 128:(n + 1) * 128].reshape(2, 11, 128, 128).transpose(0, 2, 1, 3) for n in range(8)], 0)))
    sh["wada"] = np.stack(wada, 0)
    sh["winBq"] = np.stack(winBq, 0)
    sh["winBz"] = np.stack(winBz, 0)
    sh["winBi"] = np.stack(winBi, 0)
    sh["winA"] = np.stack(winA, 0)
    sh["winCv"] = np.stack(winCv, 0)
    sh["winCu"] = np.stack(winCu, 0)
    sh["winG"] = np.stack(winG, 0)
    sh["wxo"] = np.stack(wxo, 0)
    sh["wo"] = np.stack(wo, 0)
    sh["wffi"] = np.stack(wffi, 0)
    sh["wffo"] = np.stack(wffo, 0)
    return {k: np.ascontiguousarray(v, dtype=np.float32) for k, v in sh.items()}


def _prep_cores(inp):
    xp = np.asarray(inp["x_prompt"], np.float32)
    xs_ = np.asarray(inp["x_sample"], np.float32)
    st = np.asarray(inp["state_hgrn"], np.float32)
    c = np.asarray(inp["c"], np.float32)
    cc = np.asarray(inp["c_ctx"], np.float32)
    cores = []
    for i in range(8):
        d = {}
        fl = np.zeros((128, 4), np.float32)
        if i < 4:
            d["x"] = np.ascontiguousarray(xs_[i])
            fl[:, 0] = 1.0
            fl[:, 1] = 1.0
            d["cond"] = _fm(c[i])
            d["s0"] = np.ascontiguousarray(st[i])
        else:
            blk = xp[4 * (i - 4):4 * (i - 4) + 4].reshape(4 * SEG, D)
            d["x"] = np.ascontiguousarray(np.concatenate([blk, blk], 0))
            d["cond"] = _fm(cc)
            d["s0"] = np.zeros((L, 2, NH, 128, 128), np.float32)
        d["flags"] = fl
        cores.append(d)
    return cores


def kernel(**inputs):
    shared = _prep_shared(inputs)
    cores = _prep_cores(inputs)
    nc = build_program()
    in_maps = [dict(shared, **c) for c in cores]
    res = run_bass_kernel_spmd(nc, in_maps, core_ids=list(range(8)))
    y_prompt = np.zeros((16, SEG, D), np.float32)
    y_sample = np.zeros((4, T, D), np.float32)
    new_state = np.zeros((16, L, 2, NH, 128, 128), np.float32)
    for i, r in enumerate(res.results):
        if i < 4:
            y_sample[i] = r["y"]
        else:
            y_prompt[4 * (i - 4):4 * (i - 4) + 4] = r["y"][:4 * SEG].reshape(4, SEG, D)
            stt = r["st"]
            new_state[4 * (i - 4):4 * (i - 4) + 4] = stt[:, :, 0:4].transpose(2, 0, 1, 3, 4, 5)
    return (y_prompt, y_sample, new_state)
```

```python
import math
from contextlib import ExitStack

import numpy as np
import concourse.bass as bass
import concourse.mybir as mybir
from concourse.bass_utils import run_bass_kernel_spmd

F32 = mybir.dt.float32
BF16 = mybir.dt.bfloat16
I32 = mybir.dt.int32
AF = mybir.ActivationFunctionType
ALU = mybir.AluOpType

D = 1024
KC = 8
T = 2048
HT = 1024
BLK = 512
NBLK = 4
NSEG = 8
SEG = 256
CH = 32
NCH = 64
FF = 2816
FKC = 22
L = 2
NH = 8
CONV_K = 31
HALO = 15
SEGP = SEG + 2 * HALO
ALPHA = (2 * L) ** 0.25
EPS = 1e-5
SLOT = 2048
PE_KEEPWARM = 1
NSLOT = 5

DEBUG = None


class Sem:
    def __init__(self, h):
        self.h = h
        self.count = 0


class Eng:
    def __init__(self, name, h, sem):
        self.name = name
        self.h = h
        self.sem = sem
        self.waited = {}


class Prog:
    def __init__(self, nc, es):
        self.nc = nc
        self.es = es
        self.lastw = {}
        self.readers = {}
        self.nsem = 0
        self.dma_sems = []

        def mk(name, h):
            return Eng(name, h, self.new_sem(name))

        self.pe = mk("pe", nc.tensor)
        self.dve = mk("dve", nc.vector)
        self.act = mk("act", nc.scalar)
        self.pool = mk("pool", nc.gpsimd)
        self.sp = mk("sp", nc.sync)
        self.out_toks = []

    def new_sem(self, name):
        self.nsem += 1
        sm = Sem(self.es.enter_context(self.nc.semaphore(f"s{self.nsem}_{name}")))
        self.dma_sems.append(sm)
        return sm

    def _wait(self, eng, tok):
        sem, val = tok
        if eng.waited.get(sem, 0) >= val:
            return
        eng.h.wait_ge(sem.h, val)
        eng.waited[sem] = val

    def _deps(self, R, W):
        deps = []
        for k in R:
            w = self.lastw.get(k)
            if w is not None:
                deps.append((w, "raw"))
        for k in W:
            w = self.lastw.get(k)
            if w is not None:
                deps.append((w, "waw"))
            for sem, val in self.readers.get(k, {}).items():
                deps.append(((sem, val), "war"))
        return deps

    def _record(self, tok, R, W):
        for k in W:
            self.lastw[k] = tok
            self.readers[k] = {}
        for k in R:
            d = self.readers.setdefault(k, {})
            if d.get(tok[0], 0) < tok[1]:
                d[tok[0]] = tok[1]

    def op(self, eng, fn, R=(), W=()):
        for tok, kind in self._deps(R, W):
            if tok[0] is eng.sem and kind != "raw":
                continue
            self._wait(eng, tok)
        ins = fn(eng.h)
        eng.sem.count += 1
        ins.then_inc(eng.sem.h, 1)
        tok = (eng.sem, eng.sem.count)
        self._record(tok, R, W)
        return tok

    def barrier(self):
        engs = [self.pe, self.dve, self.act, self.pool, self.sp]
        sems = [e.sem for e in engs] + list(self.dma_sems)
        for e in engs:
            for sm in sems:
                if sm is not e.sem and sm.count > 0:
                    self._wait(e, (sm, sm.count))

    def dma(self, q, sem, out, in_, R=(), W=(), is_output=False):
        for tok, kind in self._deps(R, W):
            self._wait(q, tok)
        if sem.count:
            self._wait(q, (sem, sem.count))
        ins = q.h.dma_start(out=out, in_=in_)
        sem.count += 16
        ins.then_inc(sem.h, 16)
        tok = (sem, sem.count)
        self._record(tok, R, W)
        if is_output:
            self.out_toks.append(tok)
        return tok


class PScope(ExitStack):
    def __init__(self, prog):
        super().__init__()
        self.prog = prog

    def __exit__(self, *exc):
        r = super().__exit__(*exc)
        if exc[0] is None:
            self.prog.barrier()
        return r


def _fm(v):
    return np.ascontiguousarray(np.asarray(v, np.float32).reshape(KC, 128).T)


def _wunits(w, col_groups):
    out = []
    for grp in col_groups:
        cols = np.concatenate([w[:, a:a + s] for a, s in grp], axis=1)
        out.append(cols.reshape(KC, 128, cols.shape[1]).transpose(1, 0, 2))
    return np.ascontiguousarray(np.stack(out, 0), dtype=np.float32)


A_IN = 2 * D
B_IN = 5 * D
C_IN = 2 * D
OFF_A = 0
OFF_B = A_IN
OFF_C = A_IN + B_IN
OFF_G = A_IN + B_IN + C_IN


def build_program(debug=None):
    nc = bass.Bass("TRN2", target_bir_lowering=False)

    def din(name, shape):
        return nc.dram_tensor(name, list(shape), F32, kind="ExternalInput").ap()

    x_d = din("x", [T, D])
    flags_d = din("flags", [128, 4])
    cond_d = din("cond", [128, KC])
    s0_d = din("s0", [L, 2, NH, 128, 128])
    lnin_d = din("lnin", [128, 2, KC])
    bada_d = din("bada", [L, 128, 48])
    pvec_d = din("pvec", [L, 128, 9, KC])
    lbraw_d = din("lbraw", [128, 2, KC])
    convw_d = din("convw", [L, 128, KC, CONV_K])
    sgug_d = din("sgug", [L, D])
    sgub_d = din("sgub", [L, D])
    sgubias_d = din("sgubias", [L, NH * 128])
    sguw_d = din("sguw", [L, 128, NH, 128])
    wada_d = din("wada", [L, 24, 128, KC, 256])
    winBq_d = din("winBq", [L, NH, 128, KC, 128])
    winBz_d = din("winBz", [L, NH, 128, KC, 256])
    winBi_d = din("winBi", [L, NH, 128, KC, 256])
    winA_d = din("winA", [L, 8, 128, KC, 256])
    winCv_d = din("winCv", [L, 4, 128, KC, 256])
    winCu_d = din("winCu", [L, 4, 128, KC, 256])
    winG_d = din("winG", [L, 3, 4, 128, KC, 256])
    wxo_d = din("wxo", [L, 3, 4, 128, KC, 256])
    wo_d = din("wo", [L, 4, 128, KC, 256])
    wffi_d = din("wffi", [L, 22, 128, KC, 256])
    wffo_d = din("wffo", [L, 8, 2, 128, 11, 128])

    y_d = nc.dram_tensor("y", [T, D], F32, kind="ExternalOutput").ap()
    st_d = nc.dram_tensor("st", [L, 2, NSEG, NH, 128, 128], F32, kind="ExternalOutput").ap()
    dbg_d = None
    if debug is not None:
        dbg_d = nc.dram_tensor("dbg", [128, debug[1]], F32, kind="ExternalOutput").ap()

    with ExitStack() as es:
        P = Prog(nc, es)
        block = es.enter_context(nc.Block())
        pe, dve, act, pool, sp = P.pe, P.dve, P.act, P.pool, P.sp

        def sb(name, shape, dt, stack=es):
            return stack.enter_context(nc.sbuf_tensor("sb_" + name, list(shape), dt))

        ps = [es.enter_context(nc.psum_tensor(f"ps{i}", [128, 512], F32)) for i in range(7)]
        psT = es.enter_context(nc.psum_tensor("psT", [128, 1024], BF16))
        PSK = [("ps", i) for i in range(7)]
        PSTK = ("ps", 7)

        X = sb("X", [128, KC, 2, HT], F32)
        wslots = [sb(f"wslot{i}", [128, SLOT], BF16) for i in range(NSLOT)]
        wsems = [P.new_sem(f"w{i}") for i in range(NSLOT)]
        ident_b = sb("ident_b", [128, 128], BF16)
        ident_f = sb("ident_f", [128, 128], F32)
        ones_f = sb("ones_f", [128, 128], F32)
        epsc = sb("epsc", [128, 1], F32)
        flags = sb("flags", [128, 4], F32)
        lnin = sb("lnin", [128, 2, KC], F32)
        scb = sb("scb", [128, KC], BF16)
        modSa = sb("modS", [128, L, 48], F32)
        modDa = sb("modD", [128, L, 4, KC], F32)
        badaa = sb("badaa", [128, L, 48], F32)
        pvec = sb("pvec", [128, 9, KC], F32)
        lbv = sb("lbv", [128, 3, KC], F32)
        lbraw = sb("lbraw", [128, 2, KC], F32)
        misc_sems = [P.new_sem(f"misc{i}") for i in range(8)]
        misc_i = [0]

        def msem():
            misc_i[0] += 1
            return misc_sems[misc_i[0] % len(misc_sems)]

        carry = flags[:, 0:1]
        peflag = flags[:, 1:2]

        def XK(kc, hf):
            return [("X", kc, hf, 0), ("X", kc, hf, 1)]

        class WS:
            units = []
            issued = 0
            cur = 0
            done = set()

        def ws_pump():
            while WS.issued < len(WS.units) and WS.issued <= WS.cur + NSLOT - 1:
                u = WS.issued
                if u >= NSLOT and (u - NSLOT) not in WS.done:
                    break
                ap_in, shp = WS.units[u]
                slot = u % NSLOT
                n = shp[0] * shp[1]
                ov = wslots[slot][:, 0:n].rearrange("p (k c) -> p k c", k=shp[0])
                P.dma(pool, wsems[slot], ov, ap_in, W=[("w", slot)])
                WS.issued += 1

        def ws_next(shp):
            u = WS.cur
            exp_ap, exp_shp = WS.units[u]
            assert exp_shp == shp, (u, exp_shp, shp)
            ws_pump()
            assert WS.issued > u, ("weight unit not issuable", u)
            WS.cur += 1
            ws_pump()
            slot = u % NSLOT
            n = shp[0] * shp[1]
            return wslots[slot][:, 0:n].rearrange("p (k c) -> p k c", k=shp[0]), ("w", slot), u

        def ws_done(u):
            WS.done.add(u)
            ws_pump()

        def unit_plan():
            for l in range(L):
                for u in range(24):
                    yield wada_d[l, u], (KC, 256)
            for l in range(L):
                for h in range(NH):
                    yield winBq_d[l, h], (KC, 128)
                    yield winBz_d[l, h], (KC, 256)
                    yield winBi_d[l, h], (KC, 256)
                for hf in range(2):
                    for br in (1, 2, 0):
                        if br == 2:
                            for u in range(4):
                                yield winCv_d[l, u], (KC, 256)
                            for u in range(4):
                                yield winCu_d[l, u], (KC, 256)
                        if br == 0:
                            for u in range(8):
                                yield winA_d[l, u], (KC, 256)
                        for u in range(4):
                            yield wxo_d[l, br, u], (KC, 256)
                            yield winG_d[l, br, u], (KC, 256)
                    for u in range(4):
                        yield wo_d[l, u], (KC, 256)
                for hf in range(2):
                    for u in range(22):
                        yield wffi_d[l, u], (KC, 256)
                    for n in range(8):
                        for q in range(2):
                            yield wffo_d[l, n, q], (11, 128)

        WS.units = list(unit_plan())
        if debug is not None and debug[0] in ("input",):
            WS.units = []

        def c_ident(h):
            h.memset(ident_f[:], 1.0)
            return h.affine_select(out=ident_f[:], in_=ident_f[:], pattern=[[-1, 128]],
                                   compare_op=ALU.is_equal, fill=0.0, base=0, channel_multiplier=1)
        P.op(pool, c_ident, W=["ident_f"])
        P.op(pool, lambda h: h.tensor_copy(out=ident_b[:], in_=ident_f[:]), R=["ident_f"], W=["ident_b"])
        P.op(pool, lambda h: h.memset(ones_f[:], 1.0), W=["ones_f"])
        P.op(pool, lambda h: h.memset(epsc[:], EPS), W=["epsc"])
        P.dma(sp, msem(), flags[:], flags_d, W=["flags"])
        P.dma(sp, msem(), lnin[:], lnin_d, W=["lnin"])
        P.dma(sp, msem(), lbraw[:], lbraw_d, W=["lbraw"])

        with PScope(P) as s_in:
            W2 = sb("W2", [128, 512], F32, s_in)
            PH = sb("PH", [128, 512], F32, s_in)
            pe_c = sb("pe_c", [128, 512], F32, s_in)
            jint = sb("jint", [128, 256], I32, s_in)
            pint = sb("pint", [128, 16], I32, s_in)
            pcol = sb("pcol", [128, 4], F32, s_in)
            rr = sb("rr", [128, 16], F32, s_in)
            argt = [sb(f"argt{i}", [128, 512], F32, s_in) for i in range(2)]
            xt = [sb(f"xt{i}", [128, D], F32, s_in) for i in range(2)]
            xsem = [P.new_sem(f"x{i}") for i in range(2)]
            stt = sb("stt", [128, 2, 8], F32, s_in)
            cond = sb("cond", [128, KC], F32, s_in)
            ctmp = sb("ctmp", [128, KC], F32, s_in)

            P.dma(sp, msem(), cond[:], cond_d, W=["cond"])
            P.op(pool, lambda h: h.iota(jint[:], pattern=[[1, 256]], base=0, channel_multiplier=0), W=["jint"])
            P.op(pool, lambda h: h.tensor_copy(out=W2[:, 0:256], in_=jint[:]), R=["jint"], W=["W2a"])
            P.op(act, lambda h: h.activation(out=W2[:, 0:256], in_=W2[:, 0:256], func=AF.Exp,
                                             scale=-math.log(10000.0) / 256.0), R=["W2a"], W=["W2a"])
            P.op(pool, lambda h: h.tensor_copy(out=W2[:, 256:512], in_=W2[:, 0:256]), R=["W2a"], W=["W2b"])
            P.op(pool, lambda h: h.memset(PH[:, 0:256], 0.0), W=["PHa"])
            P.op(pool, lambda h: h.memset(PH[:, 256:512], math.pi / 2), W=["PHb"])
            WK = ["W2a", "W2b", "PHa", "PHb"]
            P.op(pool, lambda h: h.iota(pint[:, 0:1], pattern=[[0, 1]], base=0, channel_multiplier=1), W=["pint0"])
            P.op(pool, lambda h: h.tensor_copy(out=pcol[:, 0:1], in_=pint[:, 0:1]), R=["pint0"], W=["pcol0"])
            P.op(pool, lambda h: h.tensor_single_scalar(out=pcol[:, 1:2], in_=pcol[:, 0:1], scalar=64.0, op=ALU.is_ge),
                 R=["pcol0"], W=["pcol1"])
            P.op(dve, lambda h: h.scalar_tensor_tensor(out=pcol[:, 2:3], in0=pcol[:, 1:2], scalar=-64.0,
                                                        in1=pcol[:, 0:1], op0=ALU.mult, op1=ALU.add),
                 R=["pcol0", "pcol1"], W=["pcol2"])
            P.op(pool, lambda h: h.iota(pint[:, 0:16], pattern=[[2, 16]], base=0, channel_multiplier=0),
                 R=["pcol0"], W=["pint0"])
            P.op(pool, lambda h: h.tensor_copy(out=rr[:], in_=pint[:, 0:16]), R=["pint0"], W=["rr"])
            P.op(pool, lambda h: h.tensor_scalar(out=rr[:], in0=rr[:], scalar1=pcol[:, 1:2], scalar2=None, op0=ALU.add),
                 R=["rr", "pcol1"], W=["rr"])

            nint = sb("nint", [128, 512], I32, s_in)
            nflt = sb("nflt", [128, 512], F32, s_in)

            def pe_table(dst, dkey, scal_ap, scal_keys, dst_flagmul, np_=128):
                nf, ni = nflt[0:np_, :], nint[0:np_, :]
                P.op(dve, lambda h: h.scalar_tensor_tensor(out=dst, in0=W2[0:np_, :], scalar=scal_ap, in1=PH[0:np_, :],
                                                           op0=ALU.mult, op1=ALU.add),
                     R=WK + scal_keys, W=[dkey])
                P.op(dve, lambda h: h.tensor_scalar(out=nf, in0=dst, scalar1=1.0 / (2 * math.pi), scalar2=0.5,
                                                    op0=ALU.mult, op1=ALU.add), R=[dkey], W=["nflt"])
                P.op(dve, lambda h: h.tensor_copy(out=ni, in_=nf), R=["nflt"], W=["nint"])
                P.op(dve, lambda h: h.tensor_copy(out=nf, in_=ni), R=["nint"], W=["nflt"])
                P.op(dve, lambda h: h.scalar_tensor_tensor(out=dst, in0=nf, scalar=-2 * math.pi, in1=dst,
                                                           op0=ALU.mult, op1=ALU.add), R=["nflt", dkey], W=[dkey])
                P.op(dve, lambda h: h.tensor_single_scalar(out=nf, in_=dst, scalar=-math.pi, op=ALU.is_lt),
                     R=[dkey], W=["nflt"])
                P.op(dve, lambda h: h.scalar_tensor_tensor(out=dst, in0=nf, scalar=2 * math.pi, in1=dst,
                                                           op0=ALU.mult, op1=ALU.add), R=["nflt", dkey], W=[dkey])
                P.op(dve, lambda h: h.tensor_scalar(out=dst, in0=dst, scalar1=math.pi, scalar2=-math.pi,
                                                    op0=ALU.min, op1=ALU.max), R=[dkey], W=[dkey])
                P.op(act, lambda h: h.activation(out=dst, in_=dst, func=AF.Sin), R=[dkey], W=[dkey])
                if dst_flagmul:
                    P.op(dve, lambda h: h.tensor_scalar(out=dst, in0=dst, scalar1=peflag, scalar2=None, op0=ALU.mult),
                         R=[dkey, "flags"], W=[dkey])

            negpi = sb("negpi", [128, 1], F32, s_in)
            P.op(pool, lambda h: h.memset(negpi[:], -math.pi), W=["negpi"])
            pe_table(pe_c[:], "pe_c", pcol[:, 2:3], ["pcol2"], True)
            Rtab = sb("Rtab", [32, 512], F32, s_in)
            Sel = sb("Sel", [32, 16, 128], F32, s_in)
            pe_table(Rtab[:], "Rtab", pcol[0:32, 0:1], ["pcol0"], False, np_=32)

            def c_sel(h):
                h.memset(Sel[:], 1.0)
                h.affine_select(out=Sel[:, :, 0:64], in_=Sel[:, :, 0:64], pattern=[[-2, 16], [0, 64]],
                                compare_op=ALU.is_equal, fill=0.0, base=0, channel_multiplier=1)
                return h.affine_select(out=Sel[:, :, 64:128], in_=Sel[:, :, 64:128], pattern=[[-2, 16], [0, 64]],
                                       compare_op=ALU.is_equal, fill=0.0, base=-1, channel_multiplier=1)
            P.op(pool, c_sel, W=["Sel"])

            P.op(act, lambda h: h.activation(out=ctmp[:], in_=cond[:], func=AF.Exp, scale=-1.0), R=["cond"], W=["ctmp"])
            P.op(dve, lambda h: h.tensor_scalar(out=ctmp[:], in0=ctmp[:], scalar1=1.0, scalar2=None, op0=ALU.add),
                 R=["ctmp"], W=["ctmp"])
            P.op(dve, lambda h: h.reciprocal(out=ctmp[:], in_=ctmp[:]), R=["ctmp"], W=["ctmp"])
            P.op(dve, lambda h: h.tensor_tensor(out=scb[:], in0=cond[:], in1=ctmp[:], op=ALU.mult),
                 R=["cond", "ctmp"], W=["scb"])

            for l_ in range(L):
                P.dma(sp, msem(), badaa[:, l_, :], bada_d[l_], W=["bada"])
            ada_list = [(l_, u) for l_ in range(L) for u in range(24)]

            def emit_ada(l_, u):
                wv, wk, wu = ws_next((KC, 256))

                def ada(h):
                    ins = None
                    for jj in range(2):
                        j = 48 * l_ + 2 * u + jj
                        for kc in range(KC):
                            ins = h.matmul(ps[6][:, j:j + 1], lhsT=wv[:, kc, jj * 128:(jj + 1) * 128],
                                           rhs=scb[:, kc:kc + 1], start=(kc == 0), stop=(kc == KC - 1))
                    return ins
                P.op(pe, ada, R=[wk, "scb"], W=[PSK[6]])
                ws_done(wu)

            for i in range(T // 128):
                for l_, u in ada_list[3 * i:3 * i + 3]:
                    if WS.units:
                        emit_ada(l_, u)
                s = i % 2
                hf, tc = divmod(i, 8)
                xk = ("xt", s)
                ak = ("argt", s)
                P.dma(sp, xsem[s], xt[s][:], x_d[i * 128:(i + 1) * 128, :], W=[xk])
                pbk = 2 + i % 2
                P.op(pe, lambda h: h.matmul(ps[pbk][:], lhsT=Sel[:, i, :], rhs=Rtab[:], start=True, stop=True),
                     R=["Sel", "Rtab"], W=[PSK[pbk]])
                P.op(dve, lambda h: h.scalar_tensor_tensor(out=xt[s][:, 0:512], in0=ps[pbk][:], scalar=peflag,
                                                           in1=xt[s][:, 0:512], op0=ALU.mult, op1=ALU.add),
                     R=[xk, "flags"], W=[PSK[pbk], xk])
                P.op(dve, lambda h: h.tensor_tensor(out=xt[s][:, 512:1024], in0=xt[s][:, 512:1024], in1=pe_c[:],
                                                    op=ALU.add), R=[xk, "pe_c"], W=[xk])

                def stats(h):
                    h.bn_stats(out=stt[:, 0, 0:6], in_=xt[s][:, 0:512])
                    return h.bn_stats(out=stt[:, 1, 0:6], in_=xt[s][:, 512:1024])
                P.op(dve, stats, R=[xk], W=["stt"])
                P.op(dve, lambda h: h.bn_aggr(out=stt[:, 0, 6:8], in_=stt[:, 0:2, 0:6]), R=["stt"], W=["mv"])
                P.op(act, lambda h: h.activation(out=stt[:, 1, 6:7], in_=stt[:, 0, 7:8], func=AF.Ln, bias=epsc[:, 0:1]),
                     R=["mv", "epsc"], W=["rstd"])
                P.op(act, lambda h: h.activation(out=stt[:, 1, 6:7], in_=stt[:, 1, 6:7], func=AF.Exp, scale=-0.5),
                     R=["rstd"], W=["rstd"])
                P.op(dve, lambda h: h.tensor_scalar(out=stt[:, 1, 7:8], in0=stt[:, 0, 6:7], scalar1=stt[:, 1, 6:7],
                                                    scalar2=-1.0, op0=ALU.mult, op1=ALU.mult),
                     R=["mv", "rstd"], W=["nmr"])
                P.op(dve, lambda h: h.tensor_scalar(out=xt[s][:], in0=xt[s][:], scalar1=stt[:, 1, 6:7],
                                                    scalar2=stt[:, 1, 7:8], op0=ALU.mult, op1=ALU.add),
                     R=[xk, "rstd", "nmr"], W=[xk])
                for half in range(2):
                    pb = ps[half]

                    def tr(h, half=half, pb=pb):
                        ins = None
                        for q in range(4):
                            kc = half * 4 + q
                            ins = h.transpose(pb[:, q * 128:(q + 1) * 128], xt[s][:, kc * 128:(kc + 1) * 128], ident_f[:])
                        return ins
                    P.op(pe, tr, R=[xk, "ident_f"], W=[PSK[half]])
                    for q in range(4):
                        kc = half * 4 + q
                        P.op(act, lambda h, kc=kc, q=q, pb=pb: h.activation(
                            out=X[:, kc, hf, tc * 128:(tc + 1) * 128], in_=pb[:, q * 128:(q + 1) * 128],
                            func=AF.Identity, scale=lnin[:, 0, kc:kc + 1], bias=lnin[:, 1, kc:kc + 1]),
                            R=["lnin"], W=[PSK[half]] + XK(kc, hf))

        if WS.units:
            P.op(dve, lambda h: h.tensor_tensor(out=modSa[:].rearrange("p l c -> p (l c)"), in0=ps[6][:, 0:96],
                                                in1=badaa[:].rearrange("p l c -> p (l c)"), op=ALU.add),
                 R=["bada"], W=[PSK[6], "mod"])
            for l_ in range(L):
                P.op(dve, lambda h: h.tensor_scalar(out=modDa[:, l_, 0, :], in0=modSa[:, l_, 8:16], scalar1=1.0, scalar2=None,
                                                    op0=ALU.add), R=["mod"], W=[("modD0", l_)])
                P.op(dve, lambda h: h.reciprocal(out=modDa[:, l_, 1, :], in_=modDa[:, l_, 0, :]), R=[("modD0", l_)], W=[("modD1", l_)])
                P.op(dve, lambda h: h.tensor_scalar(out=modDa[:, l_, 2, :], in0=modDa[:, l_, 1, :], scalar1=ALPHA, scalar2=None,
                                                    op0=ALU.mult), R=[("modD1", l_)], W=[("modD2", l_)])
                P.op(dve, lambda h: h.tensor_scalar(out=modDa[:, l_, 3, :], in0=modSa[:, l_, 32:40], scalar1=1.0, scalar2=None,
                                                    op0=ALU.add), R=["mod"], W=[("modD3", l_)])

        def alias_sync(new_keys, old_keys):
            merged = {}
            for k in old_keys:
                w = P.lastw.get(k)
                if w is not None:
                    merged[w[0]] = max(merged.get(w[0], 0), w[1])
                for sem, val in P.readers.get(k, {}).items():
                    merged[sem] = max(merged.get(sem, 0), val)
            for k in new_keys:
                P.lastw.pop(k, None)
                P.readers[k] = dict(merged)

        def rview(kc, hf):
            return X[:, kc, hf, 0:512].bitcast(BF16)

        def bview(kc, hf):
            return X[:, kc, hf, 512:1024].bitcast(BF16)

        def MK(kc, blk):
            return ("m", kc, blk)

        dbg_stop = [debug is not None and debug[0] == "input"]

        def dump(ap_list, keys):
            dsem = P.new_sem("dbg")
            off = 0
            for ap, k in zip(ap_list, keys):
                n = ap.shape[-1] if len(ap.shape) == 2 else int(np.prod(ap.shape[1:]))
                P.dma(sp if ap.dtype == F32 else pool, dsem, dbg_d[0:ap.shape[0], off:off + n], ap, R=k, is_output=True)
                off += n
            dbg_stop[0] = True

        def ln_feature_major(t_ap, t_keys, out_ap, out_keys, g_ap, b_ap, scratch, nb2=2, silu=False, gb_keys=("pvec",)):
            sqa, mean, rstd, t1a, sqb, t1b = scratch
            sqs = [sqa, sqb]
            t1s = [t1a, t1b]
            for b2 in range(nb2):
                def s1(h):
                    ins = None
                    for n in range(KC):
                        ins = h.matmul(ps[5][:], lhsT=ones_f[:], rhs=t_ap(n, b2), start=(n == 0), stop=(n == KC - 1))
                    return ins
                P.op(pe, s1, R=["ones_f"] + [k for n in range(KC) for k in t_keys(n, b2)], W=[PSK[5]])
                for n in range(KC):
                    sq = sqs[n % 2]
                    P.op(act, lambda h: h.activation(out=sq[:], in_=t_ap(n, b2), func=AF.Square),
                         R=t_keys(n, b2), W=[("lnsq", n % 2)])
                    P.op(pe, lambda h: h.matmul(ps[6][:], lhsT=ones_f[:], rhs=sq[:], start=(n == 0), stop=(n == KC - 1)),
                         R=["ones_f", ("lnsq", n % 2)], W=[PSK[6]])
                P.op(act, lambda h: h.activation(out=mean[:], in_=ps[5][:], func=AF.Identity, scale=1.0 / D),
                     W=[PSK[5], "lnmean"])
                P.op(dve, lambda h: h.tensor_tensor(out=t1a[:], in0=mean[:], in1=mean[:], op=ALU.mult),
                     R=["lnmean"], W=[("lnt1", 0)])
                P.op(dve, lambda h: h.scalar_tensor_tensor(out=rstd[:], in0=ps[6][:], scalar=1.0 / D, in1=t1a[:],
                                                           op0=ALU.mult, op1=ALU.subtract),
                     R=[("lnt1", 0)], W=[PSK[6], "lnrstd"])
                P.op(act, lambda h: h.activation(out=rstd[:], in_=rstd[:], func=AF.Ln, bias=epsc[:, 0:1]),
                     R=["lnrstd", "epsc"], W=["lnrstd"])
                P.op(act, lambda h: h.activation(out=rstd[:], in_=rstd[:], func=AF.Exp, scale=-0.5),
                     R=["lnrstd"], W=["lnrstd"])
                for n in range(KC):
                    t1 = t1s[n % 2]
                    tk = ("lnt1", n % 2)
                    P.op(dve, lambda h: h.tensor_tensor(out=t1[:], in0=t_ap(n, b2), in1=mean[:], op=ALU.subtract),
                         R=t_keys(n, b2) + ["lnmean"], W=[tk])
                    P.op(dve, lambda h: h.tensor_tensor(out=t1[:], in0=t1[:], in1=rstd[:], op=ALU.mult),
                         R=[tk, "lnrstd"], W=[tk])
                    P.op(act, lambda h: h.activation(out=out_ap(n, b2), in_=t1[:], func=(AF.Silu if silu else AF.Identity),
                                                     scale=g_ap(n), bias=b_ap(n)),
                         R=[tk] + list(gb_keys), W=out_keys(n, b2))

        maskF = sb("maskF", [32, 32], F32)
        maskB = sb("maskB", [32, 32], F32)

        def c_mask(h, mt, pat, cm):
            h.memset(mt[:], 1.0)
            return h.affine_select(out=mt[:], in_=mt[:], pattern=[[pat, 32]], compare_op=ALU.is_ge, fill=0.0,
                                   base=0, channel_multiplier=cm)
        P.op(pool, lambda h: c_mask(h, maskF, 1, -1), W=["maskF"])
        P.op(pool, lambda h: c_mask(h, maskB, -1, 1), W=["maskB"])


        NRB = 3
        ssems = [P.new_sem(f"S{i}") for i in range(NRB)]
        onec = sb("onec", [128, 1], F32)
        P.op(pool, lambda h: h.memset(onec[:], 1.0), W=["onec"])
        TriF = sb("TriF", [128, 128], F32)
        TriB = sb("TriB", [128, 128], F32)

        def c_tri(h, mt, pat, cm):
            h.memset(mt[:], 1.0)
            ins = h.affine_select(out=mt[:], in_=mt[:], pattern=[[pat, 128]], compare_op=ALU.is_ge, fill=0.0,
                                  base=0, channel_multiplier=cm)
            for c in range(4):
                blk_ = mt[:, c * CH:(c + 1) * CH]
                h.affine_select(out=blk_, in_=blk_, pattern=[[0, CH]], compare_op=ALU.is_ge, fill=0.0,
                                base=-c * CH, channel_multiplier=1)
                ins = h.affine_select(out=blk_, in_=blk_, pattern=[[0, CH]], compare_op=ALU.is_ge, fill=0.0,
                                      base=c * CH + CH - 1, channel_multiplier=-1)
            return ins
        P.op(pool, lambda h: c_tri(h, TriF, 1, -1), W=["Tri"])
        P.op(pool, lambda h: c_tri(h, TriB, -1, 1), W=["Tri"])

        for l in range(L):
            if dbg_stop[0]:
                break
            last_layer = (l == L - 1)
            with PScope(P) as s_mix:
                m = sb(f"m{l}", [128, KC, T], BF16, s_mix)
                P.dma(sp, msem(), pvec[:], pvec_d[l], W=["pvec"])

                modS = modSa[:, l, :]
                modD = modDa[:, l, :, :]
                MODK = ["mod", ("modD0", l), ("modD1", l), ("modD2", l), ("modD3", l)]
                sh1 = lambda kc: modSa[:, l, kc:kc + 1]
                g1 = lambda kc: modSa[:, l, 16 + kc:17 + kc]
                sh2 = lambda kc: modSa[:, l, 24 + kc:25 + kc]
                g2 = lambda kc: modSa[:, l, 40 + kc:41 + kc]
                opsc1 = lambda kc: modDa[:, l, 0, kc:kc + 1]
                inv1 = lambda kc: modDa[:, l, 1, kc:kc + 1]
                ainv1 = lambda kc: modDa[:, l, 2, kc:kc + 1]
                opsc2 = lambda kc: modDa[:, l, 3, kc:kc + 1]

                if l == 0:
                    P.op(pool, lambda h: h.memset(lbv[:, 0, :], 0.0), W=["lbv"])
                    P.op(pool, lambda h: h.memset(lbv[:, 1, :], 1.0), W=["lbv"])
                else:
                    P.op(dve, lambda h: h.tensor_tensor(out=lbv[:, 2, :], in0=lbraw[:, 0, :], in1=lbraw[:, 1, :], op=ALU.subtract),
                         R=["lbraw"], W=["lbv2"])
                    P.op(act, lambda h: h.activation(out=lbv[:, 2, :], in_=lbv[:, 2, :], func=AF.Exp), R=["lbv2"], W=["lbv2"])
                    P.op(dve, lambda h: h.tensor_scalar(out=lbv[:, 0, :], in0=lbv[:, 2, :], scalar1=1.0, scalar2=None, op0=ALU.add),
                         R=["lbv2"], W=["lbv"])
                    P.op(dve, lambda h: h.reciprocal(out=lbv[:, 0, :], in_=lbv[:, 0, :]), R=["lbv"], W=["lbv"])
                    P.op(dve, lambda h: h.tensor_tensor(out=lbv[:, 1, :], in0=lbv[:, 2, :], in1=lbv[:, 0, :], op=ALU.mult),
                         R=["lbv", "lbv2"], W=["lbv"])
                lb_ap = lambda hh: lbv[:, 0, hh:hh + 1]
                oml_ap = lambda hh: lbv[:, 1, hh:hh + 1]

                with PScope(P) as s_r:
                    rt32 = [sb(f"rt32_{l}_{i}", [128, HT], F32, s_r) for i in range(2)]
                    rtb = [sb(f"rtb_{l}_{i}", [128, HT], BF16, s_r) for i in range(2)]
                    cnt = 0
                    for kc in range(KC):
                        for hf in range(2):
                            sl = cnt % 2
                            cnt += 1
                            mk = [MK(kc, 2 * hf), MK(kc, 2 * hf + 1)]
                            P.op(act, lambda h: h.activation(out=m[:, kc, hf * HT:(hf + 1) * HT], in_=X[:, kc, hf, :],
                                                             func=AF.Identity, scale=opsc1(kc), bias=sh1(kc)),
                                 R=XK(kc, hf) + MODK, W=mk)
                            P.op(dve, lambda h: h.tensor_scalar(out=rt32[sl][:], in0=m[:, kc, hf * HT:(hf + 1) * HT],
                                                                scalar1=sh1(kc), scalar2=inv1(kc),
                                                                op0=ALU.subtract, op1=ALU.mult),
                                 R=mk + MODK, W=[("rt32", sl)])
                            P.op(dve, lambda h: h.tensor_tensor(out=rtb[sl][:], in0=X[:, kc, hf, :], in1=rt32[sl][:],
                                                                op=ALU.subtract),
                                 R=XK(kc, hf) + [("rt32", sl)], W=[("rtb", sl)])
                            P.op(act, lambda h: h.activation(out=rview(kc, hf), in_=rtb[sl][:], func=AF.Copy),
                                 R=[("rtb", sl)], W=[("X", kc, hf, 0)])

                if debug is not None and debug[0] == "mod" and l == debug[2]:
                    dump([modSa[:, l, :]] + [m[:, kc, :] for kc in range(KC)] + [rview(kc, hf) for kc in range(KC) for hf in range(2)],
                         [MODK] + [[MK(kc, b) for b in range(4)] for kc in range(KC)] +
                         [[("X", kc, hf, 0)] for kc in range(KC) for hf in range(2)])
                    break

                with PScope(P) as s_b:
                    NST = 2
                    ITs = [sb(f"IT{l}_{i}", [128, T], BF16, s_b) for i in range(NST)]
                    SGs = [sb(f"SG{l}_{i}", [128, T], BF16, s_b) for i in range(NST)]
                    OFs = [sb(f"OF{l}_{i}", [128, T], F32, s_b) for i in range(NST)]
                    DEC = sb(f"DEC{l}", [128, 3, 16], F32, s_b)
                    E1 = sb(f"E{l}", [128, BLK], F32, s_b)
                    NS1 = sb(f"NS{l}", [128, BLK], F32, s_b)
                    LFT1 = sb(f"LFT{l}", [128, BLK], F32, s_b)
                    EB1 = sb(f"EB{l}", [128, BLK], F32, s_b)
                    ENB1 = sb(f"ENB{l}", [128, BLK], F32, s_b)
                    GT1 = sb(f"GT{l}", [128, BLK], F32, s_b)
                    FE = sb(f"FE{l}", [128, BLK], F32, s_b)
                    FN = sb(f"FN{l}", [128, BLK], F32, s_b)
                    Qd = [sb(f"Qd{l}_{i}", [128, BLK], BF16, s_b) for i in range(3)]
                    Ktd = [sb(f"Ktd{l}_{i}", [128, BLK], BF16, s_b) for i in range(3)]
                    Khd = [sb(f"Khd{l}_{i}", [128, BLK], BF16, s_b) for i in range(2)]
                    KhTokd = [sb(f"KhTok{l}_{i}", [32, 16, 128], BF16, s_b) for i in range(2)]
                    VTokd = [sb(f"VTok{l}_{i}", [32, 16, 128], BF16, s_b) for i in range(2)]
                    MS = sb(f"MS{l}", [32, 16, 32], BF16, s_b)
                    RS = sb(f"RS{l}", [128, NRB, 5, 128], F32, s_b)
                    SB_ = sb(f"SBr{l}", [128, NRB, 5, 128], BF16, s_b)
                    psT3 = psT[:].rearrange("p (c v) -> p c v", v=128)
                    ps3b = ps[3][:].bitcast(BF16).rearrange("p (c v) -> p c v", v=128)

                    class SR:
                        rb = 0
                        ent = (0, 0)

                    def rb_next():
                        SR.rb = (SR.rb + 1) % NRB
                        return SR.rb

                    def RK(loc):
                        return ("RS", loc[0], loc[1])

                    def BK(loc):
                        return ("SBr", loc[0], loc[1])

                    def sweep_start(dirn, hh):
                        rb = rb_next()
                        P.dma(sp, ssems[rb], RS[:, rb, 0, :], s0_d[l, dirn, hh], W=[RK((rb, 0))])
                        P.op(act, lambda h: h.activation(out=SB_[:, rb, 0, :], in_=RS[:, rb, 0, :], func=AF.Copy),
                             R=[RK((rb, 0))], W=[BK((rb, 0))])
                        SR.ent = (rb, 0)

                    def emit_state(dirn, seg, hh):
                        e = SR.ent
                        P.dma(sp, ssems[e[0]], st_d[l, dirn, seg, hh], RS[:, e[0], e[1], :], R=[RK(e)], is_output=True)

                    tok_cnt = [0]

                    def run(*gens):
                        gens = [g for g in gens if g is not None]
                        while gens:
                            for g in list(gens):
                                try:
                                    next(g)
                                except StopIteration:
                                    gens.remove(g)
                            if PE_KEEPWARM:
                                def warm(h):
                                    ins = None
                                    for _ in range(PE_KEEPWARM):
                                        ins = h.matmul(ps[1][:], lhsT=m[:, 0, 0:128], rhs=m[:, 1, 0:512], start=True, stop=True)
                                    return ins
                                P.op(pe, warm, R=[MK(0, 0), MK(1, 0)], W=[PSK[1]])

                    def to_tok(src_fn, src_keys, dst, dst_key):
                        for rd in range(2):
                            pv, pk = (psT3, PSTK) if rd == 0 else (ps3b, PSK[3])

                            def tr(h):
                                ins = None
                                for q in range(8):
                                    ins = h.transpose(pv[0:32, q, :], src_fn(rd * 8 + q), ident_b[:])
                                return ins
                            P.op(pe, tr, R=src_keys + ["ident_b"], W=[pk])
                            yield
                            tok_cnt[0] += 1
                            if tok_cnt[0] % 2:
                                P.op(act, lambda h: h.activation(out=dst[:, rd * 8:(rd + 1) * 8, :], in_=pv[0:32, :, :],
                                                                 func=AF.Copy), W=[pk, dst_key + (rd,)])
                            else:
                                P.op(dve, lambda h: h.tensor_copy(out=dst[:, rd * 8:(rd + 1) * 8, :], in_=pv[0:32, :, :]),
                                     W=[pk, dst_key + (rd,)])
                            yield

                    def scan_block(dirn, hh, bi, k, kt_fn, q_fn, kq_keys, maskX, mask_key):
                        order = list(range(16)) if dirn == 0 else list(range(15, -1, -1))
                        tp_ = k % 2
                        KhTok, VTok = KhTokd[tp_], VTokd[tp_]
                        TOKK = [("KhTok", tp_, 0), ("KhTok", tp_, 1), ("VTok", tp_, 0), ("VTok", tp_, 1)]
                        obank, okey = ps[4], PSK[4]
                        dk = ("DEC", k % 3)

                        def scores(h):
                            ins = None
                            for cc in range(16):
                                ins = h.matmul(ps[3][0:32, cc * 32:(cc + 1) * 32], lhsT=kt_fn(cc), rhs=q_fn(cc),
                                               start=True, stop=True)
                            return ins
                        P.op(pe, scores, R=kq_keys, W=[PSK[3]])
                        yield
                        P.op(dve, lambda h: h.tensor_tensor(
                            out=MS[:], in0=ps[3][0:32, :].rearrange("p (c t) -> p c t", t=32),
                            in1=maskX[:].unsqueeze(1).to_broadcast([32, 16, 32]), op=ALU.mult),
                            R=[mask_key], W=[PSK[3], "MS"])
                        yield

                        def intra(h):
                            ins = None
                            for cc in range(16):
                                ins = h.matmul(obank[:, cc * 32:(cc + 1) * 32], lhsT=VTok[:, cc, :], rhs=MS[:, cc, :],
                                               start=(cc == 0), stop=False, skip_group_check=True)
                            return ins
                        P.op(pe, intra, R=["MS"] + TOKK[2:], W=[okey])
                        yield

                        def emit_dsm(rd):
                            chunks = order[4 * rd:4 * rd + 4]
                            bank = ps[5 + rd % 2]

                            def dsm(h):
                                ins = None
                                for j, cc in enumerate(chunks):
                                    ins = h.matmul(bank[:, j * 128:(j + 1) * 128], lhsT=KhTok[:, cc, :], rhs=VTok[:, cc, :],
                                                   start=True, stop=True)
                                return ins
                            P.op(pe, dsm, R=TOKK, W=[PSK[5 + rd % 2]])
                        emit_dsm(0)
                        yield
                        for rd in range(4):
                            chunks = order[4 * rd:4 * rd + 4]
                            bank = ps[5 + rd % 2]
                            bkey = PSK[5 + rd % 2]
                            c0 = 16 * bi + chunks[0]
                            if dirn == 0 and c0 % 8 == 0 and c0 > 0:
                                seg_done = c0 // 8 - 1
                            elif dirn == 1 and c0 % 8 == 7 and c0 < NCH - 1:
                                seg_done = (c0 + 1) // 8
                            else:
                                seg_done = None
                            rb = rb_next()
                            if seg_done is not None:
                                emit_state(dirn, seg_done, hh)
                                e = SR.ent
                                P.op(dve, lambda h: h.tensor_scalar(out=RS[:, rb, 0, :], in0=RS[:, e[0], e[1], :],
                                                                    scalar1=carry, scalar2=None, op0=ALU.mult),
                                     R=[RK(e), "flags"], W=[RK((rb, 0))])
                                P.op(act, lambda h: h.activation(out=SB_[:, rb, 0, :], in_=RS[:, rb, 0, :], func=AF.Copy),
                                     R=[RK((rb, 0))], W=[BK((rb, 0))])
                                SR.ent = (rb, 0)
                                yield
                            ents = []
                            for j, cc in enumerate(chunks):
                                e = SR.ent
                                ents.append(e)
                                P.op(dve, lambda h: h.scalar_tensor_tensor(
                                    out=RS[:, rb, j + 1, :], in0=RS[:, e[0], e[1], :], scalar=DEC[:, k % 3, cc:cc + 1],
                                    in1=bank[:, j * 128:(j + 1) * 128], op0=ALU.mult, op1=ALU.add),
                                    R=[RK(e), dk], W=[bkey, RK((rb, j + 1))])
                                SR.ent = (rb, j + 1)
                            yield
                            if rd < 3:
                                emit_dsm(rd + 1)
                            P.op(act, lambda h: h.activation(out=SB_[:, rb, 1:5, :], in_=RS[:, rb, 1:5, :], func=AF.Copy),
                                 R=[RK((rb, s_)) for s_ in range(1, 5)], W=[BK((rb, s_)) for s_ in range(1, 5)])
                            yield

                            def inter(h, chunks=chunks, ents=ents):
                                ins = None
                                for j, cc in enumerate(chunks):
                                    e = ents[j]
                                    ins = h.matmul(obank[:, cc * 32:(cc + 1) * 32], lhsT=SB_[:, e[0], e[1], :], rhs=q_fn(cc),
                                                   start=False, stop=True, skip_group_check=True)
                                return ins
                            P.op(pe, inter, R=[BK(e) for e in ents] + kq_keys, W=[okey])
                            yield

                    items = []
                    for hh in range(NH):
                        for bi in range(NBLK):
                            items.append((hh, 0, bi))
                        for bi in range(NBLK - 1, -1, -1):
                            items.append((hh, 1, bi))
                    wts = {}

                    def head_weights(hh):
                        if hh not in wts:
                            wts[hh] = [ws_next((KC, 128)), ws_next((KC, 256)), ws_next((KC, 256))]
                        return wts[hh]

                    def G_stage(k):
                        if k - 3 >= 0:
                            yield from Fin_stage(k - 3)
                        hh, dirn, bi = items[k]
                        (wq_, wqk, wqu), (wz_, wzk, wzu), (wi_, wik, wiu) = head_weights(hh)
                        st_ = hh % NST
                        IT, SG = ITs[st_], SGs[st_]
                        p3, par = k % 3, k % 2
                        MB = [MK(kc, bi) for kc in range(KC)]
                        sl = slice(bi * BLK, (bi + 1) * BLK)

                        def proj(bank, wv, c0):
                            def f(h):
                                ins = None
                                for kc in range(KC):
                                    ins = h.matmul(bank[:], lhsT=wv[:, kc, c0:c0 + 128], rhs=m[:, kc, bi * BLK:(bi + 1) * BLK],
                                                   start=(kc == 0), stop=(kc == KC - 1))
                                return ins
                            return f
                        P.op(pe, proj(ps[0], wz_, 128 * dirn), R=[wzk] + MB, W=[PSK[0]])
                        yield
                        P.op(act, lambda h: h.activation(out=E1[:], in_=ps[0][:], func=AF.Exp), W=[PSK[0], "E1"])
                        P.op(pe, proj(ps[2], wq_, 0), R=[wqk] + MB, W=[PSK[2]])
                        yield
                        P.op(act, lambda h: h.activation(out=NS1[:], in_=E1[:], func=AF.Ln, bias=onec[:, 0:1]),
                             R=["E1", "onec"], W=["NS1"])
                        yield
                        if l == 0:
                            P.op(dve, lambda h: h.tensor_tensor(out=E1[:], in0=ps[0][:], in1=NS1[:], op=ALU.subtract),
                                 R=["NS1", "E1"], W=[PSK[0], "E1"])
                            P.op(act, lambda h: h.activation(out=NS1[:], in_=NS1[:], func=AF.Exp, scale=-1.0),
                                 R=["NS1"], W=["NS1"])
                            yield
                        else:
                            P.op(act, lambda h: h.activation(out=NS1[:], in_=NS1[:], func=AF.Exp, scale=-1.0),
                                 R=["NS1"], W=["NS1"])
                            yield
                            P.op(dve, lambda h: h.tensor_tensor(out=E1[:], in0=E1[:], in1=NS1[:], op=ALU.mult),
                                 R=["E1", "NS1"], W=["E1"])
                            yield
                            P.op(act, lambda h: h.activation(out=NS1[:], in_=NS1[:], func=AF.Identity, scale=oml_ap(hh)),
                                 R=["NS1", "lbv"], W=["NS1"])
                            P.op(act, lambda h: h.activation(out=E1[:], in_=E1[:], func=AF.Ln, scale=oml_ap(hh),
                                                             bias=lb_ap(hh)), R=["E1", "lbv"], W=["E1"])
                            yield
                        def trl(h):
                            ins = None
                            for tl in range(4):
                                ins = h.transpose(ps[0][:, tl * 128:(tl + 1) * 128], E1[:, tl * 128:(tl + 1) * 128], ident_f[:])
                            return ins
                        P.op(pe, trl, R=["E1", "ident_f"], W=[PSK[0]])
                        yield
                        P.op(act, lambda h: h.activation(out=LFT1[:], in_=ps[0][:], func=AF.Copy), W=[PSK[0], "LFT1"])
                        yield

                        def cum(h):
                            ins = None
                            tri = TriF if dirn == 0 else TriB
                            for tl in range(4):
                                ins = h.matmul(ps[0][:, tl * 128:(tl + 1) * 128], lhsT=LFT1[:, tl * 128:(tl + 1) * 128],
                                               rhs=tri[:], start=True, stop=True)
                            return ins
                        P.op(pe, cum, R=["LFT1", "Tri"], W=[PSK[0]])
                        yield
                        P.op(act, lambda h: h.activation(out=EB1[:], in_=ps[0][:], func=AF.Exp), W=[PSK[0], "EB1"])
                        P.op(act, lambda h: h.activation(out=ENB1[:], in_=ps[0][:], func=AF.Exp, scale=-1.0), W=[PSK[0], "ENB1"])
                        yield
                        eb3 = EB1[:].rearrange("p (c j) -> p c j", j=CH)
                        ecol = (CH - 1) if dirn == 0 else 0
                        P.op(act, lambda h: h.activation(out=DEC[:, p3, :], in_=eb3[:, :, ecol], func=AF.Copy),
                             R=["EB1"], W=[("DEC", p3)])
                        P.op(dve, lambda h: h.tensor_tensor(out=Qd[p3][:], in0=ps[2][:], in1=EB1[:], op=ALU.mult),
                             R=["EB1"], W=[PSK[2], ("Qd", p3)])
                        yield
                        P.op(dve, lambda h: h.tensor_tensor(out=ENB1[:], in0=ENB1[:], in1=NS1[:], op=ALU.mult),
                             R=["ENB1", "NS1"], W=["ENB1"])
                        if dirn == 0:
                            P.op(pe, proj(ps[2], wi_, 0), R=[wik] + MB, W=[PSK[2]])
                        yield
                        P.op(act, lambda h: h.activation(out=Ktd[p3][:], in_=ENB1[:], func=AF.Copy), R=["ENB1"], W=[("Ktd", p3)])
                        yield
                        P.op(dve, lambda h: h.tensor_tensor(
                            out=Khd[par][:].rearrange("p (c j) -> p c j", j=CH),
                            in0=ENB1[:].rearrange("p (c j) -> p c j", j=CH),
                            in1=eb3[:, :, ecol:ecol + 1].to_broadcast([128, 16, CH]), op=ALU.mult),
                            R=["ENB1", "EB1"], W=[("Khd", par)])
                        yield
                        if dirn == 0:
                            P.op(act, lambda h: h.activation(out=IT[:, sl], in_=ps[2][:], func=AF.Copy), W=[PSK[2], ("IT", st_, bi)])
                            P.op(pe, proj(ps[2], wi_, 128), R=[wik] + MB, W=[PSK[2]])
                            yield
                            P.op(act, lambda h: h.activation(out=GT1[:], in_=ps[2][:], func=AF.Exp, scale=-1.0), W=[PSK[2], "GT1"])
                            yield
                            P.op(act, lambda h: h.activation(out=GT1[:], in_=GT1[:], func=AF.Ln, bias=onec[:, 0:1]),
                                 R=["GT1", "onec"], W=["GT1"])
                            yield
                            P.op(act, lambda h: h.activation(out=GT1[:], in_=GT1[:], func=AF.Exp, scale=-1.0), R=["GT1"], W=["GT1"])
                            yield
                            P.op(dve, lambda h: h.tensor_tensor(out=SG[:, sl], in0=ps[2][:], in1=GT1[:], op=ALU.mult),
                                 R=["GT1"], W=[PSK[2], ("SG", st_, bi)])
                            yield
                        if dirn == 0 and bi == NBLK - 1:
                            ws_done(wiu)
                        if dirn == 1 and bi == 0:
                            ws_done(wqu)
                            ws_done(wzu)

                    def T_stage(k):
                        hh, dirn, bi = items[k]
                        st_ = hh % NST
                        IT = ITs[st_]
                        par = k % 2
                        yield from to_tok(lambda cc: Khd[par][:, cc * CH:(cc + 1) * CH], [("Khd", par)], KhTokd[par], ("KhTok", par))
                        yield from to_tok(lambda cc: IT[:, bi * BLK + cc * CH: bi * BLK + (cc + 1) * CH], [("IT", st_, bi)],
                                          VTokd[par], ("VTok", par))

                    def S_stage(k):
                        hh, dirn, bi = items[k]
                        st_ = hh % NST
                        p3 = k % 3
                        sl = slice(bi * BLK, (bi + 1) * BLK)
                        if (dirn == 0 and bi == 0) or (dirn == 1 and bi == NBLK - 1):
                            sweep_start(dirn, hh)
                        yield from scan_block(dirn, hh, bi, k, lambda cc: Ktd[p3][:, cc * CH:(cc + 1) * CH],
                                              lambda cc: Qd[p3][:, cc * CH:(cc + 1) * CH],
                                              [("Ktd", p3), ("Qd", p3)], maskF if dirn == 0 else maskB,
                                              "maskF" if dirn == 0 else "maskB")
                        if dirn == 0:
                            P.op(act, lambda h: h.activation(out=OFs[st_][:, sl], in_=ps[4][:], func=AF.Copy),
                                 W=[PSK[4], ("OF", st_, bi)])
                            yield
                            if bi == NBLK - 1:
                                emit_state(0, NSEG - 1, hh)
                        else:
                            P.op(dve, lambda h: h.tensor_tensor(out=FE[:], in0=ps[4][:], in1=OFs[st_][:, sl], op=ALU.add),
                                 R=[("OF", st_, bi)], W=[PSK[4], "FE"])
                            yield
                            if bi == 0:
                                emit_state(1, 0, hh)

                    def Fin_stage(k):
                        hh, dirn, bi = items[k]
                        if dirn == 0:
                            return
                        yield
                        st_ = hh % NST
                        sl = slice(bi * BLK, (bi + 1) * BLK)
                        hf, b2 = divmod(bi, 2)
                        P.op(act, lambda h: h.activation(out=FN[:], in_=FE[:], func=AF.Square), R=["FE"], W=["FN"])
                        yield
                        P.op(pe, lambda h: h.matmul(ps[2][:], lhsT=ones_f[:], rhs=FN[:], start=True, stop=True),
                             R=["ones_f", "FN"], W=[PSK[2]])
                        yield
                        P.op(act, lambda h: h.activation(out=FN[:], in_=ps[2][:], func=AF.Ln, scale=1.0 / 128, bias=epsc[:, 0:1]),
                             R=["epsc"], W=[PSK[2], "FN"])
                        yield
                        P.op(act, lambda h: h.activation(out=FN[:], in_=FN[:], func=AF.Exp, scale=-0.5), R=["FN"], W=["FN"])
                        yield
                        P.op(dve, lambda h: h.tensor_tensor(out=FE[:], in0=FE[:], in1=FN[:], op=ALU.mult), R=["FE", "FN"], W=["FE"])
                        yield
                        P.op(dve, lambda h: h.scalar_tensor_tensor(
                            out=bview(hh, hf)[:, b2 * BLK:(b2 + 1) * BLK], in0=FE[:], scalar=pvec[:, 3, hh:hh + 1],
                            in1=SGs[st_][:, sl], op0=ALU.mult, op1=ALU.mult),
                            R=["FE", "pvec", ("SG", st_, bi)], W=[("X", hh, hf, 1)])
                        yield

                    NI = len(items)
                    stage = lambda fn, k: fn(k) if 0 <= k < NI else None
                    for step in range(-2, NI + 1):
                        run(stage(S_stage, step), stage(T_stage, step + 1), stage(G_stage, step + 2),
                            stage(Fin_stage, step - 1) if step + 2 >= NI else None)

                if debug is not None and debug[0] == "hgrn" and l == debug[2]:
                    dump([bview(kc, hf) for kc in range(KC) for hf in range(2)],
                         [[("X", kc, hf, 1)] for kc in range(KC) for hf in range(2)])
                    break
                with PScope(P) as s_h:
                    z = sb(f"z{l}", [128, KC, HT], F32, s_h)
                    actb = sb(f"actb{l}", [128, KC, HT], BF16, s_h)
                    sgt = [sb(f"sgt{l}_{i}", [128, BLK], F32, s_h) for i in range(2)]
                    tmp2 = sb(f"tmp2{l}", [128, BLK], F32, s_h)
                    actflat = actb[:].rearrange("p k t -> p (k t)")
                    VN = lambda tile: actflat[:, tile * 1024:(tile + 1) * 1024]
                    cact = actflat.rearrange("p (tile g c) -> p g tile c", tile=8, g=8)
                    CK_all = [("act", tile, g) for tile in range(8) for g in range(8)]
                    AK_all = [("acta", j, b2) for j in range(KC) for b2 in range(2)]
                    ZBK_all = [("zb", kc, b2) for kc in range(KC) for b2 in range(2)]
                    ycnt = [0]

                    def yproj(hf, br, rhs_fn, rhs_keys, first):
                        for u in range(4):
                            wx, wxk, wxu = ws_next((KC, 256))
                            wg, wgk, wgu = ws_next((KC, 256))
                            for nn in range(2):
                                n = 2 * u + nn
                                for b2 in range(2):
                                    blk = 2 * hf + b2
                                    pp = ycnt[0] % 2
                                    ycnt[0] += 1
                                    py, pg = ps[2 * pp], ps[2 * pp + 1]
                                    pyk, pgk = PSK[2 * pp], PSK[2 * pp + 1]

                                    def fy(h):
                                        ins = None
                                        for kc in range(KC):
                                            ins = h.matmul(py[:], lhsT=wx[:, kc, nn * 128:(nn + 1) * 128], rhs=rhs_fn(kc, b2),
                                                           start=(kc == 0), stop=(kc == KC - 1))
                                        return ins
                                    P.op(pe, fy, R=[wxk] + [k for kc in range(KC) for k in rhs_keys(kc, b2)], W=[pyk])

                                    def fg(h):
                                        ins = None
                                        for kc in range(KC):
                                            ins = h.matmul(pg[:], lhsT=wg[:, kc, nn * 128:(nn + 1) * 128],
                                                           rhs=m[:, kc, blk * BLK:(blk + 1) * BLK],
                                                           start=(kc == 0), stop=(kc == KC - 1))
                                        return ins
                                    P.op(pe, fg, R=[wgk] + [MK(kc, blk) for kc in range(KC)], W=[pgk])
                                    P.op(act, lambda h: h.activation(out=sgt[pp][:], in_=pg[:], func=AF.Sigmoid),
                                         W=[pgk, ("sgt", pp)])
                                    zsl = z[:, n, b2 * BLK:(b2 + 1) * BLK]
                                    if first:
                                        P.op(dve, lambda h: h.tensor_tensor(out=zsl, in0=py[:], in1=sgt[pp][:], op=ALU.mult),
                                             R=[("sgt", pp)], W=[pyk, ("z", n, b2)])
                                    else:
                                        P.op(dve, lambda h: h.tensor_tensor(out=tmp2[:], in0=py[:], in1=sgt[pp][:], op=ALU.mult),
                                             R=[("sgt", pp)], W=[pyk, "tmp2"])
                                        P.op(pool, lambda h: h.tensor_tensor(out=zsl, in0=zsl, in1=tmp2[:], op=ALU.add),
                                             R=["tmp2", ("z", n, b2)], W=[("z", n, b2)])
                            ws_done(wxu)
                            ws_done(wgu)

                    for hf in range(2):
                        h0 = hf * HT
                        yproj(hf, 1, lambda kc, b2: bview(kc, hf)[:, b2 * BLK:(b2 + 1) * BLK],
                              lambda kc, b2: [("X", kc, hf, 1)], True)
                        if debug is not None and debug[0] == "yb" and l == debug[2] and hf == debug[3]:
                            dump([z[:, n, :] for n in range(KC)], [[("z", n, 0), ("z", n, 1)] for n in range(KC)])
                            break

                        alias_sync(CK_all, ZBK_all + AK_all)
                        with PScope(P) as s_c:
                            gam_bc = sb(f"gam{l}{hf}", [128, D], F32, s_c)
                            BIAS = sb(f"BIAS{l}{hf}", [128, NH, 128], F32, s_c)
                            wTb = sb(f"wTb{l}{hf}", [128, NH, 128], BF16, s_c)
                            ones_b = sb(f"onesb{l}{hf}", [128, 128], BF16, s_c)
                            stc = sb(f"stc{l}{hf}", [128, 2, 8], F32, s_c)
                            ctmp2 = sb(f"ctmp2{l}{hf}", [128, BLK], F32, s_c)
                            s_bs = ExitStack()
                            bs_bc = sb(f"bsbc{l}{hf}", [128, NH, 128], F32, s_bs)
                            P.dma(sp, msem(), gam_bc[:], sgug_d[l:l + 1, :].to_broadcast([128, D]), W=["gam_bc"])
                            P.dma(sp, msem(), bs_bc[:].rearrange("p g t -> p (g t)"),
                                  sgubias_d[l:l + 1, :].to_broadcast([128, NH * 128]), W=["bs_bc"])
                            P.dma(pool, msem(), wTb[:], sguw_d[l], W=["wTb"])
                            P.op(pool, lambda h: h.memset(ones_b[:], 1.0), W=["ones_b"])
                            for gq in range(2):
                                def rs(h):
                                    ins = None
                                    for q in range(4):
                                        ins = h.matmul(ps[gq][:, q * 128:(q + 1) * 128], lhsT=ones_b[:], rhs=wTb[:, gq * 4 + q, :],
                                                       start=True, stop=True)
                                    return ins
                                P.op(pe, rs, R=["ones_b", "wTb"], W=[PSK[gq]])
                                for q in range(4):
                                    g = gq * 4 + q
                                    P.op(dve, lambda h: h.scalar_tensor_tensor(
                                        out=BIAS[:, g, :], in0=ps[gq][:, q * 128:(q + 1) * 128], scalar=pvec[:, 8, g:g + 1],
                                        in1=bs_bc[:, g, :], op0=ALU.mult, op1=ALU.add),
                                        R=["pvec", "bs_bc"], W=[PSK[gq], "BIAS"])
                            s_bs.close()
                            P.barrier()
                            vn32 = sb(f"vn32{l}{hf}", [128, D], F32, s_c)
                            wvs = [ws_next((KC, 256)) for _ in range(4)]
                            for ti in range(8):
                                t0 = h0 + ti * 128
                                pb0 = 2 * (ti % 2)
                                for q, (wv, wvk, _u) in enumerate(wvs):
                                    def fv(h, wv=wv, q=q):
                                        ins = None
                                        for kc in range(KC):
                                            ins = h.matmul(ps[pb0 + q // 2][:, (q % 2) * 256:(q % 2 + 1) * 256], lhsT=m[:, kc, t0:t0 + 128],
                                                           rhs=wv[:, kc, :], start=(kc == 0), stop=(kc == KC - 1))
                                        return ins
                                    P.op(pe, fv, R=[wvk] + [MK(kc, t0 // BLK) for kc in range(KC)], W=[PSK[pb0 + q // 2]])

                                def stats(h):
                                    h.bn_stats(out=stc[:, 0, 0:6], in_=ps[pb0][:])
                                    return h.bn_stats(out=stc[:, 1, 0:6], in_=ps[pb0 + 1][:])
                                P.op(dve, stats, W=[PSK[pb0], PSK[pb0 + 1], "stc"])
                                P.op(dve, lambda h: h.bn_aggr(out=stc[:, 0, 6:8], in_=stc[:, 0:2, 0:6]), R=["stc"], W=["cmv"])
                                P.op(act, lambda h: h.activation(out=stc[:, 1, 6:7], in_=stc[:, 0, 7:8], func=AF.Ln, bias=epsc[:, 0:1]),
                                     R=["cmv", "epsc"], W=["crstd"])
                                P.op(act, lambda h: h.activation(out=stc[:, 1, 6:7], in_=stc[:, 1, 6:7], func=AF.Exp, scale=-0.5),
                                     R=["crstd"], W=["crstd"])
                                P.op(dve, lambda h: h.tensor_scalar(out=stc[:, 1, 7:8], in0=stc[:, 0, 6:7], scalar1=stc[:, 1, 6:7],
                                                                    scalar2=-1.0, op0=ALU.mult, op1=ALU.mult),
                                     R=["cmv", "crstd"], W=["cnmr"])
                                for q in range(2):
                                    P.op(act, lambda h: h.activation(out=vn32[:, q * 512:(q + 1) * 512], in_=ps[pb0 + q][:], func=AF.Identity,
                                                                     scale=stc[:, 1, 6:7], bias=stc[:, 1, 7:8]),
                                         R=["crstd", "cnmr"], W=[PSK[pb0 + q], "vn32"])
                                P.op(dve, lambda h: h.tensor_tensor(out=VN(ti), in0=vn32[:], in1=gam_bc[:], op=ALU.mult),
                                     R=["vn32", "gam_bc"], W=[("act", ti, g) for g in range(8)])
                            for _wv, _wk, _u in wvs:
                                ws_done(_u)
                            for uq in range(4):
                                wu_, wuk, wuu = ws_next((KC, 256))
                                for gg in range(2):
                                    g = uq * 2 + gg
                                    for b2 in range(2):
                                        blk = 2 * hf + b2

                                        def fu(h):
                                            ins = None
                                            for kc in range(KC):
                                                ins = h.matmul(ps[2][:], lhsT=wu_[:, kc, gg * 128:(gg + 1) * 128],
                                                               rhs=m[:, kc, blk * BLK:(blk + 1) * BLK],
                                                               start=(kc == 0), stop=(kc == KC - 1))
                                            return ins
                                        P.op(pe, fu, R=[wuk] + [MK(kc, blk) for kc in range(KC)], W=[PSK[2]])

                                        def fm_(h):
                                            ins = None
                                            for tl in range(4):
                                                ti = b2 * 4 + tl
                                                ins = h.matmul(ps[3][:, tl * 128:(tl + 1) * 128], lhsT=VN(ti)[:, g * 128:(g + 1) * 128],
                                                               rhs=wTb[:, g, :], start=True, stop=True)
                                            return ins
                                        ck = [("act", b2 * 4 + tl, g) for tl in range(4)]
                                        P.op(pe, fm_, R=ck + ["wTb"], W=[PSK[3]])
                                        P.op(dve, lambda h: h.tensor_tensor(
                                            out=ctmp2[:].rearrange("p (a t) -> p a t", t=128),
                                            in0=ps[3][:].rearrange("p (a t) -> p a t", t=128),
                                            in1=BIAS[:, g, :].unsqueeze(1).to_broadcast([128, 4, 128]), op=ALU.add),
                                            R=["BIAS"], W=[PSK[3], "ctmp2"])
                                        P.op(dve, lambda h: h.tensor_tensor(
                                            out=cact[:, g, b2 * 4:(b2 + 1) * 4, :],
                                            in0=ps[2][:].rearrange("p (a t) -> p a t", t=128),
                                            in1=ctmp2[:].rearrange("p (a t) -> p a t", t=128), op=ALU.mult),
                                            R=["ctmp2"], W=[PSK[2]] + ck)
                                ws_done(wuu)
                        if debug is not None and debug[0] == "cact" and l == debug[2] and hf == debug[3]:
                            dump([actb[:, kc, :] for kc in range(KC)], [CK_all for kc in range(KC)])
                            break
                        yproj(hf, 2, lambda kc, b2: cact[:, kc, b2 * 4:(b2 + 1) * 4, :],
                              lambda kc, b2: [("act", b2 * 4 + tl, kc) for tl in range(4)], False)

                        alias_sync(AK_all, CK_all)
                        with PScope(P) as s_a:
                            hp = [sb(f"hp{l}{hf}{i}", [128, 4, SEGP], BF16, s_a) for i in range(2)]
                            DG = sb(f"DG{l}{hf}", [128, CONV_K, 128], BF16, s_a)
                            convw = sb(f"convw{l}{hf}", [128, KC, CONV_K], F32, s_a)
                            P.dma(sp, msem(), convw[:], convw_d[l], W=["convw"])
                            co32 = sb(f"co32{l}{hf}", [128, BLK], F32, s_a)
                            sq32 = sb(f"sq32{l}{hf}", [128, BLK], F32, s_a)
                            cmean = [sb(f"cmean{l}{hf}{i}", [128, BLK], F32, s_a) for i in range(2)]
                            crstd = [sb(f"crstd{l}{hf}{i}", [128, BLK], F32, s_a) for i in range(2)]
                            for u in range(8):
                                wa, wak, wau = ws_next((KC, 256))
                                for jj in range(1):
                                    j = u
                                    hs = j % 2
                                    hpk = ("hp", hs)
                                    hpt = hp[hs]
                                    if hf == 0:
                                        P.op(pool, lambda h: h.memset(hpt[:, 0, 0:HALO], 0.0), W=[hpk])
                                    else:
                                        P.op(pool, lambda h: h.memset(hpt[:, 3, SEG + HALO:SEGP], 0.0), W=[hpk])

                                    def vg(tok0, ntok, pv, pg):
                                        def f(h):
                                            ins = None
                                            for kc in range(KC):
                                                h.matmul(pv[:, 0:ntok], lhsT=wa[:, kc, jj * 128:(jj + 1) * 128],
                                                         rhs=m[:, kc, tok0:tok0 + ntok], start=(kc == 0), stop=(kc == KC - 1))
                                            for kc in range(KC):
                                                ins = h.matmul(pg[:, 0:ntok], lhsT=wa[:, kc, 128 + jj * 128:128 + (jj + 1) * 128],
                                                               rhs=m[:, kc, tok0:tok0 + ntok], start=(kc == 0), stop=(kc == KC - 1))
                                            return ins
                                        return f

                                    def halo(dst, pv, c0):
                                        P.op(dve, lambda h: h.scalar_tensor_tensor(out=dst, in0=pv[:, c0:c0 + HALO], scalar=carry,
                                                                                   in1=sgt[0][:, c0:c0 + HALO],
                                                                                   op0=ALU.mult, op1=ALU.mult),
                                             R=[("sgt", 0), "flags"], W=[PSK[0], hpk])

                                    for b2 in range(2):
                                        blk = 2 * hf + b2
                                        P.op(pe, vg(blk * BLK, BLK, ps[0], ps[1]), R=[wak] + [MK(kc, blk) for kc in range(KC)],
                                             W=[PSK[0], PSK[1]])
                                        P.op(act, lambda h: h.activation(out=sgt[0][:], in_=ps[1][:], func=AF.Sigmoid),
                                             W=[PSK[1], ("sgt", 0)])
                                        P.op(dve, lambda h: h.tensor_tensor(
                                            out=hpt[:, 2 * b2:2 * b2 + 2, HALO:HALO + SEG],
                                            in0=ps[0][:].rearrange("p (s t) -> p s t", t=SEG),
                                            in1=sgt[0][:].rearrange("p (s t) -> p s t", t=SEG), op=ALU.mult),
                                            R=[("sgt", 0)], W=[PSK[0], hpk])
                                        halo(hpt[:, 2 * b2 + 1, 0:HALO], ps[0], SEG - HALO)
                                        halo(hpt[:, 2 * b2, SEG + HALO:SEGP], ps[0], SEG)
                                        if b2 == 0:
                                            halo(hpt[:, 2, 0:HALO], ps[0], BLK - HALO)
                                        else:
                                            halo(hpt[:, 1, SEG + HALO:SEGP], ps[0], 0)
                                    if hf == 0:
                                        P.op(pe, vg(HT, HALO, ps[0], ps[1]), R=[wak] + [MK(kc, 2) for kc in range(KC)],
                                             W=[PSK[0], PSK[1]])
                                        P.op(act, lambda h: h.activation(out=sgt[0][:, 0:HALO], in_=ps[1][:, 0:HALO], func=AF.Sigmoid),
                                             W=[PSK[1], ("sgt", 0)])
                                        halo(hpt[:, 3, SEG + HALO:SEGP], ps[0], 0)
                                    else:
                                        P.op(pe, vg(HT - HALO, HALO, ps[0], ps[1]), R=[wak] + [MK(kc, 1) for kc in range(KC)],
                                             W=[PSK[0], PSK[1]])
                                        P.op(act, lambda h: h.activation(out=sgt[0][:, 0:HALO], in_=ps[1][:, 0:HALO], func=AF.Sigmoid),
                                             W=[PSK[1], ("sgt", 0)])
                                        halo(hpt[:, 0, 0:HALO], ps[0], 0)
                                    P.op(pool, lambda h: h.tensor_tensor(
                                        out=DG[:], in0=ident_b[:].unsqueeze(1).to_broadcast([128, CONV_K, 128]),
                                        in1=convw[:, j, :].unsqueeze(2).to_broadcast([128, CONV_K, 128]), op=ALU.mult),
                                        R=["ident_b", "convw"], W=["DG"])
                                    for b2 in range(2):
                                        def cv(h):
                                            ins = None
                                            for k in range(CONV_K):
                                                ins = h.matmul(ps[2][:].rearrange("p (s t) -> p s t", t=SEG), lhsT=DG[:, k, :],
                                                               rhs=hpt[:, 2 * b2:2 * b2 + 2, k:k + SEG],
                                                               start=(k == 0), stop=(k == CONV_K - 1))
                                            return ins
                                        P.op(pe, cv, R=["DG", hpk], W=[PSK[2]])
                                        P.op(act, lambda h: h.activation(out=co32[:], in_=ps[2][:], func=AF.Identity,
                                                                         bias=pvec[:, 0, j:j + 1], scale=1.0),
                                             R=["pvec"], W=[PSK[2], "co32"])
                                        P.op(pool, lambda h: h.tensor_copy(out=actb[:, j, b2 * BLK:(b2 + 1) * BLK], in_=co32[:]),
                                             R=["co32"], W=[("acta", j, b2)])
                                        P.op(dve, lambda h: h.tensor_tensor(out=sq32[:], in0=co32[:], in1=co32[:], op=ALU.mult),
                                             R=["co32"], W=["sq32"])
                                        P.op(pe, lambda h: h.matmul(ps[3 + b2][:], lhsT=ones_f[:], rhs=co32[:], start=(j == 0),
                                                                    stop=(j == KC - 1)), R=["ones_f", "co32"], W=[PSK[3 + b2]])
                                        P.op(pe, lambda h: h.matmul(ps[5 + b2][:], lhsT=ones_f[:], rhs=sq32[:], start=(j == 0),
                                                                    stop=(j == KC - 1)), R=["ones_f", "sq32"], W=[PSK[5 + b2]])
                                ws_done(wau)
                            for b2 in range(2):
                                P.op(act, lambda h: h.activation(out=cmean[b2][:], in_=ps[3 + b2][:], func=AF.Identity, scale=1.0 / D),
                                     W=[PSK[3 + b2], ("cmean", b2)])
                                P.op(dve, lambda h: h.tensor_tensor(out=sq32[:], in0=cmean[b2][:], in1=cmean[b2][:], op=ALU.mult),
                                     R=[("cmean", b2)], W=["sq32"])
                                P.op(dve, lambda h: h.scalar_tensor_tensor(out=crstd[b2][:], in0=ps[5 + b2][:], scalar=1.0 / D,
                                                                           in1=sq32[:], op0=ALU.mult, op1=ALU.subtract),
                                     R=["sq32"], W=[PSK[5 + b2], ("crstd", b2)])
                                P.op(act, lambda h: h.activation(out=crstd[b2][:], in_=crstd[b2][:], func=AF.Ln, bias=epsc[:, 0:1]),
                                     R=[("crstd", b2), "epsc"], W=[("crstd", b2)])
                                P.op(act, lambda h: h.activation(out=crstd[b2][:], in_=crstd[b2][:], func=AF.Exp, scale=-0.5),
                                     R=[("crstd", b2)], W=[("crstd", b2)])
                            for b2 in range(2):
                                for j in range(KC):
                                    asl = actb[:, j, b2 * BLK:(b2 + 1) * BLK]
                                    nb_, nk_ = (co32, "co32") if j % 2 == 0 else (sq32, "sq32")
                                    P.op(dve, lambda h: h.tensor_tensor(out=nb_[:], in0=asl, in1=cmean[b2][:], op=ALU.subtract),
                                         R=[("acta", j, b2), ("cmean", b2)], W=[nk_])
                                    P.op(dve, lambda h: h.tensor_tensor(out=nb_[:], in0=nb_[:], in1=crstd[b2][:], op=ALU.mult),
                                         R=[nk_, ("crstd", b2)], W=[nk_])
                                    P.op(act, lambda h: h.activation(out=asl, in_=nb_[:], func=AF.Silu,
                                                                     scale=pvec[:, 1, j:j + 1], bias=pvec[:, 2, j:j + 1]),
                                         R=[nk_, "pvec"], W=[("acta", j, b2)])
                        if debug is not None and debug[0] == "aact" and l == debug[2] and hf == debug[3]:
                            dump([actb[:, kc, :] for kc in range(KC)], [AK_all for kc in range(KC)])
                            break
                        yproj(hf, 0, lambda kc, b2: actb[:, kc, b2 * BLK:(b2 + 1) * BLK],
                              lambda kc, b2: [("acta", kc, b2)], False)
                        if debug is not None and debug[0] == "z" and l == debug[2] and hf == debug[3]:
                            dump([z[:, n, :] for n in range(KC)], [[("z", n, 0), ("z", n, 1)] for n in range(KC)])
                            break

                        alias_sync(ZBK_all, AK_all)
                        for kc in range(KC):
                            if kc % 2 == 0:
                                P.op(act, lambda h: h.activation(out=actb[:, kc, :], in_=z[:, kc, :], func=AF.Copy),
                                     R=[("z", kc, 0), ("z", kc, 1)], W=[("zb", kc, 0), ("zb", kc, 1)])
                            else:
                                P.op(dve, lambda h: h.tensor_copy(out=actb[:, kc, :], in_=z[:, kc, :]),
                                     R=[("z", kc, 0), ("z", kc, 1)], W=[("zb", kc, 0), ("zb", kc, 1)])
                        for u in range(4):
                            wo_, wok, wou = ws_next((KC, 256))
                            for nn in range(2):
                                n = 2 * u + nn
                                for b2 in range(2):
                                    blk = 2 * hf + b2
                                    pp = ycnt[0] % 2
                                    ycnt[0] += 1
                                    pm = ps[pp]

                                    def fo(h):
                                        ins = None
                                        for kc in range(KC):
                                            ins = h.matmul(pm[:], lhsT=wo_[:, kc, nn * 128:(nn + 1) * 128],
                                                           rhs=actb[:, kc, b2 * BLK:(b2 + 1) * BLK],
                                                           start=(kc == 0), stop=(kc == KC - 1))
                                        return ins
                                    P.op(pe, fo, R=[wok] + [("zb", kc, b2) for kc in range(KC)], W=[PSK[pp]])
                                    P.op(dve, lambda h: h.tensor_scalar(out=tmp2[:], in0=m[:, n, blk * BLK:(blk + 1) * BLK],
                                                                        scalar1=sh1(n), scalar2=ainv1(n),
                                                                        op0=ALU.subtract, op1=ALU.mult),
                                         R=[MK(n, blk)] + MODK, W=["tmp2"])
                                    P.op(dve, lambda h: h.scalar_tensor_tensor(out=tmp2[:], in0=rview(n, hf)[:, b2 * BLK:(b2 + 1) * BLK],
                                                                               scalar=ALPHA, in1=tmp2[:], op0=ALU.mult, op1=ALU.add),
                                         R=[("X", n, hf, 0), "tmp2"], W=["tmp2"])
                                    P.op(dve, lambda h: h.scalar_tensor_tensor(out=z[:, n, b2 * BLK:(b2 + 1) * BLK], in0=pm[:],
                                                                               scalar=g1(n), in1=tmp2[:], op0=ALU.mult, op1=ALU.add),
                                         R=["tmp2"] + MODK, W=[PSK[pp], ("z", n, b2)])
                            ws_done(wou)
                        with PScope(P) as s_ln:
                            lnsc = [sb(f"lnsc{l}{hf}_{i}", [128, BLK], F32, s_ln) for i in range(6)]
                            ln_feature_major(lambda n, b2: z[:, n, b2 * BLK:(b2 + 1) * BLK], lambda n, b2: [("z", n, b2)],
                                             lambda n, b2: X[:, n, hf, b2 * BLK:(b2 + 1) * BLK], lambda n, b2: XK(n, hf),
                                             lambda n: pvec[:, 4, n:n + 1], lambda n: pvec[:, 5, n:n + 1], lnsc)
                    if dbg_stop[0]:
                        break
                if debug is not None and debug[0] == "x1" and l == debug[2]:
                    dump([X[:, kc, hf, :] for kc in range(KC) for hf in range(2)], [XK(kc, hf) for kc in range(KC) for hf in range(2)])
                    break

            with PScope(P) as s_f:
                m2 = sb(f"m2{l}", [128, KC, HT], BF16, s_f)
                hid = sb(f"hid{l}", [128, FKC, HT], BF16, s_f)
                tb = sb(f"tb{l}", [128, KC, HT], F32, s_f)
                sgf = [sb(f"sgf{l}_{i}", [128, BLK], F32, s_f) for i in range(2)]
                lnsc2 = [sb(f"lnsc2{l}_{i}", [128, BLK], F32, s_f) for i in range(6)]
                fcnt = 0
                for hf in range(2):
                    for kc in range(KC):
                        P.op(act, lambda h: h.activation(out=m2[:, kc, :], in_=X[:, kc, hf, :], func=AF.Identity,
                                                         scale=opsc2(kc), bias=sh2(kc)),
                             R=XK(kc, hf) + MODK, W=[("m2", kc, 0), ("m2", kc, 1)])
                    for u in range(22):
                        wf, wfk, wfu = ws_next((KC, 256))
                        for jj in range(1):
                            j = u
                            for b2 in range(2):
                                pp = fcnt % 2
                                fcnt += 1
                                pa, pb_ = ps[2 * pp], ps[2 * pp + 1]

                                def fh(h):
                                    ins = None
                                    for kc in range(KC):
                                        h.matmul(pa[:], lhsT=wf[:, kc, jj * 128:(jj + 1) * 128], rhs=m2[:, kc, b2 * BLK:(b2 + 1) * BLK],
                                                 start=(kc == 0), stop=(kc == KC - 1))
                                    for kc in range(KC):
                                        ins = h.matmul(pb_[:], lhsT=wf[:, kc, 128 + jj * 128:128 + (jj + 1) * 128],
                                                       rhs=m2[:, kc, b2 * BLK:(b2 + 1) * BLK], start=(kc == 0), stop=(kc == KC - 1))
                                    return ins
                                P.op(pe, fh, R=[wfk] + [("m2", kc, b2) for kc in range(KC)], W=[PSK[2 * pp], PSK[2 * pp + 1]])
                                P.op(act, lambda h: h.activation(out=sgf[pp][:], in_=pa[:], func=AF.Silu), W=[PSK[2 * pp], ("sgf", pp)])
                                P.op(dve, lambda h: h.tensor_tensor(out=hid[:, j, b2 * BLK:(b2 + 1) * BLK], in0=pb_[:], in1=sgf[pp][:],
                                                                    op=ALU.mult), R=[("sgf", pp)], W=[PSK[2 * pp + 1], ("hid", j, b2)])
                        ws_done(wfu)
                    for n in range(KC):
                        w2a, w2ak, w2au = ws_next((11, 128))
                        w2b, w2bk, w2bu = ws_next((11, 128))
                        for b2 in range(2):
                            pp = fcnt % 2
                            fcnt += 1
                            pf = ps[pp]

                            def fo2(h):
                                ins = None
                                for kc in range(FKC):
                                    ins = h.matmul(pf[:], lhsT=(w2a if kc < 11 else w2b)[:, kc % 11, :], rhs=hid[:, kc, b2 * BLK:(b2 + 1) * BLK],
                                                   start=(kc == 0), stop=(kc == FKC - 1))
                                return ins
                            P.op(pe, fo2, R=[w2ak, w2bk] + [("hid", kc, b2) for kc in range(FKC)], W=[PSK[pp]])
                            P.op(act, lambda h: h.activation(out=sgf[pp][:], in_=pf[:], func=AF.Identity, scale=g2(n)),
                                 R=MODK, W=[PSK[pp], ("sgf", pp)])
                            P.op(dve, lambda h: h.scalar_tensor_tensor(out=tb[:, n, b2 * BLK:(b2 + 1) * BLK],
                                                                       in0=X[:, n, hf, b2 * BLK:(b2 + 1) * BLK], scalar=ALPHA,
                                                                       in1=sgf[pp][:], op0=ALU.mult, op1=ALU.add),
                                 R=XK(n, hf) + [("sgf", pp)], W=[("tb", n, b2)])
                        ws_done(w2au)
                        ws_done(w2bu)
                    ln_feature_major(lambda n, b2: tb[:, n, b2 * BLK:(b2 + 1) * BLK], lambda n, b2: [("tb", n, b2)],
                                     lambda n, b2: X[:, n, hf, b2 * BLK:(b2 + 1) * BLK], lambda n, b2: XK(n, hf),
                                     lambda n: pvec[:, 6, n:n + 1], lambda n: pvec[:, 7, n:n + 1], lnsc2)
            if debug is not None and debug[0] == "x2" and l == debug[2]:
                dump([X[:, kc, hf, :] for kc in range(KC) for hf in range(2)], [XK(kc, hf) for kc in range(KC) for hf in range(2)])
                break

        if not dbg_stop[0] and (debug is None or debug[0] == "full"):
            with PScope(P) as s_o:
                yt = [sb(f"yt{i}", [128, D], F32, s_o) for i in range(2)]
                ysem = [P.new_sem(f"y{i}") for i in range(2)]
                for i in range(T // 128):
                    s = i % 2
                    hf, tc = divmod(i, 8)
                    for half in range(2):
                        pb = ps[half]

                        def tr(h):
                            ins = None
                            for q in range(4):
                                kc = half * 4 + q
                                ins = h.transpose(pb[:, q * 128:(q + 1) * 128], X[:, kc, hf, tc * 128:(tc + 1) * 128], ident_f[:])
                            return ins
                        P.op(pe, tr, R=[k for q in range(4) for k in XK(half * 4 + q, hf)] + ["ident_f"], W=[PSK[half]])
                        P.op(act if half == 0 else dve,
                             (lambda h: h.activation(out=yt[s][:, 0:512], in_=pb[:], func=AF.Copy)) if half == 0 else
                             (lambda h: h.tensor_copy(out=yt[s][:, 512:1024], in_=pb[:])),
                             W=[PSK[half], ("yt", s, half)])
                    P.dma(sp, ysem[s], y_d[i * 128:(i + 1) * 128, :], yt[s][:], R=[("yt", s, 0), ("yt", s, 1)], is_output=True)

        if debug is not None and debug[0] == "input":
            dsem = P.new_sem("dbg")
            for kc in range(KC):
                for hf in range(2):
                    P.dma(sp, dsem, dbg_d[:, (kc * 2 + hf) * HT:(kc * 2 + hf + 1) * HT], X[:, kc, hf, :],
                          R=XK(kc, hf), is_output=True)

        for tok in P.out_toks:
            P._wait(sp, tok)
    return nc


def _prep_shared(inp):
    f = lambda k: np.asarray(inp[k], np.float32)
    w_in = f("w_in")
    sh = {}
    sh["lnin"] = np.ascontiguousarray(np.stack([_fm(f("ln_in_g")), _fm(f("ln_in_b"))], 1))
    sh["bada"] = np.ascontiguousarray(f("b_ada").reshape(L, 48, 128).transpose(0, 2, 1))
    pv = []
    for l in range(L):
        rows = [f("conv_b")[l], f("conv_ln_g")[l], f("conv_ln_b")[l], f("hgrn_norm_g")[l], f("ln1_g")[l],
                f("ln1_b")[l], f("ln2_g")[l], f("ln2_b")[l], f("sgu_ln_b")[l]]
        pv.append(np.stack([_fm(r) for r in rows], 1))
    sh["pvec"] = np.ascontiguousarray(np.stack(pv, 0))
    sh["lbraw"] = np.ascontiguousarray(np.stack([_fm(f("hgrn_lb")[0]), _fm(f("hgrn_lb")[1])], 1))
    sh["convw"] = np.ascontiguousarray(f("conv_w").transpose(0, 2, 1).reshape(L, KC, 128, CONV_K).transpose(0, 2, 1, 3))
    sh["sgug"] = f("sgu_ln_g")
    sh["sgub"] = f("sgu_ln_b")
    sh["sgubias"] = np.ascontiguousarray(f("sgu_b").reshape(L, NH * 128))
    sh["sguw"] = np.ascontiguousarray(f("sgu_w").transpose(0, 3, 1, 2))
    wada, winBq, winBz, winBi, winA, winCv, winCu, winG, wxo, wo, wffi, wffo = ([] for _ in range(12))
    for l in range(L):
        wi = w_in[l]
        wada.append(_wunits(f("w_ada")[l], [[(u * 256, 256)] for u in range(24)]))
        winBq.append(_wunits(wi, [[(OFF_B + 0 * D + h * 128, 128)] for h in range(NH)]))
        winBz.append(_wunits(wi, [[(OFF_B + g * D + h * 128, 128) for g in (1, 2)] for h in range(NH)]))
        winBi.append(_wunits(wi, [[(OFF_B + g * D + h * 128, 128) for g in (3, 4)] for h in range(NH)]))
        winA.append(_wunits(wi, [[(OFF_A + u * 128, 128), (OFF_A + D + u * 128, 128)] for u in range(8)]))
        winCu.append(_wunits(wi, [[(OFF_C + u * 256, 256)] for u in range(4)]))
        winCv.append(_wunits(wi, [[(OFF_C + D + u * 256, 256)] for u in range(4)]))
        winG.append(np.stack([_wunits(wi, [[(OFF_G + g * D + u * 256, 256)] for u in range(4)]) for g in range(3)], 0))
        wxo.append(np.stack([_wunits(f(k)[l], [[(u * 256, 256)] for u in range(4)])
                             for k in ("w_a_out", "w_b_out", "w_c_out")], 0))
        wo.append(_wunits(f("w_o")[l], [[(u * 256, 256)] for u in range(4)]))
        wf = f("w_ffn_in")[l]
        wffi.append(_wunits(wf, [[(u * 128, 128), (FF + u * 128, 128)] for u in range(22)]))
        w2 = f("w_ffn_out")[l]
        wffo.append(np.ascontiguousarray(
            np.stack([w2[:, n * 128:(n + 1) * 128].reshape(2, 11, 128, 128).transpose(0, 2, 1, 3) for n in range(8)], 0)))
    sh["wada"] = np.stack(wada, 0)
    sh["winBq"] = np.stack(winBq, 0)
    sh["winBz"] = np.stack(winBz, 0)
    sh["winBi"] = np.stack(winBi, 0)
    sh["winA"] = np.stack(winA, 0)
    sh["winCv"] = np.stack(winCv, 0)
    sh["winCu"] = np.stack(winCu, 0)
    sh["winG"] = np.stack(winG, 0)
    sh["wxo"] = np.stack(wxo, 0)
    sh["wo"] = np.stack(wo, 0)
    sh["wffi"] = np.stack(wffi, 0)
    sh["wffo"] = np.stack(wffo, 0)
    return {k: np.ascontiguousarray(v, dtype=np.float32) for k, v in sh.items()}


def _prep_cores(inp):
    xp = np.asarray(inp["x_prompt"], np.float32)
    xs_ = np.asarray(inp["x_sample"], np.float32)
    st = np.asarray(inp["state_hgrn"], np.float32)
    c = np.asarray(inp["c"], np.float32)
    cc = np.asarray(inp["c_ctx"], np.float32)
    cores = []
    for i in range(8):
        d = {}
        fl = np.zeros((128, 4), np.float32)
        if i < 4:
            d["x"] = np.ascontiguousarray(xs_[i])
            fl[:, 0] = 1.0
            fl[:, 1] = 1.0
            d["cond"] = _fm(c[i])
            d["s0"] = np.ascontiguousarray(st[i])
        else:
            blk = xp[4 * (i - 4):4 * (i - 4) + 4].reshape(4 * SEG, D)
            d["x"] = np.ascontiguousarray(np.concatenate([blk, blk], 0))
            d["cond"] = _fm(cc)
            d["s0"] = np.zeros((L, 2, NH, 128, 128), np.float32)
        d["flags"] = fl
        cores.append(d)
    return cores


def kernel(**inputs):
    shared = _prep_shared(inputs)
    cores = _prep_cores(inputs)
    nc = build_program()
    in_maps = [dict(shared, **c) for c in cores]
    res = run_bass_kernel_spmd(nc, in_maps, core_ids=list(range(8)))
    y_prompt = np.zeros((16, SEG, D), np.float32)
    y_sample = np.zeros((4, T, D), np.float32)
    new_state = np.zeros((16, L, 2, NH, 128, 128), np.float32)
    for i, r in enumerate(res.results):
        if i < 4:
            y_sample[i] = r["y"]
        else:
            y_prompt[4 * (i - 4):4 * (i - 4) + 4] = r["y"][:4 * SEG].reshape(4, SEG, D)
            stt = r["st"]
            new_state[4 * (i - 4):4 * (i - 4) + 4] = stt[:, :, 0:4].transpose(2, 0, 1, 3, 4, 5)
    return (y_prompt, y_sample, new_state)
```
